# Optimizing a Trainium2 kernel written in Bass

```python
import math
import jax, jax.numpy as jnp
from jax import lax
import numpy as np

D_MODEL = 2048
BATCH = 4
SEQ = 4096
DEPTH = 1

ROPE_THETA = 10000.0
NORM_EPS = 1e-6
Q_BLOCK = 128
NEG_INF = -1e30

MLA_HEADS = 8
MLA_NOPE_DIM = 128
MLA_ROPE_DIM = 64
MLA_QK_DIM = MLA_NOPE_DIM + MLA_ROPE_DIM
MLA_V_DIM = 128
MLA_Q_RANK = 512
MLA_KV_RANK = 256

DIFF_HEADS = 8
DIFF_HEAD_DIM = 64
DIFF_V_DIM = 2 * DIFF_HEAD_DIM

D_FF = 5632
CONV_WIDTH = 3

N_BRANCHES = 2
IN_SPLITS = (
    MLA_Q_RANK,
    MLA_KV_RANK,
    MLA_ROPE_DIM,
    DIFF_HEADS * 2 * DIFF_HEAD_DIM,
    DIFF_HEADS * 2 * DIFF_HEAD_DIM,
    DIFF_HEADS * DIFF_V_DIM,
    N_BRANCHES * D_MODEL,
)
IN_WIDTH = sum(IN_SPLITS)
MIX_WIDTH_MLA = MLA_HEADS * MLA_V_DIM
MIX_WIDTH_DIFF = DIFF_HEADS * DIFF_V_DIM

kernel_name = "hybrid_mla_diffattn_convffn_adaln"


def rms_norm(x, g):
    xf = x.astype(jnp.float32)
    y = xf * lax.rsqrt(jnp.mean(xf * xf, axis=-1, keepdims=True) + NORM_EPS)
    return (y * g.astype(jnp.float32)).astype(x.dtype)


def rope(x, positions):
    d = x.shape[-1]
    inv_freq = ROPE_THETA ** (-jnp.arange(0, d, 2, dtype=jnp.float32) / d)
    ang = positions.astype(jnp.float32)[..., None] * inv_freq
    cos = jnp.cos(ang)[:, :, None, :]
    sin = jnp.sin(ang)[:, :, None, :]
    xf = x.astype(jnp.float32)
    x1, x2 = xf[..., : d // 2], xf[..., d // 2:]
    return jnp.concatenate([x1 * cos - x2 * sin, x2 * cos + x1 * sin], axis=-1).astype(x.dtype)


def _causal_mask(i, seq):
    q_idx = i * Q_BLOCK + jnp.arange(Q_BLOCK)
    return (q_idx[:, None] >= jnp.arange(seq)[None, :])[None, None]


def _unblock(o):
    nb, b, q, h, dv = o.shape
    return jnp.moveaxis(o, 0, 1).reshape(b, nb * q, h, dv)


def causal_softmax_attention(q, k, v):
    seq = q.shape[1]
    scale = q.shape[-1] ** -0.5

    def one_block(i):
        qi = lax.dynamic_slice_in_dim(q, i * Q_BLOCK, Q_BLOCK, axis=1)
        s = jnp.einsum('bqhd,bkhd->bhqk', qi, k).astype(jnp.float32) * scale
        p = jax.nn.softmax(jnp.where(_causal_mask(i, seq), s, NEG_INF), axis=-1)
        return jnp.einsum('bhqk,bkhd->bqhd', p.astype(v.dtype), v)

    return _unblock(lax.map(one_block, jnp.arange(seq // Q_BLOCK)))


def causal_differential_attention(q1, q2, k1, k2, v, lam):
    seq = q1.shape[1]
    scale = q1.shape[-1] ** -0.5

    def one_block(i):
        mask = _causal_mask(i, seq)
        q1i = lax.dynamic_slice_in_dim(q1, i * Q_BLOCK, Q_BLOCK, axis=1)
        q2i = lax.dynamic_slice_in_dim(q2, i * Q_BLOCK, Q_BLOCK, axis=1)
        s1 = jnp.einsum('bqhd,bkhd->bhqk', q1i, k1).astype(jnp.float32) * scale
        s2 = jnp.einsum('bqhd,bkhd->bhqk', q2i, k2).astype(jnp.float32) * scale
        p1 = jax.nn.softmax(jnp.where(mask, s1, NEG_INF), axis=-1)
        p2 = jax.nn.softmax(jnp.where(mask, s2, NEG_INF), axis=-1)
        w = p1 - lam * p2
        return jnp.einsum('bhqk,bkhd->bqhd', w.astype(v.dtype), v)

    return _unblock(lax.map(one_block, jnp.arange(seq // Q_BLOCK)))


def causal_depthwise_conv(u, w, b):
    k = w.shape[0]
    seq = u.shape[1]
    up = jnp.pad(u, ((0, 0), (k - 1, 0), (0, 0)))
    return b + sum(up[:, j:j + seq, :] * w[j] for j in range(k))


def hybrid_layer(layer, x, c, positions, w_ada, b_ada, g_norm1, w_in, b_gate, g_q_lat, w_q_up,
                 g_kv_lat, w_kv_up, g_q_mla, g_k_mla, w_o_mla, g_q_diff, g_k_diff, lam_q1, lam_k1,
                 lam_q2, lam_k2, g_sub_diff, w_o_diff, w_out, g_norm2, w_up, conv_w, conv_b, w_down):
    b_, s_, _ = x.shape
    lambda_init = 0.8 - 0.6 * math.exp(-0.3 * layer)

    mod = jnp.einsum('bd,de->be', jax.nn.silu(c), w_ada) + b_ada
    shift1, scale1, gate1, shift2, scale2, gate2 = jnp.split(mod[:, None, :], 6, axis=-1)

    h = rms_norm(x, g_norm1) * (1.0 + scale1) + shift1
    proj = jnp.einsum('bsd,de->bse', h, w_in)
    offsets = [int(o) for o in np.cumsum(IN_SPLITS)[:-1]]
    q_lat, kv_lat, k_pe, dq, dk, dv, gate_logits = jnp.split(proj, offsets, axis=-1)

    q = jnp.einsum('bsr,re->bse', rms_norm(q_lat, g_q_lat), w_q_up)
    q = q.reshape(b_, s_, MLA_HEADS, MLA_QK_DIM)
    kv = jnp.einsum('bsr,re->bse', rms_norm(kv_lat, g_kv_lat), w_kv_up)
    kv = kv.reshape(b_, s_, MLA_HEADS, MLA_NOPE_DIM + MLA_V_DIM)
    k_nope, v_mla = kv[..., :MLA_NOPE_DIM], kv[..., MLA_NOPE_DIM:]
    q_nope = rms_norm(q[..., :MLA_NOPE_DIM], g_q_mla[:MLA_NOPE_DIM])
    q_pe = rope(rms_norm(q[..., MLA_NOPE_DIM:], g_q_mla[MLA_NOPE_DIM:]), positions)
    k_nope = rms_norm(k_nope, g_k_mla[:MLA_NOPE_DIM])
    k_pe = rope(rms_norm(k_pe[:, :, None, :], g_k_mla[MLA_NOPE_DIM:]), positions)
    k_pe = jnp.broadcast_to(k_pe, (b_, s_, MLA_HEADS, MLA_ROPE_DIM))
    q_mla = jnp.concatenate([q_nope, q_pe], axis=-1)
    k_mla = jnp.concatenate([k_nope, k_pe], axis=-1)
    o_mla = causal_softmax_attention(q_mla, k_mla, v_mla).reshape(b_, s_, MIX_WIDTH_MLA)
    o_mla = jnp.einsum('bse,ed->bsd', o_mla, w_o_mla)

    dq = rope(rms_norm(dq.reshape(b_, s_, 2 * DIFF_HEADS, DIFF_HEAD_DIM), g_q_diff), positions)
    dk = rope(rms_norm(dk.reshape(b_, s_, 2 * DIFF_HEADS, DIFF_HEAD_DIM), g_k_diff), positions)
    dq = dq.reshape(b_, s_, DIFF_HEADS, 2, DIFF_HEAD_DIM)
    dk = dk.reshape(b_, s_, DIFF_HEADS, 2, DIFF_HEAD_DIM)
    dv = dv.reshape(b_, s_, DIFF_HEADS, DIFF_V_DIM)
    f32 = jnp.float32
    lam = (jnp.exp(jnp.sum(lam_q1.astype(f32) * lam_k1.astype(f32)))
           - jnp.exp(jnp.sum(lam_q2.astype(f32) * lam_k2.astype(f32))) + lambda_init)
    o_diff = causal_differential_attention(dq[..., 0, :], dq[..., 1, :], dk[..., 0, :], dk[..., 1, :], dv, lam)
    o_diff = rms_norm(o_diff, g_sub_diff) * (1.0 - lambda_init)
    o_diff = jnp.einsum('bse,ed->bsd', o_diff.reshape(b_, s_, MIX_WIDTH_DIFF), w_o_diff)

    g_a, g_b = jnp.split(jax.nn.sigmoid(gate_logits + b_gate), N_BRANCHES, axis=-1)
    mixed = g_a * o_mla + g_b * o_diff
    x = x + gate1 * jnp.einsum('bsd,de->bse', mixed, w_out)

    h2 = rms_norm(x, g_norm2) * (1.0 + scale2) + shift2
    u = jnp.einsum('bsd,df->bsf', h2, w_up)
    u = causal_depthwise_conv(u, conv_w, conv_b)
    val, gte = jnp.split(u, 2, axis=-1)
    y = jnp.einsum('bsf,fd->bsd', jax.nn.silu(gte) * val, w_down)
    return x + gate2 * y


def setup_inputs(seed: int = 0) -> dict:
    key = jax.random.key(seed)
    ks = iter(jax.random.split(key, 40))

    def w(shape, fan_in, gain=1.0):
        return jax.random.normal(next(ks), (DEPTH,) + shape, jnp.float32) * (gain * fan_in ** -0.5)

    def gain(n):
        return 1.0 + 0.02 * jax.random.normal(next(ks), (DEPTH, n), jnp.float32)

    def bias(shape, s=0.02):
        return s * jax.random.normal(next(ks), (DEPTH,) + shape, jnp.float32)

    x = jax.random.normal(next(ks), (BATCH, SEQ, D_MODEL), jnp.float32)
    c = jax.random.normal(next(ks), (BATCH, D_MODEL), jnp.float32)
    offset = jax.random.randint(next(ks), (BATCH, 1), 0, 1024, dtype=jnp.int32)
    positions = (jnp.arange(SEQ, dtype=jnp.int32)[None, :] + offset).astype(jnp.int32)
    return {
        "x": x,
        "c": c,
        "positions": positions,
        "w_ada": w((D_MODEL, 6 * D_MODEL), D_MODEL, 0.5),
        "b_ada": bias((6 * D_MODEL,)),
        "g_norm1": gain(D_MODEL),
        "w_in": w((D_MODEL, IN_WIDTH), D_MODEL),
        "b_gate": bias((N_BRANCHES * D_MODEL,)),
        "g_q_lat": gain(MLA_Q_RANK),
        "w_q_up": w((MLA_Q_RANK, MLA_HEADS * MLA_QK_DIM), MLA_Q_RANK),
        "g_kv_lat": gain(MLA_KV_RANK),
        "w_kv_up": w((MLA_KV_RANK, MLA_HEADS * (MLA_NOPE_DIM + MLA_V_DIM)), MLA_KV_RANK),
        "g_q_mla": gain(MLA_QK_DIM),
        "g_k_mla": gain(MLA_QK_DIM),
        "w_o_mla": w((MIX_WIDTH_MLA, D_MODEL), MIX_WIDTH_MLA),
        "g_q_diff": gain(DIFF_HEAD_DIM),
        "g_k_diff": gain(DIFF_HEAD_DIM),
        "lam_q1": bias((DIFF_HEAD_DIM,), 0.1),
        "lam_k1": bias((DIFF_HEAD_DIM,), 0.1),
        "lam_q2": bias((DIFF_HEAD_DIM,), 0.1),
        "lam_k2": bias((DIFF_HEAD_DIM,), 0.1),
        "g_sub_diff": gain(DIFF_V_DIM),
        "w_o_diff": w((MIX_WIDTH_DIFF, D_MODEL), MIX_WIDTH_DIFF),
        "w_out": w((D_MODEL, D_MODEL), D_MODEL),
        "g_norm2": gain(D_MODEL),
        "w_up": w((D_MODEL, 2 * D_FF), D_MODEL),
        "conv_w": w((CONV_WIDTH, 2 * D_FF), CONV_WIDTH),
        "conv_b": bias((2 * D_FF,)),
        "w_down": w((D_FF, D_MODEL), D_FF),
    }


def reference(x, c, positions, w_ada, b_ada, g_norm1, w_in, b_gate, g_q_lat, w_q_up, g_kv_lat,
              w_kv_up, g_q_mla, g_k_mla, w_o_mla, g_q_diff, g_k_diff, lam_q1, lam_k1, lam_q2, lam_k2,
              g_sub_diff, w_o_diff, w_out, g_norm2, w_up, conv_w, conv_b, w_down):
    for l in range(DEPTH):
        x = hybrid_layer(l, x, c, positions, w_ada[l], b_ada[l], g_norm1[l], w_in[l], b_gate[l],
                         g_q_lat[l], w_q_up[l], g_kv_lat[l], w_kv_up[l], g_q_mla[l], g_k_mla[l],
                         w_o_mla[l], g_q_diff[l], g_k_diff[l], lam_q1[l], lam_k1[l], lam_q2[l],
                         lam_k2[l], g_sub_diff[l], w_o_diff[l], w_out[l], g_norm2[l], w_up[l],
                         conv_w[l], conv_b[l], w_down[l])
    return x
```

```python
import math
import numpy as np
import concourse.bass as bass
import concourse.mybir as mybir
from concourse.bass_utils import run_bass_kernel_spmd

F32 = mybir.dt.float32
BF16 = mybir.dt.bfloat16
I32 = mybir.dt.int32
AF = mybir.ActivationFunctionType
ALU = mybir.AluOpType
AX = mybir.AxisListType

D = 2048
KD = 16
DFF = 5632
NCORES = 8
EPS = 1e-6
LAMBDA_INIT = 0.8 - 0.6 * math.exp(-0.3 * 0)


class V:
    __slots__ = ("key", "ap")

    def __init__(self, key, ap):
        self.key = key
        self.ap = ap

    def __getitem__(self, idx):
        return V(self.key, self.ap[idx])

    def k(self, key):
        return V(key, self.ap)

    def re(self, s, **kw):
        return V(self.key, self.ap.rearrange(s, **kw))


class Op:
    __slots__ = ("eng", "fn", "deps", "dma", "semkey", "idx", "ticket", "need_inc", "semval")

    def __init__(self, eng, fn, dma, semkey):
        self.eng = eng
        self.fn = fn
        self.deps = set()
        self.dma = dma
        self.semkey = semkey
        self.ticket = None
        self.need_inc = False
        self.semval = None


class Prog:
    ENGS = ("pe", "act", "dve", "pool", "sp")

    def __init__(self):
        self.ops = []
        self.state = {}
        self.eng_ops = {e: [] for e in self.ENGS}
        self.bar_deps = set()
        self.bar_seen = {e: True for e in self.ENGS}
        self.dma_since_bar = []
        self.out_dmas = []

    def op(self, eng, fn, reads=(), writes=(), dma=False, semkey=None):
        o = Op(eng, fn, dma, semkey)
        o.idx = len(self.ops)
        for v in reads:
            st = self.state.get(v.key)
            if st is not None:
                o.deps.update(st["w"])
        for v in writes:
            st = self.state.setdefault(v.key, {"w": [], "r": [], "pend": []})
            if st["r"]:
                st["pend"] = st["r"] + st["w"]
                st["w"] = []
                st["r"] = []
            o.deps.update(st["pend"])
        for v in reads:
            st = self.state.setdefault(v.key, {"w": [], "r": [], "pend": []})
            st["r"].append(o.idx)
        for v in writes:
            self.state[v.key]["w"].append(o.idx)
        if not self.bar_seen[eng]:
            o.deps.update(self.bar_deps)
            self.bar_seen[eng] = True
        o.deps.discard(o.idx)
        self.ops.append(o)
        self.eng_ops[eng].append(o)
        if dma:
            self.dma_since_bar.append(o.idx)
        return o

    def barrier(self):
        deps = set(self.dma_since_bar)
        for e in self.ENGS:
            if self.eng_ops[e]:
                deps.add(self.eng_ops[e][-1].idx)
        self.bar_deps = deps
        self.bar_seen = {e: False for e in self.ENGS}
        self.dma_since_bar = []
        self.state = {}

    def emit(self, nc, stack):
        ops = self.ops
        waits = {}
        for e in self.ENGS:
            waited = {}
            for o in self.eng_ops[e]:
                best = {}
                for d in o.deps:
                    p = ops[d]
                    if p.dma:
                        key = ("dma", p.semkey)
                    else:
                        if p.eng == e and e == "pe":
                            continue
                        key = ("eng", p.eng)
                    if key not in best or best[key] < d:
                        best[key] = d
                lst = []
                for key, d in best.items():
                    if waited.get(key, -1) >= d:
                        continue
                    waited[key] = d
                    lst.append(d)
                    if not ops[d].dma:
                        ops[d].need_inc = True
                waits[o.idx] = lst
        cnt = {e: 0 for e in self.ENGS}
        dcnt = {}
        for o in ops:
            if o.dma:
                dcnt[o.semkey] = dcnt.get(o.semkey, 0) + 16
                o.semval = dcnt[o.semkey]
            elif o.need_inc:
                cnt[o.eng] += 1
                o.ticket = cnt[o.eng]
        esem = {e: stack.enter_context(nc.semaphore("s_" + e)) for e in self.ENGS}
        dsem = {}
        for i, k in enumerate(sorted(dcnt.keys())):
            dsem[k] = stack.enter_context(nc.semaphore("d%d" % i))
        self.nsem = len(esem) + len(dsem)
        block = stack.enter_context(nc.Block())

        def run(e, h):
            for o in self.eng_ops[e]:
                for d in waits[o.idx]:
                    p = ops[d]
                    if p.dma:
                        h.wait_ge(dsem[p.semkey], p.semval)
                    else:
                        h.wait_ge(esem[p.eng], p.ticket)
                if o.fn is None:
                    continue
                ins = o.fn(h)
                if o.dma:
                    ins.then_inc(dsem[o.semkey], 16)
                elif o.need_inc:
                    ins.then_inc(esem[o.eng], 1)

        @block.tensor
        def _(h):
            run("pe", h)

        @block.scalar
        def _(h):
            run("act", h)

        @block.vector
        def _(h):
            run("dve", h)

        @block.gpsimd
        def _(h):
            run("pool", h)

        @block.sync
        def _(h):
            run("sp", h)


class Rot:
    def __init__(self, items):
        self.items = items
        self.i = 0

    def next(self):
        x = self.items[self.i % len(self.items)]
        self.i += 1
        return x


def par_layout(NH):
    off = {}
    c = 0
    for name, n in [("b_ada", 96), ("g1", 16), ("g2", 16), ("b_gate", 32), ("conv_w", 3 * 88), ("conv_b", 88),
                    ("g_q_lat", 4), ("g_kv_lat", 2), ("gq_nope", 1), ("gq_pe", 1), ("gk_nope", 1), ("gk_pe", 1),
                    ("gq_diff", 1), ("gk_diff", 1), ("g_sub", 1), ("invfreq", 1), ("lq1", 64), ("lk1", 64),
                    ("lq2", 64), ("lk2", 64), ("c", 16), ("keep", 1), ("hvalid", NH)]:
        off[name] = (c, n)
        c += n
    return off, c


def build(S, debug=False):
    from contextlib import ExitStack
    NO = S // 2
    NOB = NO // 128
    NH = 2 * NOB
    NQ = NO + NH
    NTOK = 2 * NO + NH
    NKB = 2 * NOB
    POFF, NPAR = par_layout(NH)

    nc = bass.Bass("TRN2", target_bir_lowering=False)
    P = Prog()

    def din(name, shape, dt=F32):
        return nc.dram_tensor(name, list(shape), dt, kind="ExternalInput").ap()

    def dscr(name, shape, dt):
        return nc.dram_tensor(name, list(shape), dt, kind="Internal").ap()

    xT = din("xT", [D, NTOK])
    params = din("params", [128, NPAR])
    posrep = din("posrep", [128, NTOK], I32)
    consts = din("consts", [128, 4 * 128 + NKB * NH])
    w_ada = din("w_ada", [D, 6 * D])
    w_q = din("w_q", [D, 5632])
    w_kv = din("w_kv", [D, 2432])
    w_q_up = din("w_q_up", [512, 1536])
    w_kv_up = din("w_kv_up", [256, 2048])
    w_o_mla = din("w_o_mla", [1024, D])
    w_o_diff = din("w_o_diff", [1024, D])
    w_out = din("w_out", [D, D])
    w_up = din("w_up", [D, 2 * DFF])
    w_down = din("w_down", [DFF, D])
    outT = nc.dram_tensor("outT", [D, NO], F32, kind="ExternalOutput").ap()

    QLN = dscr("QLN", [4, 128, NQ], BF16)
    KVN = dscr("KVN", [2, 128, S], BF16)
    KPE = dscr("KPE", [128, S], BF16)
    QN = dscr("QN", [8, 128, NQ], BF16)
    QPE = dscr("QPE", [4, 128, NQ], BF16)
    KN = dscr("KN", [8, 128, S], BF16)
    VM = dscr("VM", [S, 1024], BF16)
    DQ = dscr("DQ", [8, 128, NQ], BF16)
    DK = dscr("DK", [8, 128, S], BF16)
    DV = dscr("DV", [S, 1024], BF16)
    G = dscr("G", [32, 128, NQ], BF16)
    OM = dscr("OM", [8, 128, NQ], BF16)
    OD = dscr("OD", [8, 128, NQ], BF16)
    X1 = dscr("X1", [16, 128, NO], F32)
    H2 = dscr("H2", [16, 128, NQ], BF16)
    Z = dscr("Z", [44, 128, NO], BF16)

    stack = ExitStack()
    with stack:
        PERS_F = 1024 + NPAR + 64
        ARENA_W = 48000
        pers = stack.enter_context(nc.sbuf_tensor("pers", [128, PERS_F], F32))
        cbf = stack.enter_context(nc.sbuf_tensor("cbf", [128, 4 * 128 + NKB * NH], BF16))
        arena = stack.enter_context(nc.sbuf_tensor("arena", [128, ARENA_W], F32))
        arena_b = arena.bitcast(BF16)
        arena_i = arena.bitcast(I32)
        psb = [stack.enter_context(nc.psum_tensor("ps%d" % i, [128, 512], F32)) for i in range(8)]
        PS = [V("ps%d" % i, psb[i][:]) for i in range(8)]

        class Arena:
            def __init__(self):
                self.off = 0

            def reset(self):
                self.off = 0

            def f32(self, key, shape):
                n = int(np.prod(shape))
                v = arena[:, self.off:self.off + n]
                self.off += n
                assert self.off <= ARENA_W, (key, self.off)
                return V(key, _shape(v, shape))

            def bf(self, key, shape):
                n = int(np.prod(shape))
                nw = (n + 1) // 2
                v = arena_b[:, 2 * self.off:2 * self.off + n]
                self.off += nw
                assert self.off <= ARENA_W, (key, self.off)
                return V(key, _shape(v, shape))

            def i32(self, key, shape):
                n = int(np.prod(shape))
                v = arena_i[:, self.off:self.off + n]
                self.off += n
                assert self.off <= ARENA_W, (key, self.off)
                return V(key, _shape(v, shape))

        def _shape(ap, shape):
            if len(shape) == 1:
                return ap
            if len(shape) == 2:
                return ap.rearrange("p (a b) -> p a b", a=shape[0])
            if len(shape) == 3:
                return ap.rearrange("p (a b c) -> p a b c", a=shape[0], b=shape[1])
            raise ValueError

        A = Arena()

        par = V("par", pers[:, 0:NPAR])
        ppos = NPAR

        def pcol(name, j=0, n=1):
            o, _ = POFF[name]
            return pers[:, o + j:o + j + n]

        def persf(key, n):
            nonlocal ppos
            v = V(key, pers[:, ppos:ppos + n])
            ppos += n
            assert ppos <= PERS_F
            return v

        modT = persf("modT", 96)
        a1 = persf("a1", 16)
        a2 = persf("a2", 16)
        lamv = persf("lamv", 8)
        gsub8 = persf("gsub8", 1)
        ones_b = V("cbf", cbf[:, 0:128])
        bones_b = V("cbf", cbf[:, 128:256])
        perm_b = V("cbf", cbf[:, 256:384])
        causal_b = V("cbf", cbf[:, 384:512])
        halo_b = V("cbf", cbf[:, 512:512 + NKB * NH])

        def dma(eng, out, in_, semkey):
            return P.op(eng, lambda h, o=out.ap, i=in_.ap: h.dma_start(out=o, in_=i),
                        reads=[in_], writes=[out], dma=True, semkey=semkey)

        def mm(out, lhsT, rhs, start, stop):
            return P.op("pe", lambda h, o=out.ap, l=lhsT.ap, r=rhs.ap: h.matmul(o, lhsT=l, rhs=r, start=start, stop=stop),
                        reads=[lhsT, rhs], writes=[out])

        def act(out, in_, func, bias=None, scale=None, extra_reads=()):
            kw = {}
            if bias is not None:
                kw["bias"] = bias
            if scale is not None:
                kw["scale"] = scale
            return P.op("act", lambda h, o=out.ap, i=in_.ap: h.activation(out=o, in_=i, func=func, **kw),
                        reads=[in_] + list(extra_reads), writes=[out])

        def tt(eng, out, in0, in1, op):
            return P.op(eng, lambda h, o=out.ap, a=in0.ap, b=in1.ap: h.tensor_tensor(out=o, in0=a, in1=b, op=op),
                        reads=[in0, in1], writes=[out])

        def ts(eng, out, in0, s1, s2, op0, op1=None, extra_reads=()):
            def f(h, o=out.ap, a=in0.ap):
                if op1 is None:
                    return h.tensor_scalar(out=o, in0=a, scalar1=s1, scalar2=None, op0=op0)
                return h.tensor_scalar(out=o, in0=a, scalar1=s1, scalar2=s2, op0=op0, op1=op1)
            return P.op(eng, f, reads=[in0] + list(extra_reads), writes=[out])

        def stt(eng, out, in0, scalar, in1, op0, op1, extra_reads=()):
            return P.op(eng, lambda h, o=out.ap, a=in0.ap, b=in1.ap: h.scalar_tensor_tensor(
                out=o, in0=a, scalar=scalar, in1=b, op0=op0, op1=op1),
                reads=[in0, in1] + list(extra_reads), writes=[out])

        def cp(eng, out, in_):
            return P.op(eng, lambda h, o=out.ap, i=in_.ap: h.tensor_copy(out=o, in_=i), reads=[in_], writes=[out])

        def rstd_from(ps_view, out, dim):
            ts("dve", out, ps_view, 1.0 / dim, EPS, ALU.mult, ALU.add)
            act(out, out, AF.Ln)
            act(out, out, AF.Exp, scale=-0.5)

        cosT = A.f32("cosT", [NTOK])
        sinT = A.f32("sinT", [NTOK])
        B1_MARK = A.off
        dma("sp", par, V("params", params[:, :]), "par")
        cst_f = A.f32("cst_f", [4 * 128 + NKB * NH])
        dma("sp", cst_f, V("consts", consts[:, :]), "cst_f")
        cp("dve", V("cbf", cbf[:]), cst_f)
        csil = A.bf("csil", [16])
        act(csil, V("par", pcol("c", 0, 16)), AF.Silu)
        wa_slots = Rot([A.bf("wa%d" % i, [16, 512]) for i in range(2)])
        psA = PS[0]
        for g in range(24):
            wt = wa_slots.next()
            dma("pool", wt, V("w_ada", w_ada.rearrange("(k p) e -> p k e", p=128)[:, :, g * 512:(g + 1) * 512]), wt.key)
            for j in range(4):
                col = g * 4 + j
                for k in range(KD):
                    mm(psA[:, col:col + 1], wt[:, k, j * 128:(j + 1) * 128], csil[:, k:k + 1], k == 0, k == KD - 1)
        tt("dve", modT, psA[:, 0:96], V("par", pcol("b_ada", 0, 96)), ALU.add)
        stt("dve", a1, modT[:, 16:32], 1.0, V("par", pcol("g1", 0, 16)), ALU.add, ALU.mult)
        stt("dve", a2, modT[:, 64:80], 1.0, V("par", pcol("g2", 0, 16)), ALU.add, ALU.mult)
        shift1 = lambda k: modT.ap[:, k:k + 1]
        gate1 = lambda k: modT.ap[:, 32 + k:33 + k]
        shift2 = lambda k: modT.ap[:, 48 + k:49 + k]
        gate2 = lambda k: modT.ap[:, 80 + k:81 + k]
        ltmp = A.f32("ltmp", [64])
        for i, (qn_, kn_) in enumerate((("lq1", "lk1"), ("lq2", "lk2"))):
            tt("dve", ltmp, V("par", pcol(qn_, 0, 64)), V("par", pcol(kn_, 0, 64)), ALU.mult)
            P.op("dve", lambda h, o=lamv.ap[:, i:i + 1], a=ltmp.ap: h.reduce_sum(out=o, in_=a, axis=AX.X),
                 reads=[ltmp], writes=[lamv])
        act(lamv[:, 2:4], lamv[:, 0:2], AF.Exp)
        stt("dve", lamv[:, 4:5], lamv[:, 3:4], -LAMBDA_INIT, lamv[:, 2:3], ALU.add, ALU.subtract)
        ts("dve", gsub8, V("par", pcol("g_sub")), 1.0 - LAMBDA_INIT, None, ALU.mult)
        neglam = lamv.ap[:, 4:5]
        posi = A.i32("posi", [NTOK])
        angT = A.f32("angT", [NTOK])
        dma("sp", posi, V("posrep", posrep[:, :]), "posi")
        cp("dve", angT, posi)
        ts("dve", angT, angT, pcol("invfreq"), None, ALU.mult, extra_reads=[par])
        PI = math.pi
        ki = A.i32("ki", [NTOK])
        kf = A.f32("kf", [NTOK])
        mk = A.f32("mk", [NTOK])
        for tab, shift in ((sinT, 0.0), (cosT, 0.25)):
            ts("dve", tab, angT, 1.0 / (2.0 * PI), shift, ALU.mult, ALU.add)
            cp("dve", ki, tab)
            cp("dve", kf, ki)
            tt("dve", tab, tab, kf, ALU.subtract)
            ts("dve", mk, tab, 0.5, None, ALU.is_gt)
            tt("dve", tab, tab, mk, ALU.subtract)
            ts("dve", mk, tab, -0.5, None, ALU.is_lt)
            tt("dve", tab, tab, mk, ALU.add)
            act(tab, tab, AF.Sin, scale=6.28318)
        P.barrier()

        A.off = B1_MARK

        def tiles_of(start, count):
            return [(start + i, min(512, count - i)) for i in range(0, count, 512)]
        own_tiles = [(t0, n, t0, t0) for (t0, n) in tiles_of(0, NO)]
        halo_tile = (NO, NH, NO, None)
        oth_tiles = [(t0, n, None, t0 - NH) for (t0, n) in tiles_of(NQ, NO)]
        groups = []
        for i in range(0, len(own_tiles), 2):
            g = own_tiles[i:i + 2]
            if i + 2 >= len(own_tiles):
                g = g + [halo_tile]
            groups.append(g)
        for i in range(0, len(oth_tiles), 2):
            groups.append(oth_tiles[i:i + 2])
        GMAX = 1024 + NH

        xt = A.f32("xt", [16, 512])
        hT = A.bf("hT", [16, GMAX])
        wslots = Rot([A.bf("w%d" % i, [16, 512]) for i in range(2)])
        sqb = Rot([A.bf("sqb%d" % i, [512]) for i in range(3)])
        tmpf = Rot([A.f32("tmpf%d" % i, [512]) for i in range(6)])
        ybf = Rot([A.bf("ybf%d" % i, [512]) for i in range(4)])
        yf = Rot([A.f32("yf%d" % i, [512]) for i in range(5)])
        rstds = Rot([A.f32("rstd%d" % i, [512]) for i in range(3)])
        outb = Rot([A.bf("outb%d" % i, [512]) for i in range(4)])
        stg4 = Rot([A.bf("stg4_%d" % i, [4, 512]) for i in range(2)])
        PSM = Rot([PS[0], PS[1], PS[2]])
        PSX = PS[3]
        PSN = Rot([PS[4], PS[5]])
        PSR = Rot([PS[6], PS[7]])

        def post_norm(ps_list, n, gains, ones_v, dim, rope_tl, dests):
            psn = PSN.next()
            nj = len(ps_list)
            ys = []
            for j, ps in enumerate(ps_list):
                if rope_tl is None:
                    y = yf.next()
                else:
                    y = ybf.next()
                act(y[:, :n], ps[:, :n], AF.Identity, scale=gains[j], extra_reads=[par])
                sq = sqb.next()
                act(sq[:, :n], ps[:, :n], AF.Square)
                mm(psn[:, :n], ones_v, sq[:, :n], j == 0, j == nj - 1)
                ys.append(y)
            rs = rstds.next()
            rstd_from(psn[:, :n], rs[:, :n], dim)
            for j, y in enumerate(ys):
                ob = outb.next()
                if rope_tl is None:
                    tt("dve", ob[:, :n], y[:, :n], rs[:, :n], ALU.mult)
                else:
                    psr = PSR.next()
                    mm(psr[:, :n], perm_b, y[:, :n], True, True)
                    t1 = tmpf.next()
                    tt("dve", t1[:, :n], y[:, :n], cosT[:, rope_tl:rope_tl + n], ALU.mult)
                    t2 = tmpf.next()
                    tt("dve", t2[:, :n], psr[:, :n], sinT[:, rope_tl:rope_tl + n], ALU.mult)
                    tt("pool", t1[:, :n], t1[:, :n], t2[:, :n], ALU.add)
                    tt("dve", ob[:, :n], t1[:, :n], rs[:, :n], ALU.mult)
                dma("sp", dests[j], ob[:, :n], ob.key)

        def build_h(tile, goff, a_v, shift_fn, src_fn):
            tl0, n = tile[0], tile[1]
            src_fn(xt, tl0, n)
            for k in range(KD):
                sq = sqb.next()
                act(sq[:, :n], xt[:, k, :n], AF.Square)
                mm(PSX[:, :n], ones_b, sq[:, :n], k == 0, k == KD - 1)
            rs = rstds.next()
            rstd_from(PSX[:, :n], rs[:, :n], D)
            for k in range(KD):
                t = tmpf.next()
                stt("dve", t[:, :n], xt[:, k, :n], a_v.ap[:, k:k + 1], rs[:, :n], ALU.mult, ALU.mult, extra_reads=[a_v])
                act(V(hT.key + str(goff), hT.ap[:, k, goff:goff + n]), t[:, :n], AF.Identity, bias=shift_fn(k),
                    extra_reads=[modT])

        def load_x(dst, tl0, n):
            dma("sp", V(dst.key, dst.ap[:, :, :n]),
                V("xT", xT.rearrange("(k p) t -> p k t", p=128)[:, :, tl0:tl0 + n]), dst.key)

        def load_w(src, c0, ncols, kd=KD):
            wt = wslots.next()
            dma("pool", V(wt.key, wt.ap[:, :kd, :ncols]),
                V("w", src.rearrange("(k p) e -> p k e", p=128)[:, :, c0:c0 + ncols]), wt.key)
            return wt

        def proj_fm(wt, wc0, tile_off, n, goff_key):
            ps = PSM.next()
            for k in range(KD):
                mm(ps[:, :n], wt[:, k, wc0:wc0 + 128], V(hT.key + str(goff_key), hT.ap[:, k, tile_off:tile_off + n]),
                   k == 0, k == KD - 1)
            return ps

        gq_lat = [pcol("g_q_lat", j) for j in range(4)]
        gkv_lat = [pcol("g_kv_lat", j) for j in range(2)]

        for grp in groups:
            goffs = []
            go = 0
            for tile in grp:
                build_h(tile, go, a1, shift1, load_x)
                goffs.append(go)
                go += tile[1]
            has_q = grp[0][2] is not None
            if has_q:
                wt = load_w(w_q, 0, 512)
                for tile, go in zip(grp, goffs):
                    tl0, n, q0, _ = tile
                    pss = [proj_fm(wt, j * 128, go, n, go) for j in range(3)]
                    ps4 = PSR.next()
                    for k in range(KD):
                        mm(ps4[:, :n], wt[:, k, 384:512], V(hT.key + str(go), hT.ap[:, k, go:go + n]), k == 0, k == KD - 1)
                    pss.append(ps4)
                    post_norm(pss, n, gq_lat, ones_b, 512, None,
                              [V("QLN", QLN[j, :, q0:q0 + n]) for j in range(4)])
                for cg in range(2):
                    wt = load_w(w_q, 512 + cg * 512, 512)
                    for tile, go in zip(grp, goffs):
                        tl0, n, q0, _ = tile
                        for j in range(4):
                            hd = cg * 4 + j
                            ps = proj_fm(wt, j * 128, go, n, go)
                            post_norm([ps], n, [pcol("gq_diff")], bones_b, 64, tl0, [V("DQ", DQ[hd, :, q0:q0 + n])])
                for cg in range(8):
                    wt = load_w(w_q, 1536 + cg * 512, 512)
                    for tile, go in zip(grp, goffs):
                        tl0, n, q0, _ = tile
                        st4 = stg4.next()
                        for j in range(4):
                            ch = cg * 4 + j
                            ps = proj_fm(wt, j * 128, go, n, go)
                            act(st4[:, j, :n], ps[:, :n], AF.Sigmoid, bias=pcol("b_gate", ch), extra_reads=[par])
                        dma("sp", V("G", G[cg * 4:cg * 4 + 4, :, q0:q0 + n].rearrange("c p q -> p c q")),
                            V(st4.key, st4.ap[:, :, :n]), st4.key)
            kv_tiles = [(tile, go) for tile, go in zip(grp, goffs) if tile[3] is not None]
            wt = load_w(w_kv, 0, 384)
            for tile, go in kv_tiles:
                tl0, n, _, kv0 = tile
                pss = [proj_fm(wt, j * 128, go, n, go) for j in range(2)]
                post_norm(pss, n, gkv_lat, ones_b, 256, None, [V("KVN", KVN[j, :, kv0:kv0 + n]) for j in range(2)])
                ps = proj_fm(wt, 256, go, n, go)
                post_norm([ps], n, [pcol("gk_pe")], bones_b, 64, tl0, [V("KPE", KPE[:, kv0:kv0 + n])])
            for cg in range(2):
                wt = load_w(w_kv, 384 + cg * 512, 512)
                for tile, go in kv_tiles:
                    tl0, n, _, kv0 = tile
                    for j in range(4):
                        hd = cg * 4 + j
                        ps = proj_fm(wt, j * 128, go, n, go)
                        post_norm([ps], n, [pcol("gk_diff")], bones_b, 64, tl0, [V("DK", DK[hd, :, kv0:kv0 + n])])
            for cg in range(2):
                wt = load_w(w_kv, 1408 + cg * 512, 512)
                for tile, go in kv_tiles:
                    tl0, n, _, kv0 = tile
                    for b in range(n // 128):
                        ps = PSM.next()
                        for k in range(KD):
                            mm(ps, V(hT.key + str(go), hT.ap[:, k, go + b * 128:go + (b + 1) * 128]), wt[:, k, :],
                               k == 0, k == KD - 1)
                        ob = outb.next()
                        act(ob, ps, AF.Identity)
                        dma("sp", V("DV", DV[kv0 + b * 128:kv0 + (b + 1) * 128, cg * 512:(cg + 1) * 512]), ob, ob.key)
        P.barrier()

        A.off = B1_MARK
        qln = A.bf("qln", [4, NQ])
        kvn = A.bf("kvn", [2, S])
        wqu = A.bf("wqu", [4, 1536])
        wkvu = A.bf("wkvu", [2, 2048])
        sqb = Rot([A.bf("sqb%d" % i, [512]) for i in range(3)])
        tmpf = Rot([A.f32("tmpf%d" % i, [512]) for i in range(6)])
        ybf = Rot([A.bf("ybf%d" % i, [512]) for i in range(4)])
        yf = Rot([A.f32("yf%d" % i, [512]) for i in range(5)])
        rstds = Rot([A.f32("rstd%d" % i, [512]) for i in range(3)])
        outb = Rot([A.bf("outb%d" % i, [512]) for i in range(4)])
        dma("sp", qln, V("QLN", QLN.rearrange("c p q -> p c q")), "qln")
        dma("sp", kvn, V("KVN", KVN.rearrange("c p q -> p c q")), "kvn")
        dma("pool", wqu, V("w_q_up", w_q_up.rearrange("(k p) e -> p k e", p=128)), "wqu")
        for hh in range(2):
            dma("pool", V("wkvu", wkvu.ap[:, :, hh * 1024:(hh + 1) * 1024]),
                V("w_kv_up", w_kv_up.rearrange("(k p) e -> p k e", p=128)[:, :, hh * 1024:(hh + 1) * 1024]), "wkvu")
        q_tiles = [(t0, n) for (t0, n) in tiles_of(0, NO)] + [(NO, NH)]
        for (q0, n) in q_tiles:
            for hd in range(8):
                ps = PSM.next()
                for k in range(4):
                    mm(ps[:, :n], wqu[:, k, hd * 128:(hd + 1) * 128], qln[:, k, q0:q0 + n], k == 0, k == 3)
                post_norm([ps], n, [pcol("gq_nope")], ones_b, 128, None, [V("QN", QN[hd, :, q0:q0 + n])])
            for j in range(4):
                ps = PSM.next()
                for k in range(4):
                    mm(ps[:, :n], wqu[:, k, 1024 + j * 128:1024 + (j + 1) * 128], qln[:, k, q0:q0 + n], k == 0, k == 3)
                post_norm([ps], n, [pcol("gq_pe")], bones_b, 64, q0, [V("QPE", QPE[j, :, q0:q0 + n])])
        for (kv0, n) in tiles_of(0, S):
            for hd in range(8):
                ps = PSM.next()
                for k in range(2):
                    mm(ps[:, :n], wkvu[:, k, hd * 128:(hd + 1) * 128], kvn[:, k, kv0:kv0 + n], k == 0, k == 1)
                post_norm([ps], n, [pcol("gk_nope")], ones_b, 128, None, [V("KN", KN[hd, :, kv0:kv0 + n])])
            for b in range(n // 128):
                for cg in range(2):
                    ps = PSM.next()
                    for k in range(2):
                        mm(ps, kvn[:, k, kv0 + b * 128:kv0 + (b + 1) * 128], wkvu[:, k, 1024 + cg * 512:1024 + (cg + 1) * 512],
                           k == 0, k == 1)
                    ob = outb.next()
                    act(ob, ps, AF.Identity)
                    dma("sp", V("VM", VM[kv0 + b * 128:kv0 + (b + 1) * 128, cg * 512:(cg + 1) * 512]), ob, ob.key)
        P.barrier()

        A.reset()
        kpe = A.bf("kpe", [S])
        dma("sp", kpe, V("KPE", KPE[:, :]), "kpe")
        Kh = Rot([A.bf("Kh%d" % i, [S]) for i in range(2)])
        Vh = Rot([A.bf("Vh%d" % i, [NKB, 128]) for i in range(2)])
        Qh = Rot([A.bf("Qh%d" % i, [NQ]) for i in range(2)])
        Qp = Rot([A.bf("Qp%d" % i, [NQ]) for i in range(2)])
        Oh = Rot([A.bf("Oh%d" % i, [NQ]) for i in range(2)])
        Pb = Rot([A.bf("Pb%d" % i, [512]) for i in range(6)])
        recs = Rot([A.f32("rec%d" % i, [128]) for i in range(4)])
        odf = Rot([A.f32("odf%d" % i, [128]) for i in range(4)])
        sqc = Rot([A.bf("sqc%d" % i, [128]) for i in range(2)])

        qblocks = []
        for i in range(NOB):
            kl = [(l, "full") for l in range(i)] + [(NOB + l, "full") for l in range(i)] + [(NOB + i, "keep"), (i, "causal")]
            qblocks.append((i * 128, 128, kl))
        qblocks.append((NO, NH, [(l, "halo") for l in range(NKB)]))

        def exp_group(Sps, qn, grp_list, scale, pb):
            ng = len(grp_list)
            if qn == 128:
                act(pb[:, :ng * 128], Sps[:, :ng * 128], AF.Exp, scale=scale)
            else:
                act(V(pb.key, pb.ap.rearrange("p (j q) -> p j q", q=128)[:, :ng, :qn]),
                    V(Sps.key, Sps.ap.rearrange("p (j q) -> p j q", q=128)[:, :ng, :qn]), AF.Exp, scale=scale)
            for jj, (l, mode) in enumerate(grp_list):
                reg = pb[:, jj * 128:jj * 128 + qn]
                if mode == "keep":
                    ts("dve", reg, reg, pcol("keep"), None, ALU.mult, extra_reads=[par])
                elif mode == "causal":
                    tt("dve", reg, reg, causal_b, ALU.mult)
                elif mode == "halo":
                    tt("dve", reg, reg, halo_b[:, l * NH:(l + 1) * NH], ALU.mult)

        def load_head(Ksrc, Vsrc, Qsrc, hd):
            kh = Kh.next()
            dma("sp", kh, V("K", Ksrc[hd, :, :]), kh.key)
            vh = Vh.next()
            dma("sp", vh, V("Vs", Vsrc.rearrange("(j p) d -> p j d", p=128)[:, :, hd * 128:(hd + 1) * 128]), vh.key)
            qh = Qh.next()
            dma("sp", qh, V("Q", Qsrc[hd, :, :]), qh.key)
            return kh, vh, qh

        SC_MLA = 192.0 ** -0.5
        SC_DIFF = 64.0 ** -0.5
        SB = Rot([PS[0], PS[1]])
        for hd in range(8):
            kh, vh, qh = load_head(KN, VM, QN, hd)
            if hd % 2 == 0:
                qp = Qp.next()
                dma("sp", qp, V("QPE", QPE[hd // 2, :, :]), qp.key)
            hp = (hd % 2) * 64
            oh = Oh.next()
            for bi, (q0, qn, kl) in enumerate(qblocks):
                r0 = (bi % 4) * 128
                oacc = V("ps2_r%d" % (bi % 4), PS[2].ap[:, r0:r0 + qn])
                dacc = V("ps3_r%d" % (bi % 4), PS[3].ap[:, r0:r0 + qn])
                nk = len(kl)
                for g0 in range(0, nk, 4):
                    gl = kl[g0:g0 + 4]
                    Sps = SB.next()
                    for jj, (l, mode) in enumerate(gl):
                        reg = Sps[:, jj * 128:jj * 128 + qn]
                        mm(reg, kh[:, l * 128:(l + 1) * 128], qh[:, q0:q0 + qn], True, False)
                        mm(reg, kpe[hp:hp + 64, l * 128:(l + 1) * 128], qp[hp:hp + 64, q0:q0 + qn], False, True)
                    pb = Pb.next()
                    exp_group(Sps, qn, gl, SC_MLA, pb)
                    for jj, (l, mode) in enumerate(gl):
                        first = (g0 + jj == 0)
                        last = (g0 + jj == nk - 1)
                        mm(oacc, vh[:, l, :], pb[:, jj * 128:jj * 128 + qn], first, last)
                        mm(dacc, ones_b, pb[:, jj * 128:jj * 128 + qn], first, last)
                rc = recs.next()
                P.op("dve", lambda h, o=rc.ap[:, :qn], i=dacc.ap: h.reciprocal(out=o, in_=i), reads=[dacc], writes=[rc])
                tt("dve", oh[:, q0:q0 + qn], oacc, rc[:, :qn], ALU.mult)
            dma("sp", V("OM", OM[hd, :, :]), oh, oh.key)
        S1B = Rot([PS[0], PS[1]])
        S2B = Rot([PS[4], PS[5]])
        for hd in range(8):
            kh, vh, qh = load_head(DK, DV, DQ, hd)
            oh = Oh.next()
            for bi, (q0, qn, kl) in enumerate(qblocks):
                r0 = (bi % 4) * 128
                accs = [V("ps%d_r%d" % (b_, bi % 4), PS[b_].ap[:, r0:r0 + qn]) for b_ in (2, 3, 6, 7)]
                o1, d1, o2, d2 = accs
                nk = len(kl)
                for g0 in range(0, nk, 4):
                    gl = kl[g0:g0 + 4]
                    S1 = S1B.next()
                    S2 = S2B.next()
                    for jj, (l, mode) in enumerate(gl):
                        mm(S1[:, jj * 128:jj * 128 + qn], kh[0:64, l * 128:(l + 1) * 128], qh[0:64, q0:q0 + qn], True, True)
                        mm(S2[:, jj * 128:jj * 128 + qn], kh[64:128, l * 128:(l + 1) * 128], qh[64:128, q0:q0 + qn], True, True)
                    p1 = Pb.next()
                    p2 = Pb.next()
                    exp_group(S1, qn, gl, SC_DIFF, p1)
                    exp_group(S2, qn, gl, SC_DIFF, p2)
                    for jj, (l, mode) in enumerate(gl):
                        first = (g0 + jj == 0)
                        last = (g0 + jj == nk - 1)
                        mm(o1, vh[:, l, :], p1[:, jj * 128:jj * 128 + qn], first, last)
                        mm(d1, ones_b, p1[:, jj * 128:jj * 128 + qn], first, last)
                        mm(o2, vh[:, l, :], p2[:, jj * 128:jj * 128 + qn], first, last)
                        mm(d2, ones_b, p2[:, jj * 128:jj * 128 + qn], first, last)
                r1 = recs.next()
                r2 = recs.next()
                P.op("dve", lambda h, o=r1.ap[:, :qn], i=d1.ap: h.reciprocal(out=o, in_=i), reads=[d1], writes=[r1])
                P.op("dve", lambda h, o=r2.ap[:, :qn], i=d2.ap: h.reciprocal(out=o, in_=i), reads=[d2], writes=[r2])
                ts("dve", r2[:, :qn], r2[:, :qn], neglam, None, ALU.mult, extra_reads=[lamv])
                t1 = odf.next()
                tt("dve", t1[:, :qn], o1, r1[:, :qn], ALU.mult)
                t2 = odf.next()
                tt("dve", t2[:, :qn], o2, r2[:, :qn], ALU.mult)
                tt("pool", t1[:, :qn], t1[:, :qn], t2[:, :qn], ALU.add)
                sq = sqc.next()
                act(sq[:, :qn], t1[:, :qn], AF.Square)
                psn = S1B.next()
                mm(psn[:, :qn], ones_b, sq[:, :qn], True, True)
                rs = recs.next()
                rstd_from(psn[:, :qn], rs[:, :qn], 128)
                stt("dve", oh[:, q0:q0 + qn], t1[:, :qn], gsub8.ap[:, 0:1], rs[:, :qn], ALU.mult, ALU.mult,
                    extra_reads=[gsub8])
            dma("sp", V("OD", OD[hd, :, :]), oh, oh.key)
        P.barrier()

        A.reset()
        omt = A.bf("omt", [8, 512])
        odt = A.bf("odt", [8, 512])
        gts = Rot([A.bf("gt%d" % i, [8, 512]) for i in range(2)])
        mixed = A.bf("mixed", [16, 512])
        x1t = A.f32("x1t", [16, 512])
        xcs = Rot([A.f32("xcD%d" % i, [512]) for i in range(3)])
        h2t = A.bf("h2t", [16, 512])
        wos = Rot([A.bf("wo%d" % i, [8, 512]) for i in range(4)])
        wslots = Rot([A.bf("wD%d" % i, [16, 512]) for i in range(2)])
        tmpf = Rot([A.f32("tmpfD%d" % i, [512]) for i in range(6)])
        sqb = Rot([A.bf("sqbD%d" % i, [512]) for i in range(3)])
        rstds = Rot([A.f32("rstdD%d" % i, [512]) for i in range(2)])
        PSMD = Rot([PS[0], PS[1], PS[2], PS[3]])
        PSO = Rot([PS[4], PS[5]])
        PSX = PS[6]
        for (q0, n) in q_tiles:
            dma("sp", V(omt.key, omt.ap[:, :, :n]), V("OM", OM[:, :, q0:q0 + n].rearrange("h p q -> p h q")), omt.key)
            dma("sp", V(odt.key, odt.ap[:, :, :n]), V("OD", OD[:, :, q0:q0 + n].rearrange("h p q -> p h q")), odt.key)
            for eg in range(4):
                wm = wos.next()
                dma("pool", wm, V("w", w_o_mla.rearrange("(k p) e -> p k e", p=128)[:, :, eg * 512:(eg + 1) * 512]), wm.key)
                wd_ = wos.next()
                dma("pool", wd_, V("w", w_o_diff.rearrange("(k p) e -> p k e", p=128)[:, :, eg * 512:(eg + 1) * 512]), wd_.key)
                gt = gts.next()
                for ab in range(2):
                    dma("sp", V(gt.key, gt.ap[:, ab * 4:ab * 4 + 4, :n]),
                        V("G", G[ab * 16 + eg * 4:ab * 16 + eg * 4 + 4, :, q0:q0 + n].rearrange("h p q -> p h q")), gt.key)
                for j in range(4):
                    e = eg * 4 + j
                    psm = PSMD.next()
                    for k in range(8):
                        mm(psm[:, :n], wm[:, k, j * 128:(j + 1) * 128], omt[:, k, :n], k == 0, k == 7)
                    psd = PSMD.next()
                    for k in range(8):
                        mm(psd[:, :n], wd_[:, k, j * 128:(j + 1) * 128], odt[:, k, :n], k == 0, k == 7)
                    t1 = tmpf.next()
                    tt("dve", t1[:, :n], psm[:, :n], gt[:, j, :n], ALU.mult)
                    t2 = tmpf.next()
                    tt("dve", t2[:, :n], psd[:, :n], gt[:, 4 + j, :n], ALU.mult)
                    tt("pool", mixed[:, e, :n], t1[:, :n], t2[:, :n], ALU.add)
            for eg in range(4):
                wt = wslots.next()
                dma("pool", wt, V("w", w_out.rearrange("(k p) e -> p k e", p=128)[:, :, eg * 512:(eg + 1) * 512]), wt.key)
                for j in range(4):
                    e = eg * 4 + j
                    ps = PSO.next()
                    for k in range(KD):
                        mm(ps[:, :n], wt[:, k, j * 128:(j + 1) * 128], mixed[:, k, :n], k == 0, k == KD - 1)
                    xc = xcs.next()
                    dma("sp", V(xc.key, xc.ap[:, :n]), V("xT", xT[e * 128:(e + 1) * 128, q0:q0 + n]), xc.key)
                    stt("dve", x1t[:, e, :n], ps[:, :n], gate1(e), xc[:, :n], ALU.mult, ALU.add, extra_reads=[modT])
            if q0 < NO:
                dma("sp", V("X1", X1[:, :, q0:q0 + n].rearrange("c p q -> p c q")), V(x1t.key, x1t.ap[:, :, :n]), x1t.key)
            for k in range(KD):
                sq = sqb.next()
                act(sq[:, :n], x1t[:, k, :n], AF.Square)
                mm(PSX[:, :n], ones_b, sq[:, :n], k == 0, k == KD - 1)
            rs = rstds.next()
            rstd_from(PSX[:, :n], rs[:, :n], D)
            for k in range(KD):
                t = tmpf.next()
                stt("dve", t[:, :n], x1t[:, k, :n], a2.ap[:, k:k + 1], rs[:, :n], ALU.mult, ALU.mult, extra_reads=[a2])
                if q0 >= NO:
                    act(t[:, :n], t[:, :n], AF.Identity, bias=shift2(k), extra_reads=[modT])
                    tt("dve", h2t[:, k, :n], t[:, :n], V("par", pcol("hvalid", 0, NH)), ALU.mult)
                else:
                    act(h2t[:, k, :n], t[:, :n], AF.Identity, bias=shift2(k), extra_reads=[modT])
            dma("sp", V("H2", H2[:, :, q0:q0 + n].rearrange("c p q -> p c q")), V(h2t.key, h2t.ap[:, :, :n]), h2t.key)
        P.barrier()

        A.reset()
        h2 = A.bf("h2", [16, NQ])
        dma("sp", h2, V("H2", H2.rearrange("c p q -> p c q")), "h2")
        wv_s = Rot([A.bf("wv%d" % i, [16, 256]) for i in range(2)])
        wg_s = Rot([A.bf("wg%d" % i, [16, 256]) for i in range(2)])
        uext = Rot([A.f32("uext%d" % i, [NOB, 130]) for i in range(4)])
        ycv = Rot([A.f32("ycv%d" % i, [NOB, 128]) for i in range(4)])
        zb = Rot([A.bf("zb%d" % i, [NOB, 128]) for i in range(2)])
        PSE = Rot(PS)
        own_q_tiles = tiles_of(0, NO)

        def conv_chunk(wt, wc0, ch):
            ue = uext.next()
            for (q0, n) in own_q_tiles:
                ps = PSE.next()
                for k in range(KD):
                    mm(ps[:, :n], wt[:, k, wc0:wc0 + 128], h2[:, k, q0:q0 + n], k == 0, k == KD - 1)
                nb = n // 128
                b0 = q0 // 128
                act(V(ue.key, ue.ap[:, b0:b0 + nb, 2:130]), V(ps.key, ps.ap[:, :n].rearrange("p (b t) -> p b t", t=128)),
                    AF.Identity)
            ps = PSE.next()
            for k in range(KD):
                mm(ps[:, :NH], wt[:, k, wc0:wc0 + 128], h2[:, k, NO:NO + NH], k == 0, k == KD - 1)
            cp("dve", V(ue.key, ue.ap[:, :, 0:2]), V(ps.key, ps.ap[:, :NH].rearrange("p (b t) -> p b t", t=2)))
            cw = lambda j: pcol("conv_w", j * 88 + ch)
            y = ycv.next()
            act(y, V(ue.key, ue.ap[:, :, 2:130]), AF.Identity, bias=pcol("conv_b", ch), scale=cw(2), extra_reads=[par])
            stt("dve", y, V(ue.key, ue.ap[:, :, 1:129]), cw(1), y, ALU.mult, ALU.add, extra_reads=[par])
            stt("dve", y, V(ue.key, ue.ap[:, :, 0:128]), cw(0), y, ALU.mult, ALU.add, extra_reads=[par])
            return y

        for cp0 in range(0, 44, 2):
            wv = wv_s.next()
            dma("pool", wv, V("w", w_up.rearrange("(k p) e -> p k e", p=128)[:, :, cp0 * 128:cp0 * 128 + 256]), wv.key)
            wg = wg_s.next()
            dma("pool", wg, V("w", w_up.rearrange("(k p) e -> p k e", p=128)[:, :, DFF + cp0 * 128:DFF + cp0 * 128 + 256]), wg.key)
            for cc in range(2):
                c = cp0 + cc
                yv = conv_chunk(wv, cc * 128, c)
                yg = conv_chunk(wg, cc * 128, 44 + c)
                act(yg, yg, AF.Silu)
                z = zb.next()
                tt("dve", z, yg, yv, ALU.mult)
                dma("sp", V("Z", Z[c, :, :]), V(z.key, z.ap.rearrange("p b t -> p (b t)")), z.key)
        P.barrier()

        A.reset()
        wd_s = Rot([A.bf("wdn%d" % i, [44, 512]) for i in range(2)])
        zt_s = Rot([A.bf("zt%d" % i, [44, 512]) for i in range(2)])
        x1c = Rot([A.f32("x1c%d" % i, [512]) for i in range(2)])
        oc = Rot([A.f32("oc%d" % i, [512]) for i in range(2)])
        PSE = Rot(PS)
        for eg in range(4):
            wd = wd_s.next()
            dma("pool", wd, V("w", w_down.rearrange("(k p) e -> p k e", p=128)[:, :, eg * 512:(eg + 1) * 512]), wd.key)
            for (q0, n) in own_q_tiles:
                zt = zt_s.next()
                dma("sp", V(zt.key, zt.ap[:, :, :n]), V("Z", Z[:, :, q0:q0 + n].rearrange("c p q -> p c q")), zt.key)
                for j in range(4):
                    e = eg * 4 + j
                    ps = PSE.next()
                    for c in range(44):
                        mm(ps[:, :n], wd[:, c, j * 128:(j + 1) * 128], zt[:, c, :n], c == 0, c == 43)
                    xc = x1c.next()
                    dma("sp", V(xc.key, xc.ap[:, :n]), V("X1", X1[e, :, q0:q0 + n]), xc.key)
                    o = oc.next()
                    stt("dve", o[:, :n], ps[:, :n], gate2(e), xc[:, :n], ALU.mult, ALU.add, extra_reads=[modT])
                    od_ = dma("sp", V("outT", outT[e * 128:(e + 1) * 128, q0:q0 + n]), V(o.key, o.ap[:, :n]), o.key)
                    P.out_dmas.append(od_)
        fin = P.op("sp", None)
        fin.deps.update(o.idx for o in P.out_dmas)
        P.emit(nc, stack)
    return nc


def host_prep(inp, S):
    NO = S // 2
    NOB = NO // 128
    NH = 2 * NOB
    NKB = 2 * NOB
    POFF, NPAR = par_layout(NH)
    f32 = np.float32
    x = np.asarray(inp["x"], f32)
    pos = np.asarray(inp["positions"], np.int32)
    B = x.shape[0]

    def fm(v):
        v = np.asarray(v, f32).reshape(-1, 128)
        return np.ascontiguousarray(v.T)

    def rep(v):
        v = np.asarray(v, f32).reshape(1, -1)
        return np.repeat(v, 128, axis=0)

    w_in = np.asarray(inp["w_in"][0], f32)
    q_lat, kv_lat, k_pe, dq, dk, dv, gl = np.split(w_in, np.cumsum([512, 256, 64, 1024, 1024, 1024])[:], axis=1)
    w_q = np.ascontiguousarray(np.concatenate([q_lat, dq, gl], axis=1))
    w_kv = np.ascontiguousarray(np.concatenate([kv_lat, k_pe, k_pe, dk, dv], axis=1))
    wqu = np.asarray(inp["w_q_up"][0], f32).reshape(512, 8, 192)
    w_q_up = np.ascontiguousarray(np.concatenate([wqu[:, :, :128].reshape(512, 1024), wqu[:, :, 128:].reshape(512, 512)], axis=1))
    wkvu = np.asarray(inp["w_kv_up"][0], f32).reshape(256, 8, 256)
    w_kv_up = np.ascontiguousarray(np.concatenate([wkvu[:, :, :128].reshape(256, 1024), wkvu[:, :, 128:].reshape(256, 1024)], axis=1))

    ones = np.ones((128, 128), f32)
    bones = np.zeros((128, 128), f32)
    bones[:64, :64] = 1
    bones[64:, 64:] = 1
    perm = np.zeros((128, 128), f32)
    for m in range(128):
        if (m % 64) < 32:
            perm[m + 32, m] = -1.0
        else:
            perm[m - 32, m] = 1.0
    causal = (np.arange(128)[None, :] >= np.arange(128)[:, None]).astype(f32)
    invfreq = (10000.0 ** (-(np.arange(0, 64, 2, dtype=f32)) / f32(64))).astype(f32)
    invf128 = np.tile(invfreq, 4).reshape(128, 1)

    gq = np.asarray(inp["g_q_mla"][0], f32)
    gk = np.asarray(inp["g_k_mla"][0], f32)
    shared = {
        "b_ada": fm(inp["b_ada"][0]), "g1": fm(inp["g_norm1"][0]), "g2": fm(inp["g_norm2"][0]),
        "b_gate": fm(inp["b_gate"][0]),
        "conv_w": np.concatenate([fm(inp["conv_w"][0][j]) for j in range(3)], axis=1),
        "conv_b": fm(inp["conv_b"][0]),
        "g_q_lat": fm(inp["g_q_lat"][0]), "g_kv_lat": fm(inp["g_kv_lat"][0]),
        "gq_nope": gq[:128].reshape(128, 1), "gq_pe": np.tile(gq[128:], 2).reshape(128, 1),
        "gk_nope": gk[:128].reshape(128, 1), "gk_pe": np.tile(gk[128:], 2).reshape(128, 1),
        "gq_diff": np.tile(np.asarray(inp["g_q_diff"][0], f32), 2).reshape(128, 1),
        "gk_diff": np.tile(np.asarray(inp["g_k_diff"][0], f32), 2).reshape(128, 1),
        "g_sub": np.asarray(inp["g_sub_diff"][0], f32).reshape(128, 1),
        "invfreq": invf128,
        "lq1": rep(inp["lam_q1"][0]), "lk1": rep(inp["lam_k1"][0]),
        "lq2": rep(inp["lam_q2"][0]), "lk2": rep(inp["lam_k2"][0]),
    }
    big = {
        "w_ada": np.ascontiguousarray(np.asarray(inp["w_ada"][0], f32)),
        "w_q": w_q, "w_kv": w_kv, "w_q_up": w_q_up, "w_kv_up": w_kv_up,
        "w_o_mla": np.ascontiguousarray(np.asarray(inp["w_o_mla"][0], f32)),
        "w_o_diff": np.ascontiguousarray(np.asarray(inp["w_o_diff"][0], f32)),
        "w_out": np.ascontiguousarray(np.asarray(inp["w_out"][0], f32)),
        "w_up": np.ascontiguousarray(np.asarray(inp["w_up"][0], f32)),
        "w_down": np.ascontiguousarray(np.asarray(inp["w_down"][0], f32)),
    }
    in_maps = []
    own_idx = []
    for core in range(NCORES):
        b, c = core // 2, core % 2
        own_blocks = [2 * i + c for i in range(NOB)]
        oth_blocks = [2 * i + (1 - c) for i in range(NOB)]
        own_tok = np.concatenate([np.arange(j * 128, (j + 1) * 128) for j in own_blocks])
        oth_tok = np.concatenate([np.arange(j * 128, (j + 1) * 128) for j in oth_blocks])
        halo_tok = []
        hvalid = []
        for j in own_blocks:
            for d_ in (2, 1):
                t = j * 128 - d_
                if t >= 0:
                    halo_tok.append(t)
                    hvalid.append(1.0)
                else:
                    halo_tok.append(2 - d_)
                    hvalid.append(0.0)
        halo_tok = np.array(halo_tok)
        tl = np.concatenate([own_tok, halo_tok, oth_tok])
        kv_tok = np.concatenate([own_tok, oth_tok])
        xT = np.ascontiguousarray(x[b][tl].T)
        posr = np.ascontiguousarray(np.repeat(pos[b][tl].reshape(1, -1), 128, axis=0)).astype(np.int32)
        hm = (kv_tok.reshape(NKB, 128).T[:, :, None] <= halo_tok[None, None, :]).astype(f32).reshape(128, NKB * NH)
        cst = np.ascontiguousarray(np.concatenate([ones, bones, perm, causal, hm], axis=1))
        par = np.zeros((128, NPAR), f32)
        for name, arr in shared.items():
            o, n = POFF[name]
            par[:, o:o + n] = arr
        o, n = POFF["c"]
        par[:, o:o + n] = fm(np.asarray(inp["c"], f32)[b])
        o, n = POFF["keep"]
        par[:, o] = 1.0 if c == 1 else 0.0
        o, n = POFF["hvalid"]
        par[:, o:o + n] = np.asarray(hvalid, f32)[None, :]
        m = {"xT": xT, "params": par, "posrep": posr, "consts": cst}
        m.update(big)
        in_maps.append(m)
        own_idx.append((b, own_tok))
    return in_maps, own_idx


_NC_CACHE = {}


def kernel(**inputs):
    S = int(np.asarray(inputs["x"]).shape[1])
    if S not in _NC_CACHE:
        _NC_CACHE[S] = build(S)
    nc = _NC_CACHE[S]
    in_maps, own_idx = host_prep(inputs, S)
    res = run_bass_kernel_spmd(nc, in_maps, core_ids=list(range(NCORES)))
    B = np.asarray(inputs["x"]).shape[0]
    out = np.empty((B, S, D), np.float32)
    for core in range(NCORES):
        b, own_tok = own_idx[core]
        out[b, own_tok, :] = np.asarray(res.results[core]["outT"]).T
    return out
```

```python
import math
import numpy as np
import concourse.bass as bass
import concourse.mybir as mybir
from concourse.bass_utils import run_bass_kernel_spmd

F32 = mybir.dt.float32
BF16 = mybir.dt.bfloat16
I32 = mybir.dt.int32
AF = mybir.ActivationFunctionType
ALU = mybir.AluOpType
AX = mybir.AxisListType

D = 2048
KD = 16
DFF = 5632
NCORES = 8
EPS = 1e-6
LAMBDA_INIT = 0.8 - 0.6 * math.exp(-0.3 * 0)
SAME_ENGINE_WAITS = True


class V:
    __slots__ = ("key", "ap")

    def __init__(self, key, ap):
        self.key = key
        self.ap = ap

    def __getitem__(self, idx):
        return V(self.key, self.ap[idx])

    def k(self, key):
        return V(key, self.ap)

    def re(self, s, **kw):
        return V(self.key, self.ap.rearrange(s, **kw))


class Op:
    __slots__ = ("eng", "fn", "deps", "dma", "semkey", "idx", "ticket", "need_inc", "semval")

    def __init__(self, eng, fn, dma, semkey):
        self.eng = eng
        self.fn = fn
        self.deps = set()
        self.dma = dma
        self.semkey = semkey
        self.ticket = None
        self.need_inc = False
        self.semval = None


class Prog:
    ENGS = ("pe", "act", "dve", "pool", "sp")

    def __init__(self):
        self.ops = []
        self.state = {}
        self.eng_ops = {e: [] for e in self.ENGS}
        self.bar_deps = set()
        self.bar_seen = {e: True for e in self.ENGS}
        self.dma_since_bar = []
        self.out_dmas = []

    def op(self, eng, fn, reads=(), writes=(), dma=False, semkey=None):
        o = Op(eng, fn, dma, semkey)
        o.idx = len(self.ops)
        for v in reads:
            st = self.state.get(v.key)
            if st is not None:
                o.deps.update(st["w"])
        for v in writes:
            st = self.state.setdefault(v.key, {"w": [], "r": [], "pend": []})
            if st["r"]:
                st["pend"] = st["r"] + st["w"]
                st["w"] = []
                st["r"] = []
            o.deps.update(st["pend"])
        for v in reads:
            st = self.state.setdefault(v.key, {"w": [], "r": [], "pend": []})
            st["r"].append(o.idx)
        for v in writes:
            self.state[v.key]["w"].append(o.idx)
        if not self.bar_seen[eng]:
            o.deps.update(self.bar_deps)
            self.bar_seen[eng] = True
        o.deps.discard(o.idx)
        self.ops.append(o)
        self.eng_ops[eng].append(o)
        if dma:
            self.dma_since_bar.append(o.idx)
        return o

    def barrier(self):
        deps = set(self.dma_since_bar)
        for e in self.ENGS:
            if self.eng_ops[e]:
                deps.add(self.eng_ops[e][-1].idx)
        self.bar_deps = deps
        self.bar_seen = {e: False for e in self.ENGS}
        self.dma_since_bar = []
        self.state = {}

    def emit(self, nc, stack):
        ops = self.ops
        waits = {}
        for e in self.ENGS:
            waited = {}
            for o in self.eng_ops[e]:
                best = {}
                for d in o.deps:
                    p = ops[d]
                    if p.dma:
                        key = ("dma", p.semkey)
                    else:
                        if p.eng == e and (e == "pe" or not SAME_ENGINE_WAITS):
                            continue
                        key = ("eng", p.eng)
                    if key not in best or best[key] < d:
                        best[key] = d
                lst = []
                for key, d in best.items():
                    if waited.get(key, -1) >= d:
                        continue
                    waited[key] = d
                    lst.append(d)
                    if not ops[d].dma:
                        ops[d].need_inc = True
                waits[o.idx] = lst
        cnt = {e: 0 for e in self.ENGS}
        dcnt = {}
        for o in ops:
            if o.dma:
                dcnt[o.semkey] = dcnt.get(o.semkey, 0) + 16
                o.semval = dcnt[o.semkey]
            elif o.need_inc:
                cnt[o.eng] += 1
                o.ticket = cnt[o.eng]
        esem = {e: stack.enter_context(nc.semaphore("s_" + e)) for e in self.ENGS}
        dsem = {}
        for i, k in enumerate(sorted(dcnt.keys())):
            dsem[k] = stack.enter_context(nc.semaphore("d%d" % i))
        self.nsem = len(esem) + len(dsem)
        block = stack.enter_context(nc.Block())

        def run(e, h):
            for o in self.eng_ops[e]:
                for d in waits[o.idx]:
                    p = ops[d]
                    if p.dma:
                        h.wait_ge(dsem[p.semkey], p.semval)
                    else:
                        h.wait_ge(esem[p.eng], p.ticket)
                if o.fn is None:
                    continue
                ins = o.fn(h)
                if o.dma:
                    ins.then_inc(dsem[o.semkey], 16)
                elif o.need_inc:
                    ins.then_inc(esem[o.eng], 1)

        @block.tensor
        def _(h):
            run("pe", h)

        @block.scalar
        def _(h):
            run("act", h)

        @block.vector
        def _(h):
            run("dve", h)

        @block.gpsimd
        def _(h):
            run("pool", h)

        @block.sync
        def _(h):
            run("sp", h)


class Rot:
    def __init__(self, items):
        self.items = items
        self.i = 0

    def next(self):
        x = self.items[self.i % len(self.items)]
        self.i += 1
        return x


def par_layout(NH):
    off = {}
    c = 0
    for name, n in [("b_ada", 96), ("g1", 16), ("g2", 16), ("b_gate", 32), ("conv_w", 3 * 88), ("conv_b", 88),
                    ("g_q_lat", 4), ("g_kv_lat", 2), ("gq_nope", 1), ("gq_pe", 1), ("gk_nope", 1), ("gk_pe", 1),
                    ("gq_diff", 1), ("gk_diff", 1), ("g_sub", 1), ("invfreq", 1), ("lq1", 64), ("lk1", 64),
                    ("lq2", 64), ("lk2", 64), ("c", 16), ("keep", 1), ("hvalid", NH)]:
        off[name] = (c, n)
        c += n
    return off, c


def build(S, debug=False):
    from contextlib import ExitStack
    NO = S // 2
    NOB = NO // 128
    NH = 2 * NOB
    NQ = NO + NH
    NTOK = 2 * NO + NH
    NKB = 2 * NOB
    POFF, NPAR = par_layout(NH)

    nc = bass.Bass("TRN2", target_bir_lowering=False)
    P = Prog()

    def din(name, shape, dt=F32):
        return nc.dram_tensor(name, list(shape), dt, kind="ExternalInput").ap()

    def dscr(name, shape, dt):
        return nc.dram_tensor(name, list(shape), dt, kind="Internal").ap()

    xT = din("xT", [D, NTOK])
    params = din("params", [128, NPAR])
    posrep = din("posrep", [128, NTOK], I32)
    consts = din("consts", [128, 4 * 128 + NKB * NH])
    w_ada = din("w_ada", [D, 6 * D])
    w_q = din("w_q", [D, 5632])
    w_kv = din("w_kv", [D, 2432])
    w_q_up = din("w_q_up", [512, 1536])
    w_kv_up = din("w_kv_up", [256, 2048])
    w_o_mla = din("w_o_mla", [1024, D])
    w_o_diff = din("w_o_diff", [1024, D])
    w_out = din("w_out", [D, D])
    w_up = din("w_up", [D, 2 * DFF])
    w_down = din("w_down", [DFF, D])
    outT = nc.dram_tensor("outT", [D, NO], F32, kind="ExternalOutput").ap()

    QLN = dscr("QLN", [4, 128, NQ], BF16)
    KVN = dscr("KVN", [2, 128, S], BF16)
    KPE = dscr("KPE", [128, S], BF16)
    QN = dscr("QN", [8, 128, NQ], BF16)
    QPE = dscr("QPE", [4, 128, NQ], BF16)
    KN = dscr("KN", [8, 128, S], BF16)
    VM = dscr("VM", [S, 1024], BF16)
    DQ = dscr("DQ", [8, 128, NQ], BF16)
    DK = dscr("DK", [8, 128, S], BF16)
    DV = dscr("DV", [S, 1024], BF16)
    G = dscr("G", [32, 128, NQ], BF16)
    OM = dscr("OM", [8, 128, NQ], BF16)
    OD = dscr("OD", [8, 128, NQ], BF16)
    X1 = dscr("X1", [16, 128, NO], F32)
    H2 = dscr("H2", [16, 128, NQ], BF16)
    Z = dscr("Z", [44, 128, NO], BF16)

    stack = ExitStack()
    with stack:
        PERS_F = 1024 + NPAR + 64
        ARENA_W = 48000
        pers = stack.enter_context(nc.sbuf_tensor("pers", [128, PERS_F], F32))
        cbf = stack.enter_context(nc.sbuf_tensor("cbf", [128, 4 * 128 + NKB * NH], BF16))
        arena = stack.enter_context(nc.sbuf_tensor("arena", [128, ARENA_W], F32))
        arena_b = arena.bitcast(BF16)
        arena_i = arena.bitcast(I32)
        psb = [stack.enter_context(nc.psum_tensor("ps%d" % i, [128, 512], F32)) for i in range(8)]
        PS = [V("ps%d" % i, psb[i][:]) for i in range(8)]

        class Arena:
            def __init__(self):
                self.off = 0

            def reset(self):
                self.off = 0

            def f32(self, key, shape):
                n = int(np.prod(shape))
                v = arena[:, self.off:self.off + n]
                self.off += n
                assert self.off <= ARENA_W, (key, self.off)
                return V(key, _shape(v, shape))

            def bf(self, key, shape):
                n = int(np.prod(shape))
                nw = (n + 1) // 2
                v = arena_b[:, 2 * self.off:2 * self.off + n]
                self.off += nw
                assert self.off <= ARENA_W, (key, self.off)
                return V(key, _shape(v, shape))

            def i32(self, key, shape):
                n = int(np.prod(shape))
                v = arena_i[:, self.off:self.off + n]
                self.off += n
                assert self.off <= ARENA_W, (key, self.off)
                return V(key, _shape(v, shape))

        def _shape(ap, shape):
            if len(shape) == 1:
                return ap
            if len(shape) == 2:
                return ap.rearrange("p (a b) -> p a b", a=shape[0])
            if len(shape) == 3:
                return ap.rearrange("p (a b c) -> p a b c", a=shape[0], b=shape[1])
            raise ValueError

        A = Arena()

        par = V("par", pers[:, 0:NPAR])
        ppos = NPAR

        def pcol(name, j=0, n=1):
            o, _ = POFF[name]
            return pers[:, o + j:o + j + n]

        def persf(key, n):
            nonlocal ppos
            v = V(key, pers[:, ppos:ppos + n])
            ppos += n
            assert ppos <= PERS_F
            return v

        modT = persf("modT", 96)
        a1 = persf("a1", 16)
        a2 = persf("a2", 16)
        lamv = persf("lamv", 8)
        gsub8 = persf("gsub8", 1)
        ones_b = V("cbf", cbf[:, 0:128])
        bones_b = V("cbf", cbf[:, 128:256])
        perm_b = V("cbf", cbf[:, 256:384])
        causal_b = V("cbf", cbf[:, 384:512])
        halo_b = V("cbf", cbf[:, 512:512 + NKB * NH])

        def dma(eng, out, in_, semkey, store=False):
            if store:
                eng = "act"
            return P.op(eng, lambda h, o=out.ap, i=in_.ap: h.dma_start(out=o, in_=i),
                        reads=[in_], writes=[out], dma=True, semkey=semkey)

        def mm(out, lhsT, rhs, start, stop):
            return P.op("pe", lambda h, o=out.ap, l=lhsT.ap, r=rhs.ap: h.matmul(o, lhsT=l, rhs=r, start=start, stop=stop),
                        reads=[lhsT, rhs], writes=[out])

        def act(out, in_, func, bias=None, scale=None, extra_reads=()):
            kw = {}
            if bias is not None:
                kw["bias"] = bias
            if scale is not None:
                kw["scale"] = scale
            return P.op("act", lambda h, o=out.ap, i=in_.ap: h.activation(out=o, in_=i, func=func, **kw),
                        reads=[in_] + list(extra_reads), writes=[out])

        def tt(eng, out, in0, in1, op):
            return P.op(eng, lambda h, o=out.ap, a=in0.ap, b=in1.ap: h.tensor_tensor(out=o, in0=a, in1=b, op=op),
                        reads=[in0, in1], writes=[out])

        def ts(eng, out, in0, s1, s2, op0, op1=None, extra_reads=()):
            def f(h, o=out.ap, a=in0.ap):
                if op1 is None:
                    return h.tensor_scalar(out=o, in0=a, scalar1=s1, scalar2=None, op0=op0)
                return h.tensor_scalar(out=o, in0=a, scalar1=s1, scalar2=s2, op0=op0, op1=op1)
            return P.op(eng, f, reads=[in0] + list(extra_reads), writes=[out])

        def stt(eng, out, in0, scalar, in1, op0, op1, extra_reads=()):
            return P.op(eng, lambda h, o=out.ap, a=in0.ap, b=in1.ap: h.scalar_tensor_tensor(
                out=o, in0=a, scalar=scalar, in1=b, op0=op0, op1=op1),
                reads=[in0, in1] + list(extra_reads), writes=[out])

        def cp(eng, out, in_):
            return P.op(eng, lambda h, o=out.ap, i=in_.ap: h.tensor_copy(out=o, in_=i), reads=[in_], writes=[out])

        def rstd_from(ps_view, out, dim):
            ts("dve", out, ps_view, 1.0 / dim, EPS, ALU.mult, ALU.add)
            act(out, out, AF.Ln)
            act(out, out, AF.Exp, scale=-0.5)

        cosT = A.f32("cosT", [NTOK])
        sinT = A.f32("sinT", [NTOK])
        B1_MARK = A.off
        dma("sp", par, V("params", params[:, :]), "par")
        cst_f = A.f32("cst_f", [4 * 128 + NKB * NH])
        dma("sp", cst_f, V("consts", consts[:, :]), "cst_f")
        cp("dve", V("cbf", cbf[:]), cst_f)
        csil = A.bf("csil", [16])
        act(csil, V("par", pcol("c", 0, 16)), AF.Silu)
        wa_slots = Rot([A.bf("wa%d" % i, [16, 512]) for i in range(2)])
        psA = PS[0]
        for g in range(24):
            wt = wa_slots.next()
            dma("pool", wt, V("w_ada", w_ada.rearrange("(k p) e -> p k e", p=128)[:, :, g * 512:(g + 1) * 512]), wt.key)
            for j in range(4):
                col = g * 4 + j
                for k in range(KD):
                    mm(psA[:, col:col + 1], wt[:, k, j * 128:(j + 1) * 128], csil[:, k:k + 1], k == 0, k == KD - 1)
        tt("dve", modT, psA[:, 0:96], V("par", pcol("b_ada", 0, 96)), ALU.add)
        stt("dve", a1, modT[:, 16:32], 1.0, V("par", pcol("g1", 0, 16)), ALU.add, ALU.mult)
        stt("dve", a2, modT[:, 64:80], 1.0, V("par", pcol("g2", 0, 16)), ALU.add, ALU.mult)
        shift1 = lambda k: modT.ap[:, k:k + 1]
        gate1 = lambda k: modT.ap[:, 32 + k:33 + k]
        shift2 = lambda k: modT.ap[:, 48 + k:49 + k]
        gate2 = lambda k: modT.ap[:, 80 + k:81 + k]
        ltmp = A.f32("ltmp", [64])
        for i, (qn_, kn_) in enumerate((("lq1", "lk1"), ("lq2", "lk2"))):
            tt("dve", ltmp, V("par", pcol(qn_, 0, 64)), V("par", pcol(kn_, 0, 64)), ALU.mult)
            P.op("dve", lambda h, o=lamv.ap[:, i:i + 1], a=ltmp.ap: h.reduce_sum(out=o, in_=a, axis=AX.X),
                 reads=[ltmp], writes=[lamv])
        act(lamv[:, 2:4], lamv[:, 0:2], AF.Exp)
        stt("dve", lamv[:, 4:5], lamv[:, 3:4], -LAMBDA_INIT, lamv[:, 2:3], ALU.add, ALU.subtract)
        ts("dve", gsub8, V("par", pcol("g_sub")), 1.0 - LAMBDA_INIT, None, ALU.mult)
        neglam = lamv.ap[:, 4:5]
        posi = A.i32("posi", [NTOK])
        angT = A.f32("angT", [NTOK])
        dma("sp", posi, V("posrep", posrep[:, :]), "posi")
        cp("dve", angT, posi)
        ts("dve", angT, angT, pcol("invfreq"), None, ALU.mult, extra_reads=[par])
        PI = math.pi
        ki = A.i32("ki", [NTOK])
        kf = A.f32("kf", [NTOK])
        mk = A.f32("mk", [NTOK])
        for tab, shift in ((sinT, 0.0), (cosT, 0.25)):
            ts("dve", tab, angT, 1.0 / (2.0 * PI), shift, ALU.mult, ALU.add)
            cp("dve", ki, tab)
            cp("dve", kf, ki)
            tt("dve", tab, tab, kf, ALU.subtract)
            ts("dve", mk, tab, 0.5, None, ALU.is_gt)
            tt("dve", tab, tab, mk, ALU.subtract)
            ts("dve", mk, tab, -0.5, None, ALU.is_lt)
            tt("dve", tab, tab, mk, ALU.add)
            act(tab, tab, AF.Sin, scale=6.28318)
        P.barrier()

        A.off = B1_MARK

        def tiles_of(start, count):
            return [(start + i, min(512, count - i)) for i in range(0, count, 512)]
        own_tiles = [(t0, n, t0, t0) for (t0, n) in tiles_of(0, NO)]
        halo_tile = (NO, NH, NO, None)
        oth_tiles = [(t0, n, None, t0 - NH) for (t0, n) in tiles_of(NQ, NO)]
        groups = []
        for i in range(0, len(own_tiles), 2):
            g = own_tiles[i:i + 2]
            if i + 2 >= len(own_tiles):
                g = g + [halo_tile]
            groups.append(g)
        for i in range(0, len(oth_tiles), 2):
            groups.append(oth_tiles[i:i + 2])
        GMAX = 1024 + NH

        xt = A.f32("xt", [16, 512])
        hT = A.bf("hT", [16, GMAX])
        wslots = Rot([A.bf("w%d" % i, [16, 512]) for i in range(2)])
        sqb = Rot([A.bf("sqb%d" % i, [512]) for i in range(3)])
        tmpf = Rot([A.f32("tmpf%d" % i, [512]) for i in range(6)])
        ybf = Rot([A.bf("ybf%d" % i, [512]) for i in range(4)])
        yf = Rot([A.f32("yf%d" % i, [512]) for i in range(8)])
        rstds = Rot([A.f32("rstd%d" % i, [512]) for i in range(3)])
        outb = Rot([A.bf("outb%d" % i, [512]) for i in range(4)])
        stg4 = Rot([A.bf("stg4_%d" % i, [4, 512]) for i in range(2)])
        PSM = Rot([PS[0], PS[1], PS[2]])
        PSX = PS[3]
        PSN = Rot([PS[4], PS[5]])
        PSR = Rot([PS[6], PS[7]])

        pn_pending = []

        def pn_flush():
            while pn_pending:
                pn_pending.pop(0)()

        def post_norm(ps_list, n, gains, ones_v, dim, rope_tl, dests):
            psn = PSN.next()
            nj = len(ps_list)
            ys = []
            for j, ps in enumerate(ps_list):
                if rope_tl is None:
                    y = yf.next()
                else:
                    y = ybf.next()
                act(y[:, :n], ps[:, :n], AF.Identity, scale=gains[j], extra_reads=[par])
                sq = sqb.next()
                act(sq[:, :n], ps[:, :n], AF.Square)
                mm(psn[:, :n], ones_v, sq[:, :n], j == 0, j == nj - 1)
                ys.append(y)
            psrs = []
            if rope_tl is not None:
                for y in ys:
                    psr = PSR.next()
                    mm(psr[:, :n], perm_b, y[:, :n], True, True)
                    psrs.append(psr)

            def tail():
                rs = rstds.next()
                rstd_from(psn[:, :n], rs[:, :n], dim)
                for j, y in enumerate(ys):
                    ob = outb.next()
                    if rope_tl is None:
                        tt("dve", ob[:, :n], y[:, :n], rs[:, :n], ALU.mult)
                    else:
                        psr = psrs[j]
                        t1 = tmpf.next()
                        tt("dve", t1[:, :n], y[:, :n], cosT[:, rope_tl:rope_tl + n], ALU.mult)
                        t2 = tmpf.next()
                        tt("dve", t2[:, :n], psr[:, :n], sinT[:, rope_tl:rope_tl + n], ALU.mult)
                        tt("dve", t1[:, :n], t1[:, :n], t2[:, :n], ALU.add)
                        tt("dve", ob[:, :n], t1[:, :n], rs[:, :n], ALU.mult)
                    dma("sp", dests[j], ob[:, :n], ob.key, store=True)
            prev = list(pn_pending)
            del pn_pending[:]
            for f in prev:
                f()
            pn_pending.append(tail)

        def build_h(tile, goff, a_v, shift_fn, src_fn):
            tl0, n = tile[0], tile[1]
            src_fn(xt, tl0, n)
            for k in range(KD):
                sq = sqb.next()
                act(sq[:, :n], xt[:, k, :n], AF.Square)
                mm(PSX[:, :n], ones_b, sq[:, :n], k == 0, k == KD - 1)
            rs = rstds.next()
            rstd_from(PSX[:, :n], rs[:, :n], D)
            for k in range(KD):
                t = tmpf.next()
                stt("dve", t[:, :n], xt[:, k, :n], a_v.ap[:, k:k + 1], rs[:, :n], ALU.mult, ALU.mult, extra_reads=[a_v])
                act(V(hT.key + str(goff), hT.ap[:, k, goff:goff + n]), t[:, :n], AF.Identity, bias=shift_fn(k),
                    extra_reads=[modT])

        def load_x(dst, tl0, n):
            dma("sp", V(dst.key, dst.ap[:, :, :n]),
                V("xT", xT.rearrange("(k p) t -> p k t", p=128)[:, :, tl0:tl0 + n]), dst.key)

        def load_w(src, c0, ncols, kd=KD):
            wt = wslots.next()
            dma("pool", V(wt.key, wt.ap[:, :kd, :ncols]),
                V("w", src.rearrange("(k p) e -> p k e", p=128)[:, :, c0:c0 + ncols]), wt.key)
            return wt

        def proj_fm(wt, wc0, tile_off, n, goff_key):
            ps = PSM.next()
            for k in range(KD):
                mm(ps[:, :n], wt[:, k, wc0:wc0 + 128], V(hT.key + str(goff_key), hT.ap[:, k, tile_off:tile_off + n]),
                   k == 0, k == KD - 1)
            return ps

        gq_lat = [pcol("g_q_lat", j) for j in range(4)]
        gkv_lat = [pcol("g_kv_lat", j) for j in range(2)]

        for grp in groups:
            goffs = []
            go = 0
            for tile in grp:
                build_h(tile, go, a1, shift1, load_x)
                goffs.append(go)
                go += tile[1]
            has_q = grp[0][2] is not None
            if has_q:
                wt = load_w(w_q, 0, 512)
                for tile, go in zip(grp, goffs):
                    tl0, n, q0, _ = tile
                    pss = [proj_fm(wt, j * 128, go, n, go) for j in range(3)]
                    ps4 = PSR.next()
                    for k in range(KD):
                        mm(ps4[:, :n], wt[:, k, 384:512], V(hT.key + str(go), hT.ap[:, k, go:go + n]), k == 0, k == KD - 1)
                    pss.append(ps4)
                    post_norm(pss, n, gq_lat, ones_b, 512, None,
                              [V("QLN", QLN[j, :, q0:q0 + n]) for j in range(4)])
                for cg in range(2):
                    wt = load_w(w_q, 512 + cg * 512, 512)
                    for tile, go in zip(grp, goffs):
                        tl0, n, q0, _ = tile
                        for j in range(4):
                            hd = cg * 4 + j
                            ps = proj_fm(wt, j * 128, go, n, go)
                            post_norm([ps], n, [pcol("gq_diff")], bones_b, 64, tl0, [V("DQ", DQ[hd, :, q0:q0 + n])])
                for cg in range(8):
                    wt = load_w(w_q, 1536 + cg * 512, 512)
                    for tile, go in zip(grp, goffs):
                        tl0, n, q0, _ = tile
                        st4 = stg4.next()
                        for j in range(4):
                            ch = cg * 4 + j
                            ps = proj_fm(wt, j * 128, go, n, go)
                            act(st4[:, j, :n], ps[:, :n], AF.Sigmoid, bias=pcol("b_gate", ch), extra_reads=[par])
                        dma("sp", V("G", G[cg * 4:cg * 4 + 4, :, q0:q0 + n].rearrange("c p q -> p c q")),
                            V(st4.key, st4.ap[:, :, :n]), st4.key, store=True)
            kv_tiles = [(tile, go) for tile, go in zip(grp, goffs) if tile[3] is not None]
            wt = load_w(w_kv, 0, 384)
            for tile, go in kv_tiles:
                tl0, n, _, kv0 = tile
                pss = [proj_fm(wt, j * 128, go, n, go) for j in range(2)]
                post_norm(pss, n, gkv_lat, ones_b, 256, None, [V("KVN", KVN[j, :, kv0:kv0 + n]) for j in range(2)])
                ps = proj_fm(wt, 256, go, n, go)
                post_norm([ps], n, [pcol("gk_pe")], bones_b, 64, tl0, [V("KPE", KPE[:, kv0:kv0 + n])])
            for cg in range(2):
                wt = load_w(w_kv, 384 + cg * 512, 512)
                for tile, go in kv_tiles:
                    tl0, n, _, kv0 = tile
                    for j in range(4):
                        hd = cg * 4 + j
                        ps = proj_fm(wt, j * 128, go, n, go)
                        post_norm([ps], n, [pcol("gk_diff")], bones_b, 64, tl0, [V("DK", DK[hd, :, kv0:kv0 + n])])
            for cg in range(2):
                wt = load_w(w_kv, 1408 + cg * 512, 512)
                for tile, go in kv_tiles:
                    tl0, n, _, kv0 = tile
                    for b in range(n // 128):
                        ps = PSM.next()
                        for k in range(KD):
                            mm(ps, V(hT.key + str(go), hT.ap[:, k, go + b * 128:go + (b + 1) * 128]), wt[:, k, :],
                               k == 0, k == KD - 1)
                        ob = outb.next()
                        act(ob, ps, AF.Identity)
                        dma("sp", V("DV", DV[kv0 + b * 128:kv0 + (b + 1) * 128, cg * 512:(cg + 1) * 512]), ob, ob.key, store=True)
        pn_flush()
        P.barrier()

        A.off = B1_MARK
        qln = A.bf("qln", [4, NQ])
        kvn = A.bf("kvn", [2, S])
        wqu = A.bf("wqu", [4, 1536])
        wkvu = A.bf("wkvu", [2, 2048])
        sqb = Rot([A.bf("sqb%d" % i, [512]) for i in range(3)])
        tmpf = Rot([A.f32("tmpf%d" % i, [512]) for i in range(6)])
        ybf = Rot([A.bf("ybf%d" % i, [512]) for i in range(4)])
        yf = Rot([A.f32("yf%d" % i, [512]) for i in range(8)])
        rstds = Rot([A.f32("rstd%d" % i, [512]) for i in range(3)])
        outb = Rot([A.bf("outb%d" % i, [512]) for i in range(4)])
        dma("sp", qln, V("QLN", QLN.rearrange("c p q -> p c q")), "qln")
        dma("sp", kvn, V("KVN", KVN.rearrange("c p q -> p c q")), "kvn")
        dma("pool", wqu, V("w_q_up", w_q_up.rearrange("(k p) e -> p k e", p=128)), "wqu")
        for hh in range(2):
            dma("pool", V("wkvu", wkvu.ap[:, :, hh * 1024:(hh + 1) * 1024]),
                V("w_kv_up", w_kv_up.rearrange("(k p) e -> p k e", p=128)[:, :, hh * 1024:(hh + 1) * 1024]), "wkvu")
        q_tiles = [(t0, n) for (t0, n) in tiles_of(0, NO)] + [(NO, NH)]
        for (q0, n) in q_tiles:
            for hd in range(8):
                ps = PSM.next()
                for k in range(4):
                    mm(ps[:, :n], wqu[:, k, hd * 128:(hd + 1) * 128], qln[:, k, q0:q0 + n], k == 0, k == 3)
                post_norm([ps], n, [pcol("gq_nope")], ones_b, 128, None, [V("QN", QN[hd, :, q0:q0 + n])])
            for j in range(4):
                ps = PSM.next()
                for k in range(4):
                    mm(ps[:, :n], wqu[:, k, 1024 + j * 128:1024 + (j + 1) * 128], qln[:, k, q0:q0 + n], k == 0, k == 3)
                post_norm([ps], n, [pcol("gq_pe")], bones_b, 64, q0, [V("QPE", QPE[j, :, q0:q0 + n])])
        for (kv0, n) in tiles_of(0, S):
            for hd in range(8):
                ps = PSM.next()
                for k in range(2):
                    mm(ps[:, :n], wkvu[:, k, hd * 128:(hd + 1) * 128], kvn[:, k, kv0:kv0 + n], k == 0, k == 1)
                post_norm([ps], n, [pcol("gk_nope")], ones_b, 128, None, [V("KN", KN[hd, :, kv0:kv0 + n])])
            for b in range(n // 128):
                for cg in range(2):
                    ps = PSM.next()
                    for k in range(2):
                        mm(ps, kvn[:, k, kv0 + b * 128:kv0 + (b + 1) * 128], wkvu[:, k, 1024 + cg * 512:1024 + (cg + 1) * 512],
                           k == 0, k == 1)
                    ob = outb.next()
                    act(ob, ps, AF.Identity)
                    dma("sp", V("VM", VM[kv0 + b * 128:kv0 + (b + 1) * 128, cg * 512:(cg + 1) * 512]), ob, ob.key, store=True)
        pn_flush()
        P.barrier()

        A.reset()
        kpe = A.bf("kpe", [S])
        dma("sp", kpe, V("KPE", KPE[:, :]), "kpe")
        Kh = Rot([A.bf("Kh%d" % i, [S]) for i in range(2)])
        Vh = Rot([A.bf("Vh%d" % i, [NKB, 128]) for i in range(2)])
        Qh = Rot([A.bf("Qh%d" % i, [NQ]) for i in range(2)])
        Qp = Rot([A.bf("Qp%d" % i, [NQ]) for i in range(2)])
        Oh = Rot([A.bf("Oh%d" % i, [NQ]) for i in range(2)])
        Pb = Rot([A.bf("Pb%d" % i, [512]) for i in range(6)])
        recs = Rot([A.f32("rec%d" % i, [128]) for i in range(4)])
        odf = Rot([A.f32("odf%d" % i, [128]) for i in range(4)])
        sqc = Rot([A.bf("sqc%d" % i, [128]) for i in range(2)])

        qblocks = []
        for i in range(NOB):
            kl = [(l, "full") for l in range(i)] + [(NOB + l, "full") for l in range(i)] + [(NOB + i, "keep"), (i, "causal")]
            qblocks.append((i * 128, 128, kl))
        qblocks.append((NO, NH, [(l, "halo") for l in range(NKB)]))

        def exp_group(Sps, qn, grp_list, scale, pb):
            ng = len(grp_list)
            if qn == 128:
                act(pb[:, :ng * 128], Sps[:, :ng * 128], AF.Exp, scale=scale)
            else:
                act(V(pb.key, pb.ap.rearrange("p (j q) -> p j q", q=128)[:, :ng, :qn]),
                    V(Sps.key, Sps.ap.rearrange("p (j q) -> p j q", q=128)[:, :ng, :qn]), AF.Exp, scale=scale)
            for jj, (l, mode) in enumerate(grp_list):
                reg = pb[:, jj * 128:jj * 128 + qn]
                if mode == "keep":
                    ts("dve", reg, reg, pcol("keep"), None, ALU.mult, extra_reads=[par])
                elif mode == "causal":
                    tt("dve", reg, reg, causal_b, ALU.mult)
                elif mode == "halo":
                    tt("dve", reg, reg, halo_b[:, l * NH:(l + 1) * NH], ALU.mult)

        def load_head(Ksrc, Vsrc, Qsrc, hd):
            kh = Kh.next()
            dma("sp", kh, V("K", Ksrc[hd, :, :]), kh.key)
            vh = Vh.next()
            dma("sp", vh, V("Vs", Vsrc.rearrange("(j p) d -> p j d", p=128)[:, :, hd * 128:(hd + 1) * 128]), vh.key)
            qh = Qh.next()
            dma("sp", qh, V("Q", Qsrc[hd, :, :]), qh.key)
            return kh, vh, qh

        SC_MLA = 192.0 ** -0.5
        SC_DIFF = 64.0 ** -0.5
        SB = Rot([PS[0], PS[1]])
        for hd in range(8):
            kh, vh, qh = load_head(KN, VM, QN, hd)
            if hd % 2 == 0:
                qp = Qp.next()
                dma("sp", qp, V("QPE", QPE[hd // 2, :, :]), qp.key)
            hp = (hd % 2) * 64
            oh = Oh.next()
            for bi, (q0, qn, kl) in enumerate(qblocks):
                r0 = (bi % 4) * 128
                oacc = V("ps2_r%d" % (bi % 4), PS[2].ap[:, r0:r0 + qn])
                dacc = V("ps3_r%d" % (bi % 4), PS[3].ap[:, r0:r0 + qn])
                nk = len(kl)
                for g0 in range(0, nk, 4):
                    gl = kl[g0:g0 + 4]
                    Sps = SB.next()
                    for jj, (l, mode) in enumerate(gl):
                        reg = Sps[:, jj * 128:jj * 128 + qn]
                        mm(reg, kh[:, l * 128:(l + 1) * 128], qh[:, q0:q0 + qn], True, False)
                        mm(reg, kpe[hp:hp + 64, l * 128:(l + 1) * 128], qp[hp:hp + 64, q0:q0 + qn], False, True)
                    pb = Pb.next()
                    exp_group(Sps, qn, gl, SC_MLA, pb)
                    for jj, (l, mode) in enumerate(gl):
                        first = (g0 + jj == 0)
                        last = (g0 + jj == nk - 1)
                        mm(oacc, vh[:, l, :], pb[:, jj * 128:jj * 128 + qn], first, last)
                        mm(dacc, ones_b, pb[:, jj * 128:jj * 128 + qn], first, last)
                rc = recs.next()
                P.op("dve", lambda h, o=rc.ap[:, :qn], i=dacc.ap: h.reciprocal(out=o, in_=i), reads=[dacc], writes=[rc])
                tt("dve", oh[:, q0:q0 + qn], oacc, rc[:, :qn], ALU.mult)
            dma("sp", V("OM", OM[hd, :, :]), oh, oh.key, store=True)
        S1B = Rot([PS[0], PS[1]])
        S2B = Rot([PS[4], PS[5]])
        for hd in range(8):
            kh, vh, qh = load_head(DK, DV, DQ, hd)
            oh = Oh.next()
            for bi, (q0, qn, kl) in enumerate(qblocks):
                r0 = (bi % 4) * 128
                accs = [V("ps%d_r%d" % (b_, bi % 4), PS[b_].ap[:, r0:r0 + qn]) for b_ in (2, 3, 6, 7)]
                o1, d1, o2, d2 = accs
                nk = len(kl)
                for g0 in range(0, nk, 4):
                    gl = kl[g0:g0 + 4]
                    S1 = S1B.next()
                    S2 = S2B.next()
                    for jj, (l, mode) in enumerate(gl):
                        mm(S1[:, jj * 128:jj * 128 + qn], kh[0:64, l * 128:(l + 1) * 128], qh[0:64, q0:q0 + qn], True, True)
                        mm(S2[:, jj * 128:jj * 128 + qn], kh[64:128, l * 128:(l + 1) * 128], qh[64:128, q0:q0 + qn], True, True)
                    p1 = Pb.next()
                    p2 = Pb.next()
                    exp_group(S1, qn, gl, SC_DIFF, p1)
                    exp_group(S2, qn, gl, SC_DIFF, p2)
                    for jj, (l, mode) in enumerate(gl):
                        first = (g0 + jj == 0)
                        last = (g0 + jj == nk - 1)
                        mm(o1, vh[:, l, :], p1[:, jj * 128:jj * 128 + qn], first, last)
                        mm(d1, ones_b, p1[:, jj * 128:jj * 128 + qn], first, last)
                        mm(o2, vh[:, l, :], p2[:, jj * 128:jj * 128 + qn], first, last)
                        mm(d2, ones_b, p2[:, jj * 128:jj * 128 + qn], first, last)
                r1 = recs.next()
                r2 = recs.next()
                P.op("dve", lambda h, o=r1.ap[:, :qn], i=d1.ap: h.reciprocal(out=o, in_=i), reads=[d1], writes=[r1])
                P.op("dve", lambda h, o=r2.ap[:, :qn], i=d2.ap: h.reciprocal(out=o, in_=i), reads=[d2], writes=[r2])
                ts("dve", r2[:, :qn], r2[:, :qn], neglam, None, ALU.mult, extra_reads=[lamv])
                t1 = odf.next()
                tt("dve", t1[:, :qn], o1, r1[:, :qn], ALU.mult)
                t2 = odf.next()
                tt("dve", t2[:, :qn], o2, r2[:, :qn], ALU.mult)
                tt("dve", t1[:, :qn], t1[:, :qn], t2[:, :qn], ALU.add)
                sq = sqc.next()
                act(sq[:, :qn], t1[:, :qn], AF.Square)
                psn = S1B.next()
                mm(psn[:, :qn], ones_b, sq[:, :qn], True, True)
                rs = recs.next()
                rstd_from(psn[:, :qn], rs[:, :qn], 128)
                stt("dve", oh[:, q0:q0 + qn], t1[:, :qn], gsub8.ap[:, 0:1], rs[:, :qn], ALU.mult, ALU.mult,
                    extra_reads=[gsub8])
            dma("sp", V("OD", OD[hd, :, :]), oh, oh.key, store=True)
        pn_flush()
        P.barrier()

        A.reset()
        omt = A.bf("omt", [8, 512])
        odt = A.bf("odt", [8, 512])
        gts = Rot([A.bf("gt%d" % i, [8, 512]) for i in range(2)])
        mixed = A.bf("mixed", [16, 512])
        x1t = A.f32("x1t", [16, 512])
        xcs = Rot([A.f32("xcD%d" % i, [512]) for i in range(3)])
        h2t = A.bf("h2t", [16, 512])
        wos = Rot([A.bf("wo%d" % i, [8, 512]) for i in range(4)])
        wslots = Rot([A.bf("wD%d" % i, [16, 512]) for i in range(2)])
        tmpf = Rot([A.f32("tmpfD%d" % i, [512]) for i in range(6)])
        sqb = Rot([A.bf("sqbD%d" % i, [512]) for i in range(3)])
        rstds = Rot([A.f32("rstdD%d" % i, [512]) for i in range(2)])
        PSMD = Rot([PS[0], PS[1], PS[2], PS[3]])
        PSO = Rot([PS[4], PS[5]])
        PSX = PS[6]
        for (q0, n) in q_tiles:
            dma("sp", V(omt.key, omt.ap[:, :, :n]), V("OM", OM[:, :, q0:q0 + n].rearrange("h p q -> p h q")), omt.key)
            dma("sp", V(odt.key, odt.ap[:, :, :n]), V("OD", OD[:, :, q0:q0 + n].rearrange("h p q -> p h q")), odt.key)
            for eg in range(4):
                wm = wos.next()
                dma("pool", wm, V("w", w_o_mla.rearrange("(k p) e -> p k e", p=128)[:, :, eg * 512:(eg + 1) * 512]), wm.key)
                wd_ = wos.next()
                dma("pool", wd_, V("w", w_o_diff.rearrange("(k p) e -> p k e", p=128)[:, :, eg * 512:(eg + 1) * 512]), wd_.key)
                gt = gts.next()
                for ab in range(2):
                    dma("sp", V(gt.key, gt.ap[:, ab * 4:ab * 4 + 4, :n]),
                        V("G", G[ab * 16 + eg * 4:ab * 16 + eg * 4 + 4, :, q0:q0 + n].rearrange("h p q -> p h q")), gt.key)
                for j in range(4):
                    e = eg * 4 + j
                    psm = PSMD.next()
                    for k in range(8):
                        mm(psm[:, :n], wm[:, k, j * 128:(j + 1) * 128], omt[:, k, :n], k == 0, k == 7)
                    psd = PSMD.next()
                    for k in range(8):
                        mm(psd[:, :n], wd_[:, k, j * 128:(j + 1) * 128], odt[:, k, :n], k == 0, k == 7)
                    t1 = tmpf.next()
                    tt("dve", t1[:, :n], psm[:, :n], gt[:, j, :n], ALU.mult)
                    t2 = tmpf.next()
                    tt("dve", t2[:, :n], psd[:, :n], gt[:, 4 + j, :n], ALU.mult)
                    tt("dve", mixed[:, e, :n], t1[:, :n], t2[:, :n], ALU.add)
            for eg in range(4):
                wt = wslots.next()
                dma("pool", wt, V("w", w_out.rearrange("(k p) e -> p k e", p=128)[:, :, eg * 512:(eg + 1) * 512]), wt.key)
                for j in range(4):
                    e = eg * 4 + j
                    ps = PSO.next()
                    for k in range(KD):
                        mm(ps[:, :n], wt[:, k, j * 128:(j + 1) * 128], mixed[:, k, :n], k == 0, k == KD - 1)
                    xc = xcs.next()
                    dma("sp", V(xc.key, xc.ap[:, :n]), V("xT", xT[e * 128:(e + 1) * 128, q0:q0 + n]), xc.key)
                    stt("dve", x1t[:, e, :n], ps[:, :n], gate1(e), xc[:, :n], ALU.mult, ALU.add, extra_reads=[modT])
            if q0 < NO:
                dma("sp", V("X1", X1[:, :, q0:q0 + n].rearrange("c p q -> p c q")), V(x1t.key, x1t.ap[:, :, :n]), x1t.key, store=True)
            for k in range(KD):
                sq = sqb.next()
                act(sq[:, :n], x1t[:, k, :n], AF.Square)
                mm(PSX[:, :n], ones_b, sq[:, :n], k == 0, k == KD - 1)
            rs = rstds.next()
            rstd_from(PSX[:, :n], rs[:, :n], D)
            for k in range(KD):
                t = tmpf.next()
                stt("dve", t[:, :n], x1t[:, k, :n], a2.ap[:, k:k + 1], rs[:, :n], ALU.mult, ALU.mult, extra_reads=[a2])
                if q0 >= NO:
                    act(t[:, :n], t[:, :n], AF.Identity, bias=shift2(k), extra_reads=[modT])
                    tt("dve", h2t[:, k, :n], t[:, :n], V("par", pcol("hvalid", 0, NH)), ALU.mult)
                else:
                    act(h2t[:, k, :n], t[:, :n], AF.Identity, bias=shift2(k), extra_reads=[modT])
            dma("sp", V("H2", H2[:, :, q0:q0 + n].rearrange("c p q -> p c q")), V(h2t.key, h2t.ap[:, :, :n]), h2t.key, store=True)
        pn_flush()
        P.barrier()

        A.reset()
        h2 = A.bf("h2", [16, NQ])
        dma("sp", h2, V("H2", H2.rearrange("c p q -> p c q")), "h2")
        wv_s = Rot([A.bf("wv%d" % i, [16, 256]) for i in range(2)])
        wg_s = Rot([A.bf("wg%d" % i, [16, 256]) for i in range(2)])
        uext = Rot([A.f32("uext%d" % i, [NOB, 130]) for i in range(4)])
        ycv = Rot([A.f32("ycv%d" % i, [NOB, 128]) for i in range(4)])
        zb = Rot([A.bf("zb%d" % i, [NOB, 128]) for i in range(2)])
        PSE = Rot(PS)
        own_q_tiles = tiles_of(0, NO)

        def conv_chunk(wt, wc0, ch):
            ue = uext.next()
            for (q0, n) in own_q_tiles:
                ps = PSE.next()
                for k in range(KD):
                    mm(ps[:, :n], wt[:, k, wc0:wc0 + 128], h2[:, k, q0:q0 + n], k == 0, k == KD - 1)
                nb = n // 128
                b0 = q0 // 128
                act(V(ue.key, ue.ap[:, b0:b0 + nb, 2:130]), V(ps.key, ps.ap[:, :n].rearrange("p (b t) -> p b t", t=128)),
                    AF.Identity)
            ps = PSE.next()
            for k in range(KD):
                mm(ps[:, :NH], wt[:, k, wc0:wc0 + 128], h2[:, k, NO:NO + NH], k == 0, k == KD - 1)
            cp("dve", V(ue.key, ue.ap[:, :, 0:2]), V(ps.key, ps.ap[:, :NH].rearrange("p (b t) -> p b t", t=2)))
            cw = lambda j: pcol("conv_w", j * 88 + ch)
            y = ycv.next()
            act(y, V(ue.key, ue.ap[:, :, 2:130]), AF.Identity, bias=pcol("conv_b", ch), scale=cw(2), extra_reads=[par])
            stt("dve", y, V(ue.key, ue.ap[:, :, 1:129]), cw(1), y, ALU.mult, ALU.add, extra_reads=[par])
            stt("dve", y, V(ue.key, ue.ap[:, :, 0:128]), cw(0), y, ALU.mult, ALU.add, extra_reads=[par])
            return y

        for cp0 in range(0, 44, 2):
            wv = wv_s.next()
            dma("pool", wv, V("w", w_up.rearrange("(k p) e -> p k e", p=128)[:, :, cp0 * 128:cp0 * 128 + 256]), wv.key)
            wg = wg_s.next()
            dma("pool", wg, V("w", w_up.rearrange("(k p) e -> p k e", p=128)[:, :, DFF + cp0 * 128:DFF + cp0 * 128 + 256]), wg.key)
            for cc in range(2):
                c = cp0 + cc
                yv = conv_chunk(wv, cc * 128, c)
                yg = conv_chunk(wg, cc * 128, 44 + c)
                act(yg, yg, AF.Silu)
                z = zb.next()
                tt("dve", z, yg, yv, ALU.mult)
                dma("sp", V("Z", Z[c, :, :]), V(z.key, z.ap.rearrange("p b t -> p (b t)")), z.key, store=True)
        pn_flush()
        P.barrier()

        A.reset()
        wd_s = Rot([A.bf("wdn%d" % i, [44, 512]) for i in range(2)])
        zt_s = Rot([A.bf("zt%d" % i, [44, 512]) for i in range(2)])
        x1c = Rot([A.f32("x1c%d" % i, [512]) for i in range(2)])
        oc = Rot([A.f32("oc%d" % i, [512]) for i in range(2)])
        PSE = Rot(PS)
        for eg in range(4):
            wd = wd_s.next()
            dma("pool", wd, V("w", w_down.rearrange("(k p) e -> p k e", p=128)[:, :, eg * 512:(eg + 1) * 512]), wd.key)
            for (q0, n) in own_q_tiles:
                zt = zt_s.next()
                dma("sp", V(zt.key, zt.ap[:, :, :n]), V("Z", Z[:, :, q0:q0 + n].rearrange("c p q -> p c q")), zt.key)
                for j in range(4):
                    e = eg * 4 + j
                    ps = PSE.next()
                    for c in range(44):
                        mm(ps[:, :n], wd[:, c, j * 128:(j + 1) * 128], zt[:, c, :n], c == 0, c == 43)
                    xc = x1c.next()
                    dma("sp", V(xc.key, xc.ap[:, :n]), V("X1", X1[e, :, q0:q0 + n]), xc.key)
                    o = oc.next()
                    stt("dve", o[:, :n], ps[:, :n], gate2(e), xc[:, :n], ALU.mult, ALU.add, extra_reads=[modT])
                    od_ = dma("sp", V("outT", outT[e * 128:(e + 1) * 128, q0:q0 + n]), V(o.key, o.ap[:, :n]), o.key, store=True)
                    P.out_dmas.append(od_)
        fin = P.op("sp", None)
        fin.deps.update(o.idx for o in P.out_dmas)
        P.emit(nc, stack)
    return nc


def host_prep(inp, S):
    NO = S // 2
    NOB = NO // 128
    NH = 2 * NOB
    NKB = 2 * NOB
    POFF, NPAR = par_layout(NH)
    f32 = np.float32
    x = np.asarray(inp["x"], f32)
    pos = np.asarray(inp["positions"], np.int32)
    B = x.shape[0]

    def fm(v):
        v = np.asarray(v, f32).reshape(-1, 128)
        return np.ascontiguousarray(v.T)

    def rep(v):
        v = np.asarray(v, f32).reshape(1, -1)
        return np.repeat(v, 128, axis=0)

    w_in = np.asarray(inp["w_in"][0], f32)
    q_lat, kv_lat, k_pe, dq, dk, dv, gl = np.split(w_in, np.cumsum([512, 256, 64, 1024, 1024, 1024])[:], axis=1)
    w_q = np.ascontiguousarray(np.concatenate([q_lat, dq, gl], axis=1))
    w_kv = np.ascontiguousarray(np.concatenate([kv_lat, k_pe, k_pe, dk, dv], axis=1))
    wqu = np.asarray(inp["w_q_up"][0], f32).reshape(512, 8, 192)
    w_q_up = np.ascontiguousarray(np.concatenate([wqu[:, :, :128].reshape(512, 1024), wqu[:, :, 128:].reshape(512, 512)], axis=1))
    wkvu = np.asarray(inp["w_kv_up"][0], f32).reshape(256, 8, 256)
    w_kv_up = np.ascontiguousarray(np.concatenate([wkvu[:, :, :128].reshape(256, 1024), wkvu[:, :, 128:].reshape(256, 1024)], axis=1))

    ones = np.ones((128, 128), f32)
    bones = np.zeros((128, 128), f32)
    bones[:64, :64] = 1
    bones[64:, 64:] = 1
    perm = np.zeros((128, 128), f32)
    for m in range(128):
        if (m % 64) < 32:
            perm[m + 32, m] = -1.0
        else:
            perm[m - 32, m] = 1.0
    causal = (np.arange(128)[None, :] >= np.arange(128)[:, None]).astype(f32)
    invfreq = (10000.0 ** (-(np.arange(0, 64, 2, dtype=f32)) / f32(64))).astype(f32)
    invf128 = np.tile(invfreq, 4).reshape(128, 1)

    gq = np.asarray(inp["g_q_mla"][0], f32)
    gk = np.asarray(inp["g_k_mla"][0], f32)
    shared = {
        "b_ada": fm(inp["b_ada"][0]), "g1": fm(inp["g_norm1"][0]), "g2": fm(inp["g_norm2"][0]),
        "b_gate": fm(inp["b_gate"][0]),
        "conv_w": np.concatenate([fm(inp["conv_w"][0][j]) for j in range(3)], axis=1),
        "conv_b": fm(inp["conv_b"][0]),
        "g_q_lat": fm(inp["g_q_lat"][0]), "g_kv_lat": fm(inp["g_kv_lat"][0]),
        "gq_nope": gq[:128].reshape(128, 1), "gq_pe": np.tile(gq[128:], 2).reshape(128, 1),
        "gk_nope": gk[:128].reshape(128, 1), "gk_pe": np.tile(gk[128:], 2).reshape(128, 1),
        "gq_diff": np.tile(np.asarray(inp["g_q_diff"][0], f32), 2).reshape(128, 1),
        "gk_diff": np.tile(np.asarray(inp["g_k_diff"][0], f32), 2).reshape(128, 1),
        "g_sub": np.asarray(inp["g_sub_diff"][0], f32).reshape(128, 1),
        "invfreq": invf128,
        "lq1": rep(inp["lam_q1"][0]), "lk1": rep(inp["lam_k1"][0]),
        "lq2": rep(inp["lam_q2"][0]), "lk2": rep(inp["lam_k2"][0]),
    }
    big = {
        "w_ada": np.ascontiguousarray(np.asarray(inp["w_ada"][0], f32)),
        "w_q": w_q, "w_kv": w_kv, "w_q_up": w_q_up, "w_kv_up": w_kv_up,
        "w_o_mla": np.ascontiguousarray(np.asarray(inp["w_o_mla"][0], f32)),
        "w_o_diff": np.ascontiguousarray(np.asarray(inp["w_o_diff"][0], f32)),
        "w_out": np.ascontiguousarray(np.asarray(inp["w_out"][0], f32)),
        "w_up": np.ascontiguousarray(np.asarray(inp["w_up"][0], f32)),
        "w_down": np.ascontiguousarray(np.asarray(inp["w_down"][0], f32)),
    }
    in_maps = []
    own_idx = []
    for core in range(NCORES):
        b, c = core // 2, core % 2
        own_blocks = [2 * i + c for i in range(NOB)]
        oth_blocks = [2 * i + (1 - c) for i in range(NOB)]
        own_tok = np.concatenate([np.arange(j * 128, (j + 1) * 128) for j in own_blocks])
        oth_tok = np.concatenate([np.arange(j * 128, (j + 1) * 128) for j in oth_blocks])
        halo_tok = []
        hvalid = []
        for j in own_blocks:
            for d_ in (2, 1):
                t = j * 128 - d_
                if t >= 0:
                    halo_tok.append(t)
                    hvalid.append(1.0)
                else:
                    halo_tok.append(2 - d_)
                    hvalid.append(0.0)
        halo_tok = np.array(halo_tok)
        tl = np.concatenate([own_tok, halo_tok, oth_tok])
        kv_tok = np.concatenate([own_tok, oth_tok])
        xT = np.ascontiguousarray(x[b][tl].T)
        posr = np.ascontiguousarray(np.repeat(pos[b][tl].reshape(1, -1), 128, axis=0)).astype(np.int32)
        hm = (kv_tok.reshape(NKB, 128).T[:, :, None] <= halo_tok[None, None, :]).astype(f32).reshape(128, NKB * NH)
        cst = np.ascontiguousarray(np.concatenate([ones, bones, perm, causal, hm], axis=1))
        par = np.zeros((128, NPAR), f32)
        for name, arr in shared.items():
            o, n = POFF[name]
            par[:, o:o + n] = arr
        o, n = POFF["c"]
        par[:, o:o + n] = fm(np.asarray(inp["c"], f32)[b])
        o, n = POFF["keep"]
        par[:, o] = 1.0 if c == 1 else 0.0
        o, n = POFF["hvalid"]
        par[:, o:o + n] = np.asarray(hvalid, f32)[None, :]
        m = {"xT": xT, "params": par, "posrep": posr, "consts": cst}
        m.update(big)
        in_maps.append(m)
        own_idx.append((b, own_tok))
    return in_maps, own_idx


_NC_CACHE = {}


def kernel(**inputs):
    S = int(np.asarray(inputs["x"]).shape[1])
    if S not in _NC_CACHE:
        _NC_CACHE[S] = build(S)
    nc = _NC_CACHE[S]
    in_maps, own_idx = host_prep(inputs, S)
    res = run_bass_kernel_spmd(nc, in_maps, core_ids=list(range(NCORES)))
    B = np.asarray(inputs["x"]).shape[0]
    out = np.empty((B, S, D), np.float32)
    for core in range(NCORES):
        b, own_tok = own_idx[core]
        out[b, own_tok, :] = np.asarray(res.results[core]["outT"]).T
    return out
```

```python
import math
import numpy as np
import concourse.bass as bass
import concourse.mybir as mybir
from concourse.bass_utils import run_bass_kernel_spmd

F32 = mybir.dt.float32
BF16 = mybir.dt.bfloat16
I32 = mybir.dt.int32
AF = mybir.ActivationFunctionType
ALU = mybir.AluOpType
AX = mybir.AxisListType

D = 2048
KD = 16
DFF = 5632
NCORES = 8
EPS = 1e-6
LAMBDA_INIT = 0.8 - 0.6 * math.exp(-0.3 * 0)
SAME_ENGINE_WAITS = True
ATTN_PIPE = False


class V:
    __slots__ = ("key", "ap")

    def __init__(self, key, ap):
        self.key = key
        self.ap = ap

    def __getitem__(self, idx):
        return V(self.key, self.ap[idx])

    def k(self, key):
        return V(key, self.ap)

    def re(self, s, **kw):
        return V(self.key, self.ap.rearrange(s, **kw))


class Op:
    __slots__ = ("eng", "fn", "deps", "dma", "semkey", "idx", "ticket", "need_inc", "semval")

    def __init__(self, eng, fn, dma, semkey):
        self.eng = eng
        self.fn = fn
        self.deps = set()
        self.dma = dma
        self.semkey = semkey
        self.ticket = None
        self.need_inc = False
        self.semval = None


class Prog:
    ENGS = ("pe", "act", "dve", "pool", "sp")

    def __init__(self):
        self.ops = []
        self.state = {}
        self.eng_ops = {e: [] for e in self.ENGS}
        self.bar_deps = set()
        self.bar_seen = {e: True for e in self.ENGS}
        self.dma_since_bar = []
        self.out_dmas = []

    def op(self, eng, fn, reads=(), writes=(), dma=False, semkey=None):
        o = Op(eng, fn, dma, semkey)
        o.idx = len(self.ops)
        for v in reads:
            st = self.state.get(v.key)
            if st is not None:
                o.deps.update(st["w"])
        for v in writes:
            st = self.state.setdefault(v.key, {"w": [], "r": [], "pend": []})
            if st["r"]:
                st["pend"] = st["r"] + st["w"]
                st["w"] = []
                st["r"] = []
            o.deps.update(st["pend"])
        for v in reads:
            st = self.state.setdefault(v.key, {"w": [], "r": [], "pend": []})
            st["r"].append(o.idx)
        for v in writes:
            self.state[v.key]["w"].append(o.idx)
        if not self.bar_seen[eng]:
            o.deps.update(self.bar_deps)
            self.bar_seen[eng] = True
        o.deps.discard(o.idx)
        self.ops.append(o)
        self.eng_ops[eng].append(o)
        if dma:
            self.dma_since_bar.append(o.idx)
        return o

    def barrier(self):
        deps = set(self.dma_since_bar)
        for e in self.ENGS:
            if self.eng_ops[e]:
                deps.add(self.eng_ops[e][-1].idx)
        self.bar_deps = deps
        self.bar_seen = {e: False for e in self.ENGS}
        self.dma_since_bar = []
        self.state = {}

    def emit(self, nc, stack):
        ops = self.ops
        waits = {}
        for e in self.ENGS:
            waited = {}
            for o in self.eng_ops[e]:
                best = {}
                for d in o.deps:
                    p = ops[d]
                    if p.dma:
                        key = ("dma", p.semkey)
                    else:
                        if p.eng == e and (e == "pe" or not SAME_ENGINE_WAITS):
                            continue
                        key = ("eng", p.eng)
                    if key not in best or best[key] < d:
                        best[key] = d
                lst = []
                for key, d in best.items():
                    if waited.get(key, -1) >= d:
                        continue
                    waited[key] = d
                    lst.append(d)
                    if not ops[d].dma:
                        ops[d].need_inc = True
                waits[o.idx] = lst
        cnt = {e: 0 for e in self.ENGS}
        dcnt = {}
        for o in ops:
            if o.dma:
                dcnt[o.semkey] = dcnt.get(o.semkey, 0) + 16
                o.semval = dcnt[o.semkey]
            elif o.need_inc:
                cnt[o.eng] += 1
                o.ticket = cnt[o.eng]
        esem = {e: stack.enter_context(nc.semaphore("s_" + e)) for e in self.ENGS}
        dsem = {}
        for i, k in enumerate(sorted(dcnt.keys())):
            dsem[k] = stack.enter_context(nc.semaphore("d%d" % i))
        self.nsem = len(esem) + len(dsem)
        block = stack.enter_context(nc.Block())

        def run(e, h):
            for o in self.eng_ops[e]:
                for d in waits[o.idx]:
                    p = ops[d]
                    if p.dma:
                        h.wait_ge(dsem[p.semkey], p.semval)
                    else:
                        h.wait_ge(esem[p.eng], p.ticket)
                if o.fn is None:
                    continue
                ins = o.fn(h)
                if o.dma:
                    ins.then_inc(dsem[o.semkey], 16)
                elif o.need_inc:
                    ins.then_inc(esem[o.eng], 1)

        @block.tensor
        def _(h):
            run("pe", h)

        @block.scalar
        def _(h):
            run("act", h)

        @block.vector
        def _(h):
            run("dve", h)

        @block.gpsimd
        def _(h):
            run("pool", h)

        @block.sync
        def _(h):
            run("sp", h)


class Rot:
    def __init__(self, items):
        self.items = items
        self.i = 0

    def next(self):
        x = self.items[self.i % len(self.items)]
        self.i += 1
        return x


def par_layout(NH):
    off = {}
    c = 0
    for name, n in [("b_ada", 96), ("g1", 16), ("g2", 16), ("b_gate", 32), ("conv_w", 3 * 88), ("conv_b", 88),
                    ("g_q_lat", 4), ("g_kv_lat", 2), ("gq_nope", 1), ("gq_pe", 1), ("gk_nope", 1), ("gk_pe", 1),
                    ("gq_diff", 1), ("gk_diff", 1), ("g_sub", 1), ("invfreq", 1), ("lq1", 64), ("lk1", 64),
                    ("lq2", 64), ("lk2", 64), ("c", 16), ("keep", 1), ("hvalid", NH)]:
        off[name] = (c, n)
        c += n
    return off, c


def build(S, debug=False):
    from contextlib import ExitStack
    NO = S // 2
    NOB = NO // 128
    NH = 2 * NOB
    NQ = NO + NH
    NTOK = 2 * NO + NH
    NKB = 2 * NOB
    POFF, NPAR = par_layout(NH)

    nc = bass.Bass("TRN2", target_bir_lowering=False)
    P = Prog()

    def din(name, shape, dt=F32):
        return nc.dram_tensor(name, list(shape), dt, kind="ExternalInput").ap()

    def dscr(name, shape, dt):
        return nc.dram_tensor(name, list(shape), dt, kind="Internal").ap()

    xT = din("xT", [D, NTOK])
    params = din("params", [128, NPAR])
    posrep = din("posrep", [128, NTOK], I32)
    consts = din("consts", [128, 4 * 128 + NKB * NH])
    w_ada = din("w_ada", [D, 6 * D])
    w_q = din("w_q", [D, 5632])
    w_kv = din("w_kv", [D, 2432])
    w_q_up = din("w_q_up", [512, 1536])
    w_kv_up = din("w_kv_up", [256, 2048])
    w_o_mla = din("w_o_mla", [1024, D])
    w_o_diff = din("w_o_diff", [1024, D])
    w_out = din("w_out", [D, D])
    w_up = din("w_up", [D, 2 * DFF])
    w_down = din("w_down", [DFF, D])
    outT = nc.dram_tensor("outT", [D, NO], F32, kind="ExternalOutput").ap()

    QLN = dscr("QLN", [4, 128, NQ], BF16)
    KVN = dscr("KVN", [2, 128, S], BF16)
    KPE = dscr("KPE", [128, S], BF16)
    QN = dscr("QN", [8, 128, NQ], BF16)
    QPE = dscr("QPE", [4, 128, NQ], BF16)
    KN = dscr("KN", [8, 128, S], BF16)
    VM = dscr("VM", [S, 1024], BF16)
    DQ = dscr("DQ", [8, 128, NQ], BF16)
    DK = dscr("DK", [8, 128, S], BF16)
    DV = dscr("DV", [S, 1024], BF16)
    G = dscr("G", [32, 128, NQ], BF16)
    OM = dscr("OM", [8, 128, NQ], BF16)
    OD = dscr("OD", [8, 128, NQ], BF16)
    X1 = dscr("X1", [16, 128, NO], F32)
    H2 = dscr("H2", [16, 128, NQ], BF16)
    Z = dscr("Z", [44, 128, NO], BF16)

    stack = ExitStack()
    with stack:
        PERS_F = 1024 + NPAR + 64
        ARENA_W = 48000
        pers = stack.enter_context(nc.sbuf_tensor("pers", [128, PERS_F], F32))
        cbf = stack.enter_context(nc.sbuf_tensor("cbf", [128, 4 * 128 + NKB * NH], BF16))
        arena = stack.enter_context(nc.sbuf_tensor("arena", [128, ARENA_W], F32))
        arena_b = arena.bitcast(BF16)
        arena_i = arena.bitcast(I32)
        psb = [stack.enter_context(nc.psum_tensor("ps%d" % i, [128, 512], F32)) for i in range(8)]
        PS = [V("ps%d" % i, psb[i][:]) for i in range(8)]

        class Arena:
            def __init__(self):
                self.off = 0

            def reset(self):
                self.off = 0

            def f32(self, key, shape):
                n = int(np.prod(shape))
                v = arena[:, self.off:self.off + n]
                self.off += n
                assert self.off <= ARENA_W, (key, self.off)
                return V(key, _shape(v, shape))

            def bf(self, key, shape):
                n = int(np.prod(shape))
                nw = (n + 1) // 2
                v = arena_b[:, 2 * self.off:2 * self.off + n]
                self.off += nw
                assert self.off <= ARENA_W, (key, self.off)
                return V(key, _shape(v, shape))

            def i32(self, key, shape):
                n = int(np.prod(shape))
                v = arena_i[:, self.off:self.off + n]
                self.off += n
                assert self.off <= ARENA_W, (key, self.off)
                return V(key, _shape(v, shape))

        def _shape(ap, shape):
            if len(shape) == 1:
                return ap
            if len(shape) == 2:
                return ap.rearrange("p (a b) -> p a b", a=shape[0])
            if len(shape) == 3:
                return ap.rearrange("p (a b c) -> p a b c", a=shape[0], b=shape[1])
            raise ValueError

        A = Arena()

        par = V("par", pers[:, 0:NPAR])
        ppos = NPAR

        def pcol(name, j=0, n=1):
            o, _ = POFF[name]
            return pers[:, o + j:o + j + n]

        def persf(key, n):
            nonlocal ppos
            v = V(key, pers[:, ppos:ppos + n])
            ppos += n
            assert ppos <= PERS_F
            return v

        modT = persf("modT", 96)
        a1 = persf("a1", 16)
        a2 = persf("a2", 16)
        lamv = persf("lamv", 8)
        gsub8 = persf("gsub8", 1)
        ones_b = V("cbf", cbf[:, 0:128])
        bones_b = V("cbf", cbf[:, 128:256])
        perm_b = V("cbf", cbf[:, 256:384])
        causal_b = V("cbf", cbf[:, 384:512])
        halo_b = V("cbf", cbf[:, 512:512 + NKB * NH])

        def dma(eng, out, in_, semkey, store=False):
            if store:
                eng = "act"
            return P.op(eng, lambda h, o=out.ap, i=in_.ap: h.dma_start(out=o, in_=i),
                        reads=[in_], writes=[out], dma=True, semkey=semkey)

        def mm(out, lhsT, rhs, start, stop):
            return P.op("pe", lambda h, o=out.ap, l=lhsT.ap, r=rhs.ap: h.matmul(o, lhsT=l, rhs=r, start=start, stop=stop),
                        reads=[lhsT, rhs], writes=[out])

        def act(out, in_, func, bias=None, scale=None, extra_reads=()):
            kw = {}
            if bias is not None:
                kw["bias"] = bias
            if scale is not None:
                kw["scale"] = scale
            return P.op("act", lambda h, o=out.ap, i=in_.ap: h.activation(out=o, in_=i, func=func, **kw),
                        reads=[in_] + list(extra_reads), writes=[out])

        def tt(eng, out, in0, in1, op):
            return P.op(eng, lambda h, o=out.ap, a=in0.ap, b=in1.ap: h.tensor_tensor(out=o, in0=a, in1=b, op=op),
                        reads=[in0, in1], writes=[out])

        def ts(eng, out, in0, s1, s2, op0, op1=None, extra_reads=()):
            def f(h, o=out.ap, a=in0.ap):
                if op1 is None:
                    return h.tensor_scalar(out=o, in0=a, scalar1=s1, scalar2=None, op0=op0)
                return h.tensor_scalar(out=o, in0=a, scalar1=s1, scalar2=s2, op0=op0, op1=op1)
            return P.op(eng, f, reads=[in0] + list(extra_reads), writes=[out])

        def stt(eng, out, in0, scalar, in1, op0, op1, extra_reads=()):
            return P.op(eng, lambda h, o=out.ap, a=in0.ap, b=in1.ap: h.scalar_tensor_tensor(
                out=o, in0=a, scalar=scalar, in1=b, op0=op0, op1=op1),
                reads=[in0, in1] + list(extra_reads), writes=[out])

        def cp(eng, out, in_):
            return P.op(eng, lambda h, o=out.ap, i=in_.ap: h.tensor_copy(out=o, in_=i), reads=[in_], writes=[out])

        def rstd_from(ps_view, out, dim):
            ts("dve", out, ps_view, 1.0 / dim, EPS, ALU.mult, ALU.add)
            act(out, out, AF.Ln)
            act(out, out, AF.Exp, scale=-0.5)

        cosT = A.f32("cosT", [NTOK])
        sinT = A.f32("sinT", [NTOK])
        B1_MARK = A.off
        dma("sp", par, V("params", params[:, :]), "par")
        cst_f = A.f32("cst_f", [4 * 128 + NKB * NH])
        dma("sp", cst_f, V("consts", consts[:, :]), "cst_f")
        cp("dve", V("cbf", cbf[:]), cst_f)
        csil = A.bf("csil", [16])
        act(csil, V("par", pcol("c", 0, 16)), AF.Silu)
        wa_slots = Rot([A.bf("wa%d" % i, [16, 512]) for i in range(2)])
        psA = PS[0]
        for g in range(24):
            wt = wa_slots.next()
            dma("pool", wt, V("w_ada", w_ada.rearrange("(k p) e -> p k e", p=128)[:, :, g * 512:(g + 1) * 512]), wt.key)
            for j in range(4):
                col = g * 4 + j
                for k in range(KD):
                    mm(psA[:, col:col + 1], wt[:, k, j * 128:(j + 1) * 128], csil[:, k:k + 1], k == 0, k == KD - 1)
        tt("dve", modT, psA[:, 0:96], V("par", pcol("b_ada", 0, 96)), ALU.add)
        stt("dve", a1, modT[:, 16:32], 1.0, V("par", pcol("g1", 0, 16)), ALU.add, ALU.mult)
        stt("dve", a2, modT[:, 64:80], 1.0, V("par", pcol("g2", 0, 16)), ALU.add, ALU.mult)
        shift1 = lambda k: modT.ap[:, k:k + 1]
        gate1 = lambda k: modT.ap[:, 32 + k:33 + k]
        shift2 = lambda k: modT.ap[:, 48 + k:49 + k]
        gate2 = lambda k: modT.ap[:, 80 + k:81 + k]
        ltmp = A.f32("ltmp", [64])
        for i, (qn_, kn_) in enumerate((("lq1", "lk1"), ("lq2", "lk2"))):
            tt("dve", ltmp, V("par", pcol(qn_, 0, 64)), V("par", pcol(kn_, 0, 64)), ALU.mult)
            P.op("dve", lambda h, o=lamv.ap[:, i:i + 1], a=ltmp.ap: h.reduce_sum(out=o, in_=a, axis=AX.X),
                 reads=[ltmp], writes=[lamv])
        act(lamv[:, 2:4], lamv[:, 0:2], AF.Exp)
        stt("dve", lamv[:, 4:5], lamv[:, 3:4], -LAMBDA_INIT, lamv[:, 2:3], ALU.add, ALU.subtract)
        ts("dve", gsub8, V("par", pcol("g_sub")), 1.0 - LAMBDA_INIT, None, ALU.mult)
        neglam = lamv.ap[:, 4:5]
        posi = A.i32("posi", [NTOK])
        angT = A.f32("angT", [NTOK])
        dma("sp", posi, V("posrep", posrep[:, :]), "posi")
        cp("dve", angT, posi)
        ts("dve", angT, angT, pcol("invfreq"), None, ALU.mult, extra_reads=[par])
        PI = math.pi
        ki = A.i32("ki", [NTOK])
        kf = A.f32("kf", [NTOK])
        mk = A.f32("mk", [NTOK])
        for tab, shift in ((sinT, 0.0), (cosT, 0.25)):
            ts("dve", tab, angT, 1.0 / (2.0 * PI), shift, ALU.mult, ALU.add)
            cp("dve", ki, tab)
            cp("dve", kf, ki)
            tt("dve", tab, tab, kf, ALU.subtract)
            ts("dve", mk, tab, 0.5, None, ALU.is_gt)
            tt("dve", tab, tab, mk, ALU.subtract)
            ts("dve", mk, tab, -0.5, None, ALU.is_lt)
            tt("dve", tab, tab, mk, ALU.add)
            act(tab, tab, AF.Sin, scale=6.28318)
        P.barrier()

        A.off = B1_MARK

        def tiles_of(start, count):
            return [(start + i, min(512, count - i)) for i in range(0, count, 512)]
        own_tiles = [(t0, n, t0, t0) for (t0, n) in tiles_of(0, NO)]
        halo_tile = (NO, NH, NO, None)
        oth_tiles = [(t0, n, None, t0 - NH) for (t0, n) in tiles_of(NQ, NO)]
        groups = []
        for i in range(0, len(own_tiles), 2):
            g = own_tiles[i:i + 2]
            if i + 2 >= len(own_tiles):
                g = g + [halo_tile]
            groups.append(g)
        for i in range(0, len(oth_tiles), 2):
            groups.append(oth_tiles[i:i + 2])
        GMAX = 1024 + NH

        xt = A.f32("xt", [16, 512])
        hT = A.bf("hT", [16, GMAX])
        wslots = Rot([A.bf("w%d" % i, [16, 512]) for i in range(2)])
        sqb = Rot([A.bf("sqb%d" % i, [512]) for i in range(3)])
        tmpf = Rot([A.f32("tmpf%d" % i, [512]) for i in range(6)])
        ybf = Rot([A.bf("ybf%d" % i, [512]) for i in range(4)])
        yf = Rot([A.f32("yf%d" % i, [512]) for i in range(8)])
        rstds = Rot([A.f32("rstd%d" % i, [512]) for i in range(3)])
        outb = Rot([A.bf("outb%d" % i, [512]) for i in range(4)])
        stg4 = Rot([A.bf("stg4_%d" % i, [4, 512]) for i in range(2)])
        PSM = Rot([PS[0], PS[1], PS[2]])
        PSX = PS[3]
        PSN = Rot([PS[4], PS[5]])
        PSR = Rot([PS[6], PS[7]])

        pn_pending = []

        def pn_flush():
            while pn_pending:
                pn_pending.pop(0)()

        def post_norm(ps_list, n, gains, ones_v, dim, rope_tl, dests):
            psn = PSN.next()
            nj = len(ps_list)
            ys = []
            for j, ps in enumerate(ps_list):
                if rope_tl is None:
                    y = yf.next()
                else:
                    y = ybf.next()
                act(y[:, :n], ps[:, :n], AF.Identity, scale=gains[j], extra_reads=[par])
                sq = sqb.next()
                act(sq[:, :n], ps[:, :n], AF.Square)
                mm(psn[:, :n], ones_v, sq[:, :n], j == 0, j == nj - 1)
                ys.append(y)
            psrs = []
            if rope_tl is not None:
                for y in ys:
                    psr = PSR.next()
                    mm(psr[:, :n], perm_b, y[:, :n], True, True)
                    psrs.append(psr)

            def tail():
                rs = rstds.next()
                rstd_from(psn[:, :n], rs[:, :n], dim)
                for j, y in enumerate(ys):
                    ob = outb.next()
                    if rope_tl is None:
                        tt("dve", ob[:, :n], y[:, :n], rs[:, :n], ALU.mult)
                    else:
                        psr = psrs[j]
                        t1 = tmpf.next()
                        tt("dve", t1[:, :n], y[:, :n], cosT[:, rope_tl:rope_tl + n], ALU.mult)
                        t2 = tmpf.next()
                        tt("dve", t2[:, :n], psr[:, :n], sinT[:, rope_tl:rope_tl + n], ALU.mult)
                        tt("dve", t1[:, :n], t1[:, :n], t2[:, :n], ALU.add)
                        tt("dve", ob[:, :n], t1[:, :n], rs[:, :n], ALU.mult)
                    dma("sp", dests[j], ob[:, :n], ob.key, store=True)
            prev = list(pn_pending)
            del pn_pending[:]
            for f in prev:
                f()
            pn_pending.append(tail)

        def build_h(tile, goff, a_v, shift_fn, src_fn):
            tl0, n = tile[0], tile[1]
            src_fn(xt, tl0, n)
            for k in range(KD):
                sq = sqb.next()
                act(sq[:, :n], xt[:, k, :n], AF.Square)
                mm(PSX[:, :n], ones_b, sq[:, :n], k == 0, k == KD - 1)
            rs = rstds.next()
            rstd_from(PSX[:, :n], rs[:, :n], D)
            for k in range(KD):
                t = tmpf.next()
                stt("dve", t[:, :n], xt[:, k, :n], a_v.ap[:, k:k + 1], rs[:, :n], ALU.mult, ALU.mult, extra_reads=[a_v])
                act(V(hT.key + str(goff), hT.ap[:, k, goff:goff + n]), t[:, :n], AF.Identity, bias=shift_fn(k),
                    extra_reads=[modT])

        def load_x(dst, tl0, n):
            dma("sp", V(dst.key, dst.ap[:, :, :n]),
                V("xT", xT.rearrange("(k p) t -> p k t", p=128)[:, :, tl0:tl0 + n]), dst.key)

        def load_w(src, c0, ncols, kd=KD):
            wt = wslots.next()
            dma("pool", V(wt.key, wt.ap[:, :kd, :ncols]),
                V("w", src.rearrange("(k p) e -> p k e", p=128)[:, :, c0:c0 + ncols]), wt.key)
            return wt

        def proj_fm(wt, wc0, tile_off, n, goff_key):
            ps = PSM.next()
            for k in range(KD):
                mm(ps[:, :n], wt[:, k, wc0:wc0 + 128], V(hT.key + str(goff_key), hT.ap[:, k, tile_off:tile_off + n]),
                   k == 0, k == KD - 1)
            return ps

        gq_lat = [pcol("g_q_lat", j) for j in range(4)]
        gkv_lat = [pcol("g_kv_lat", j) for j in range(2)]

        for grp in groups:
            goffs = []
            go = 0
            for tile in grp:
                build_h(tile, go, a1, shift1, load_x)
                goffs.append(go)
                go += tile[1]
            has_q = grp[0][2] is not None
            if has_q:
                wt = load_w(w_q, 0, 512)
                for tile, go in zip(grp, goffs):
                    tl0, n, q0, _ = tile
                    pss = [proj_fm(wt, j * 128, go, n, go) for j in range(3)]
                    ps4 = PSR.next()
                    for k in range(KD):
                        mm(ps4[:, :n], wt[:, k, 384:512], V(hT.key + str(go), hT.ap[:, k, go:go + n]), k == 0, k == KD - 1)
                    pss.append(ps4)
                    post_norm(pss, n, gq_lat, ones_b, 512, None,
                              [V("QLN", QLN[j, :, q0:q0 + n]) for j in range(4)])
                for cg in range(2):
                    wt = load_w(w_q, 512 + cg * 512, 512)
                    for tile, go in zip(grp, goffs):
                        tl0, n, q0, _ = tile
                        for j in range(4):
                            hd = cg * 4 + j
                            ps = proj_fm(wt, j * 128, go, n, go)
                            post_norm([ps], n, [pcol("gq_diff")], bones_b, 64, tl0, [V("DQ", DQ[hd, :, q0:q0 + n])])
                for cg in range(8):
                    wt = load_w(w_q, 1536 + cg * 512, 512)
                    for tile, go in zip(grp, goffs):
                        tl0, n, q0, _ = tile
                        st4 = stg4.next()
                        for j in range(4):
                            ch = cg * 4 + j
                            ps = proj_fm(wt, j * 128, go, n, go)
                            act(st4[:, j, :n], ps[:, :n], AF.Sigmoid, bias=pcol("b_gate", ch), extra_reads=[par])
                        dma("sp", V("G", G[cg * 4:cg * 4 + 4, :, q0:q0 + n].rearrange("c p q -> p c q")),
                            V(st4.key, st4.ap[:, :, :n]), st4.key, store=True)
            kv_tiles = [(tile, go) for tile, go in zip(grp, goffs) if tile[3] is not None]
            wt = load_w(w_kv, 0, 384)
            for tile, go in kv_tiles:
                tl0, n, _, kv0 = tile
                pss = [proj_fm(wt, j * 128, go, n, go) for j in range(2)]
                post_norm(pss, n, gkv_lat, ones_b, 256, None, [V("KVN", KVN[j, :, kv0:kv0 + n]) for j in range(2)])
                ps = proj_fm(wt, 256, go, n, go)
                post_norm([ps], n, [pcol("gk_pe")], bones_b, 64, tl0, [V("KPE", KPE[:, kv0:kv0 + n])])
            for cg in range(2):
                wt = load_w(w_kv, 384 + cg * 512, 512)
                for tile, go in kv_tiles:
                    tl0, n, _, kv0 = tile
                    for j in range(4):
                        hd = cg * 4 + j
                        ps = proj_fm(wt, j * 128, go, n, go)
                        post_norm([ps], n, [pcol("gk_diff")], bones_b, 64, tl0, [V("DK", DK[hd, :, kv0:kv0 + n])])
            for cg in range(2):
                wt = load_w(w_kv, 1408 + cg * 512, 512)
                for tile, go in kv_tiles:
                    tl0, n, _, kv0 = tile
                    for b in range(n // 128):
                        ps = PSM.next()
                        for k in range(KD):
                            mm(ps, V(hT.key + str(go), hT.ap[:, k, go + b * 128:go + (b + 1) * 128]), wt[:, k, :],
                               k == 0, k == KD - 1)
                        ob = outb.next()
                        act(ob, ps, AF.Identity)
                        dma("sp", V("DV", DV[kv0 + b * 128:kv0 + (b + 1) * 128, cg * 512:(cg + 1) * 512]), ob, ob.key, store=True)
        pn_flush()
        P.barrier()

        A.off = B1_MARK
        qln = A.bf("qln", [4, NQ])
        kvn = A.bf("kvn", [2, S])
        wqu = A.bf("wqu", [4, 1536])
        wkvu = A.bf("wkvu", [2, 2048])
        sqb = Rot([A.bf("sqb%d" % i, [512]) for i in range(3)])
        tmpf = Rot([A.f32("tmpf%d" % i, [512]) for i in range(6)])
        ybf = Rot([A.bf("ybf%d" % i, [512]) for i in range(4)])
        yf = Rot([A.f32("yf%d" % i, [512]) for i in range(8)])
        rstds = Rot([A.f32("rstd%d" % i, [512]) for i in range(3)])
        outb = Rot([A.bf("outb%d" % i, [512]) for i in range(4)])
        dma("sp", qln, V("QLN", QLN.rearrange("c p q -> p c q")), "qln")
        dma("sp", kvn, V("KVN", KVN.rearrange("c p q -> p c q")), "kvn")
        dma("pool", wqu, V("w_q_up", w_q_up.rearrange("(k p) e -> p k e", p=128)), "wqu")
        for hh in range(2):
            dma("pool", V("wkvu", wkvu.ap[:, :, hh * 1024:(hh + 1) * 1024]),
                V("w_kv_up", w_kv_up.rearrange("(k p) e -> p k e", p=128)[:, :, hh * 1024:(hh + 1) * 1024]), "wkvu")
        q_tiles = [(t0, n) for (t0, n) in tiles_of(0, NO)] + [(NO, NH)]
        for (q0, n) in q_tiles:
            for hd in range(8):
                ps = PSM.next()
                for k in range(4):
                    mm(ps[:, :n], wqu[:, k, hd * 128:(hd + 1) * 128], qln[:, k, q0:q0 + n], k == 0, k == 3)
                post_norm([ps], n, [pcol("gq_nope")], ones_b, 128, None, [V("QN", QN[hd, :, q0:q0 + n])])
            for j in range(4):
                ps = PSM.next()
                for k in range(4):
                    mm(ps[:, :n], wqu[:, k, 1024 + j * 128:1024 + (j + 1) * 128], qln[:, k, q0:q0 + n], k == 0, k == 3)
                post_norm([ps], n, [pcol("gq_pe")], bones_b, 64, q0, [V("QPE", QPE[j, :, q0:q0 + n])])
        for (kv0, n) in tiles_of(0, S):
            for hd in range(8):
                ps = PSM.next()
                for k in range(2):
                    mm(ps[:, :n], wkvu[:, k, hd * 128:(hd + 1) * 128], kvn[:, k, kv0:kv0 + n], k == 0, k == 1)
                post_norm([ps], n, [pcol("gk_nope")], ones_b, 128, None, [V("KN", KN[hd, :, kv0:kv0 + n])])
            for b in range(n // 128):
                for cg in range(2):
                    ps = PSM.next()
                    for k in range(2):
                        mm(ps, kvn[:, k, kv0 + b * 128:kv0 + (b + 1) * 128], wkvu[:, k, 1024 + cg * 512:1024 + (cg + 1) * 512],
                           k == 0, k == 1)
                    ob = outb.next()
                    act(ob, ps, AF.Identity)
                    dma("sp", V("VM", VM[kv0 + b * 128:kv0 + (b + 1) * 128, cg * 512:(cg + 1) * 512]), ob, ob.key, store=True)
        pn_flush()
        P.barrier()

        A.reset()
        kpe = A.bf("kpe", [S])
        dma("sp", kpe, V("KPE", KPE[:, :]), "kpe")
        Kh = Rot([A.bf("Kh%d" % i, [S]) for i in range(2)])
        Vh = Rot([A.bf("Vh%d" % i, [NKB, 128]) for i in range(2)])
        Qh = Rot([A.bf("Qh%d" % i, [NQ]) for i in range(2)])
        Qp = Rot([A.bf("Qp%d" % i, [NQ]) for i in range(2)])
        Oh = Rot([A.bf("Oh%d" % i, [NQ]) for i in range(2)])
        Pb = Rot([A.bf("Pb%d" % i, [512]) for i in range(8)])
        recs = Rot([A.f32("rec%d" % i, [128]) for i in range(6)])
        odf = Rot([A.f32("odf%d" % i, [128]) for i in range(6)])
        sqc = Rot([A.bf("sqc%d" % i, [128]) for i in range(3)])

        qblocks = []
        for i in range(NOB):
            kl = [(l, "full") for l in range(i)] + [(NOB + l, "full") for l in range(i)] + [(NOB + i, "keep"), (i, "causal")]
            qblocks.append((i * 128, 128, kl))
        qblocks.append((NO, NH, [(l, "halo") for l in range(NKB)]))

        def exp_group(Sps, qn, grp_list, scale, pb):
            ng = len(grp_list)
            if qn == 128:
                act(pb[:, :ng * 128], Sps[:, :ng * 128], AF.Exp, scale=scale)
            else:
                act(V(pb.key, pb.ap.rearrange("p (j q) -> p j q", q=128)[:, :ng, :qn]),
                    V(Sps.key, Sps.ap.rearrange("p (j q) -> p j q", q=128)[:, :ng, :qn]), AF.Exp, scale=scale)
            for jj, (l, mode) in enumerate(grp_list):
                reg = pb[:, jj * 128:jj * 128 + qn]
                if mode == "keep":
                    ts("dve", reg, reg, pcol("keep"), None, ALU.mult, extra_reads=[par])
                elif mode == "causal":
                    tt("dve", reg, reg, causal_b, ALU.mult)
                elif mode == "halo":
                    tt("dve", reg, reg, halo_b[:, l * NH:(l + 1) * NH], ALU.mult)

        def load_head(Ksrc, Vsrc, Qsrc, hd):
            kh = Kh.next()
            dma("sp", kh, V("K", Ksrc[hd, :, :]), kh.key)
            vh = Vh.next()
            dma("sp", vh, V("Vs", Vsrc.rearrange("(j p) d -> p j d", p=128)[:, :, hd * 128:(hd + 1) * 128]), vh.key)
            qh = Qh.next()
            dma("sp", qh, V("Q", Qsrc[hd, :, :]), qh.key)
            return kh, vh, qh

        SC_MLA = 192.0 ** -0.5
        SC_DIFF = 64.0 ** -0.5
        def attn_driver(heads, emit_S, emit_exp, emit_PV, emit_fin_a, emit_fin_b, head_done):
            items = []
            for hd in heads:
                for bi, (q0, qn, kl) in enumerate(qblocks):
                    nk = len(kl)
                    ng = (nk + 3) // 4
                    for gi in range(ng):
                        items.append({"hd": hd, "bi": bi, "q0": q0, "qn": qn, "kl": kl, "g0": gi * 4,
                                      "gl": kl[gi * 4:gi * 4 + 4], "nk": nk,
                                      "first_of_head": bi == 0 and gi == 0,
                                      "last_of_qb": gi == ng - 1,
                                      "last_of_head": bi == len(qblocks) - 1 and gi == ng - 1})
            pend_b = []
            if ATTN_PIPE:
                emit_S(items[0])
            for k, it in enumerate(items):
                if ATTN_PIPE:
                    if k + 1 < len(items):
                        emit_S(items[k + 1])
                else:
                    emit_S(it)
                emit_exp(it)
                emit_PV(it)
                while pend_b:
                    pend_b.pop(0)()
                if it["last_of_qb"]:
                    emit_fin_a(it)
                    if emit_fin_b is not None:
                        pend_b.append(lambda it=it: emit_fin_b(it))
                if it["last_of_head"]:
                    while pend_b:
                        pend_b.pop(0)()
                    head_done(it)

        SB = Rot([PS[0], PS[1], PS[4], PS[5]])
        hctx = {}

        def mla_S(it):
            hd = it["hd"]
            if it["first_of_head"]:
                kh, vh, qh = load_head(KN, VM, QN, hd)
                if hd % 2 == 0:
                    qp = Qp.next()
                    dma("sp", qp, V("QPE", QPE[hd // 2, :, :]), qp.key)
                    hctx["qp"] = qp
                hctx[hd] = (kh, vh, qh, hctx["qp"], Oh.next())
            kh, vh, qh, qp, oh = hctx[hd]
            hp = (hd % 2) * 64
            q0, qn = it["q0"], it["qn"]
            Sps = SB.next()
            it["S"] = Sps
            for jj, (l, mode) in enumerate(it["gl"]):
                reg = Sps[:, jj * 128:jj * 128 + qn]
                mm(reg, kh[:, l * 128:(l + 1) * 128], qh[:, q0:q0 + qn], True, False)
                mm(reg, kpe[hp:hp + 64, l * 128:(l + 1) * 128], qp[hp:hp + 64, q0:q0 + qn], False, True)

        def mla_exp(it):
            pb = Pb.next()
            it["pb"] = pb
            exp_group(it["S"], it["qn"], it["gl"], SC_MLA, pb)

        def mla_acc(it):
            r0 = (it["bi"] % 4) * 128
            qn = it["qn"]
            return (V("ps2_r%d" % (it["bi"] % 4), PS[2].ap[:, r0:r0 + qn]),
                    V("ps3_r%d" % (it["bi"] % 4), PS[3].ap[:, r0:r0 + qn]))

        def mla_PV(it):
            kh, vh, qh, qp, oh = hctx[it["hd"]]
            oacc, dacc = mla_acc(it)
            qn, pb = it["qn"], it["pb"]
            for jj, (l, mode) in enumerate(it["gl"]):
                first = (it["g0"] + jj == 0)
                last = (it["g0"] + jj == it["nk"] - 1)
                mm(oacc, vh[:, l, :], pb[:, jj * 128:jj * 128 + qn], first, last)
                mm(dacc, ones_b, pb[:, jj * 128:jj * 128 + qn], first, last)

        def mla_fin(it):
            kh, vh, qh, qp, oh = hctx[it["hd"]]
            oacc, dacc = mla_acc(it)
            q0, qn = it["q0"], it["qn"]
            rc = recs.next()
            P.op("dve", lambda h, o=rc.ap[:, :qn], i=dacc.ap: h.reciprocal(out=o, in_=i), reads=[dacc], writes=[rc])
            tt("dve", oh[:, q0:q0 + qn], oacc, rc[:, :qn], ALU.mult)

        def mla_done(it):
            oh = hctx[it["hd"]][4]
            dma("sp", V("OM", OM[it["hd"], :, :]), oh, oh.key, store=True)

        attn_driver(range(8), mla_S, mla_exp, mla_PV, mla_fin, None, mla_done)

        S1B = Rot([PS[0], PS[1]])
        S2B = Rot([PS[4], PS[5]])
        dctx = {}

        def df_S(it):
            hd = it["hd"]
            if it["first_of_head"]:
                kh, vh, qh = load_head(DK, DV, DQ, hd)
                dctx[hd] = (kh, vh, qh, Oh.next())
            kh, vh, qh, oh = dctx[hd]
            q0, qn = it["q0"], it["qn"]
            S1 = S1B.next()
            S2 = S2B.next()
            it["S1"], it["S2"] = S1, S2
            for jj, (l, mode) in enumerate(it["gl"]):
                mm(S1[:, jj * 128:jj * 128 + qn], kh[0:64, l * 128:(l + 1) * 128], qh[0:64, q0:q0 + qn], True, True)
                mm(S2[:, jj * 128:jj * 128 + qn], kh[64:128, l * 128:(l + 1) * 128], qh[64:128, q0:q0 + qn], True, True)

        def df_exp(it):
            p1 = Pb.next()
            p2 = Pb.next()
            it["p1"], it["p2"] = p1, p2
            exp_group(it["S1"], it["qn"], it["gl"], SC_DIFF, p1)
            exp_group(it["S2"], it["qn"], it["gl"], SC_DIFF, p2)

        def df_acc(it):
            r0 = (it["bi"] % 4) * 128
            qn = it["qn"]
            return [V("ps%d_r%d" % (b_, it["bi"] % 4), PS[b_].ap[:, r0:r0 + qn]) for b_ in (2, 3, 6, 7)]

        def df_PV(it):
            kh, vh, qh, oh = dctx[it["hd"]]
            o1, d1, o2, d2 = df_acc(it)
            qn, p1, p2 = it["qn"], it["p1"], it["p2"]
            for jj, (l, mode) in enumerate(it["gl"]):
                first = (it["g0"] + jj == 0)
                last = (it["g0"] + jj == it["nk"] - 1)
                mm(o1, vh[:, l, :], p1[:, jj * 128:jj * 128 + qn], first, last)
                mm(d1, ones_b, p1[:, jj * 128:jj * 128 + qn], first, last)
                mm(o2, vh[:, l, :], p2[:, jj * 128:jj * 128 + qn], first, last)
                mm(d2, ones_b, p2[:, jj * 128:jj * 128 + qn], first, last)

        def df_fin_a(it):
            o1, d1, o2, d2 = df_acc(it)
            qn = it["qn"]
            r1 = recs.next()
            r2 = recs.next()
            P.op("dve", lambda h, o=r1.ap[:, :qn], i=d1.ap: h.reciprocal(out=o, in_=i), reads=[d1], writes=[r1])
            P.op("dve", lambda h, o=r2.ap[:, :qn], i=d2.ap: h.reciprocal(out=o, in_=i), reads=[d2], writes=[r2])
            ts("dve", r2[:, :qn], r2[:, :qn], neglam, None, ALU.mult, extra_reads=[lamv])
            t1 = odf.next()
            tt("dve", t1[:, :qn], o1, r1[:, :qn], ALU.mult)
            t2 = odf.next()
            tt("dve", t2[:, :qn], o2, r2[:, :qn], ALU.mult)
            tt("dve", t1[:, :qn], t1[:, :qn], t2[:, :qn], ALU.add)
            sq = sqc.next()
            act(sq[:, :qn], t1[:, :qn], AF.Square)
            it["t1"], it["sq"] = t1, sq

        def df_fin_b(it):
            kh, vh, qh, oh = dctx[it["hd"]]
            q0, qn = it["q0"], it["qn"]
            t1, sq = it["t1"], it["sq"]
            psn = S1B.items[S1B.i % len(S1B.items)]
            mm(psn[:, :qn], ones_b, sq[:, :qn], True, True)
            rs = recs.next()
            rstd_from(psn[:, :qn], rs[:, :qn], 128)
            stt("dve", oh[:, q0:q0 + qn], t1[:, :qn], gsub8.ap[:, 0:1], rs[:, :qn], ALU.mult, ALU.mult,
                extra_reads=[gsub8])

        def df_done(it):
            oh = dctx[it["hd"]][3]
            dma("sp", V("OD", OD[it["hd"], :, :]), oh, oh.key, store=True)

        attn_driver(range(8), df_S, df_exp, df_PV, df_fin_a, df_fin_b, df_done)
        pn_flush()
        P.barrier()

        A.reset()
        omt = A.bf("omt", [8, 512])
        odt = A.bf("odt", [8, 512])
        gts = Rot([A.bf("gt%d" % i, [8, 512]) for i in range(2)])
        mixed = A.bf("mixed", [16, 512])
        x1t = A.f32("x1t", [16, 512])
        xcs = Rot([A.f32("xcD%d" % i, [512]) for i in range(3)])
        h2t = A.bf("h2t", [16, 512])
        wos = Rot([A.bf("wo%d" % i, [8, 512]) for i in range(4)])
        wslots = Rot([A.bf("wD%d" % i, [16, 512]) for i in range(2)])
        tmpf = Rot([A.f32("tmpfD%d" % i, [512]) for i in range(6)])
        sqb = Rot([A.bf("sqbD%d" % i, [512]) for i in range(3)])
        rstds = Rot([A.f32("rstdD%d" % i, [512]) for i in range(2)])
        PSMD = Rot([PS[0], PS[1], PS[2], PS[3]])
        PSO = Rot([PS[4], PS[5]])
        PSX = PS[6]
        for (q0, n) in q_tiles:
            dma("sp", V(omt.key, omt.ap[:, :, :n]), V("OM", OM[:, :, q0:q0 + n].rearrange("h p q -> p h q")), omt.key)
            dma("sp", V(odt.key, odt.ap[:, :, :n]), V("OD", OD[:, :, q0:q0 + n].rearrange("h p q -> p h q")), odt.key)
            for eg in range(4):
                wm = wos.next()
                dma("pool", wm, V("w", w_o_mla.rearrange("(k p) e -> p k e", p=128)[:, :, eg * 512:(eg + 1) * 512]), wm.key)
                wd_ = wos.next()
                dma("pool", wd_, V("w", w_o_diff.rearrange("(k p) e -> p k e", p=128)[:, :, eg * 512:(eg + 1) * 512]), wd_.key)
                gt = gts.next()
                for ab in range(2):
                    dma("sp", V(gt.key, gt.ap[:, ab * 4:ab * 4 + 4, :n]),
                        V("G", G[ab * 16 + eg * 4:ab * 16 + eg * 4 + 4, :, q0:q0 + n].rearrange("h p q -> p h q")), gt.key)
                for j in range(4):
                    e = eg * 4 + j
                    psm = PSMD.next()
                    for k in range(8):
                        mm(psm[:, :n], wm[:, k, j * 128:(j + 1) * 128], omt[:, k, :n], k == 0, k == 7)
                    psd = PSMD.next()
                    for k in range(8):
                        mm(psd[:, :n], wd_[:, k, j * 128:(j + 1) * 128], odt[:, k, :n], k == 0, k == 7)
                    t1 = tmpf.next()
                    tt("dve", t1[:, :n], psm[:, :n], gt[:, j, :n], ALU.mult)
                    t2 = tmpf.next()
                    tt("dve", t2[:, :n], psd[:, :n], gt[:, 4 + j, :n], ALU.mult)
                    tt("dve", mixed[:, e, :n], t1[:, :n], t2[:, :n], ALU.add)
            for eg in range(4):
                wt = wslots.next()
                dma("pool", wt, V("w", w_out.rearrange("(k p) e -> p k e", p=128)[:, :, eg * 512:(eg + 1) * 512]), wt.key)
                for j in range(4):
                    e = eg * 4 + j
                    ps = PSO.next()
                    for k in range(KD):
                        mm(ps[:, :n], wt[:, k, j * 128:(j + 1) * 128], mixed[:, k, :n], k == 0, k == KD - 1)
                    xc = xcs.next()
                    dma("sp", V(xc.key, xc.ap[:, :n]), V("xT", xT[e * 128:(e + 1) * 128, q0:q0 + n]), xc.key)
                    stt("dve", x1t[:, e, :n], ps[:, :n], gate1(e), xc[:, :n], ALU.mult, ALU.add, extra_reads=[modT])
            if q0 < NO:
                dma("sp", V("X1", X1[:, :, q0:q0 + n].rearrange("c p q -> p c q")), V(x1t.key, x1t.ap[:, :, :n]), x1t.key, store=True)
            for k in range(KD):
                sq = sqb.next()
                act(sq[:, :n], x1t[:, k, :n], AF.Square)
                mm(PSX[:, :n], ones_b, sq[:, :n], k == 0, k == KD - 1)
            rs = rstds.next()
            rstd_from(PSX[:, :n], rs[:, :n], D)
            for k in range(KD):
                t = tmpf.next()
                stt("dve", t[:, :n], x1t[:, k, :n], a2.ap[:, k:k + 1], rs[:, :n], ALU.mult, ALU.mult, extra_reads=[a2])
                if q0 >= NO:
                    act(t[:, :n], t[:, :n], AF.Identity, bias=shift2(k), extra_reads=[modT])
                    tt("dve", h2t[:, k, :n], t[:, :n], V("par", pcol("hvalid", 0, NH)), ALU.mult)
                else:
                    act(h2t[:, k, :n], t[:, :n], AF.Identity, bias=shift2(k), extra_reads=[modT])
            dma("sp", V("H2", H2[:, :, q0:q0 + n].rearrange("c p q -> p c q")), V(h2t.key, h2t.ap[:, :, :n]), h2t.key, store=True)
        pn_flush()
        P.barrier()

        A.reset()
        h2 = A.bf("h2", [16, NQ])
        dma("sp", h2, V("H2", H2.rearrange("c p q -> p c q")), "h2")
        wv_s = Rot([A.bf("wv%d" % i, [16, 256]) for i in range(2)])
        wg_s = Rot([A.bf("wg%d" % i, [16, 256]) for i in range(2)])
        uext = Rot([A.f32("uext%d" % i, [NOB, 130]) for i in range(4)])
        ycv = Rot([A.f32("ycv%d" % i, [NOB, 128]) for i in range(4)])
        zb = Rot([A.bf("zb%d" % i, [NOB, 128]) for i in range(2)])
        PSE = Rot(PS)
        own_q_tiles = tiles_of(0, NO)

        def conv_chunk(wt, wc0, ch):
            ue = uext.next()
            for (q0, n) in own_q_tiles:
                ps = PSE.next()
                for k in range(KD):
                    mm(ps[:, :n], wt[:, k, wc0:wc0 + 128], h2[:, k, q0:q0 + n], k == 0, k == KD - 1)
                nb = n // 128
                b0 = q0 // 128
                act(V(ue.key, ue.ap[:, b0:b0 + nb, 2:130]), V(ps.key, ps.ap[:, :n].rearrange("p (b t) -> p b t", t=128)),
                    AF.Identity)
            ps = PSE.next()
            for k in range(KD):
                mm(ps[:, :NH], wt[:, k, wc0:wc0 + 128], h2[:, k, NO:NO + NH], k == 0, k == KD - 1)
            cp("dve", V(ue.key, ue.ap[:, :, 0:2]), V(ps.key, ps.ap[:, :NH].rearrange("p (b t) -> p b t", t=2)))
            cw = lambda j: pcol("conv_w", j * 88 + ch)
            y = ycv.next()
            act(y, V(ue.key, ue.ap[:, :, 2:130]), AF.Identity, bias=pcol("conv_b", ch), scale=cw(2), extra_reads=[par])
            stt("dve", y, V(ue.key, ue.ap[:, :, 1:129]), cw(1), y, ALU.mult, ALU.add, extra_reads=[par])
            stt("dve", y, V(ue.key, ue.ap[:, :, 0:128]), cw(0), y, ALU.mult, ALU.add, extra_reads=[par])
            return y

        for cp0 in range(0, 44, 2):
            wv = wv_s.next()
            dma("pool", wv, V("w", w_up.rearrange("(k p) e -> p k e", p=128)[:, :, cp0 * 128:cp0 * 128 + 256]), wv.key)
            wg = wg_s.next()
            dma("pool", wg, V("w", w_up.rearrange("(k p) e -> p k e", p=128)[:, :, DFF + cp0 * 128:DFF + cp0 * 128 + 256]), wg.key)
            for cc in range(2):
                c = cp0 + cc
                yv = conv_chunk(wv, cc * 128, c)
                yg = conv_chunk(wg, cc * 128, 44 + c)
                act(yg, yg, AF.Silu)
                z = zb.next()
                tt("dve", z, yg, yv, ALU.mult)
                dma("sp", V("Z", Z[c, :, :]), V(z.key, z.ap.rearrange("p b t -> p (b t)")), z.key, store=True)
        pn_flush()
        P.barrier()

        A.reset()
        wd_s = Rot([A.bf("wdn%d" % i, [44, 512]) for i in range(2)])
        zt_s = Rot([A.bf("zt%d" % i, [44, 512]) for i in range(2)])
        x1c = Rot([A.f32("x1c%d" % i, [512]) for i in range(2)])
        oc = Rot([A.f32("oc%d" % i, [512]) for i in range(2)])
        PSE = Rot(PS)
        for eg in range(4):
            wd = wd_s.next()
            dma("pool", wd, V("w", w_down.rearrange("(k p) e -> p k e", p=128)[:, :, eg * 512:(eg + 1) * 512]), wd.key)
            for (q0, n) in own_q_tiles:
                zt = zt_s.next()
                dma("sp", V(zt.key, zt.ap[:, :, :n]), V("Z", Z[:, :, q0:q0 + n].rearrange("c p q -> p c q")), zt.key)
                for j in range(4):
                    e = eg * 4 + j
                    ps = PSE.next()
                    for c in range(44):
                        mm(ps[:, :n], wd[:, c, j * 128:(j + 1) * 128], zt[:, c, :n], c == 0, c == 43)
                    xc = x1c.next()
                    dma("sp", V(xc.key, xc.ap[:, :n]), V("X1", X1[e, :, q0:q0 + n]), xc.key)
                    o = oc.next()
                    stt("dve", o[:, :n], ps[:, :n], gate2(e), xc[:, :n], ALU.mult, ALU.add, extra_reads=[modT])
                    od_ = dma("sp", V("outT", outT[e * 128:(e + 1) * 128, q0:q0 + n]), V(o.key, o.ap[:, :n]), o.key, store=True)
                    P.out_dmas.append(od_)
        fin = P.op("sp", None)
        fin.deps.update(o.idx for o in P.out_dmas)
        P.emit(nc, stack)
    return nc


def host_prep(inp, S):
    NO = S // 2
    NOB = NO // 128
    NH = 2 * NOB
    NKB = 2 * NOB
    POFF, NPAR = par_layout(NH)
    f32 = np.float32
    x = np.asarray(inp["x"], f32)
    pos = np.asarray(inp["positions"], np.int32)
    B = x.shape[0]

    def fm(v):
        v = np.asarray(v, f32).reshape(-1, 128)
        return np.ascontiguousarray(v.T)

    def rep(v):
        v = np.asarray(v, f32).reshape(1, -1)
        return np.repeat(v, 128, axis=0)

    w_in = np.asarray(inp["w_in"][0], f32)
    q_lat, kv_lat, k_pe, dq, dk, dv, gl = np.split(w_in, np.cumsum([512, 256, 64, 1024, 1024, 1024])[:], axis=1)
    w_q = np.ascontiguousarray(np.concatenate([q_lat, dq, gl], axis=1))
    w_kv = np.ascontiguousarray(np.concatenate([kv_lat, k_pe, k_pe, dk, dv], axis=1))
    wqu = np.asarray(inp["w_q_up"][0], f32).reshape(512, 8, 192)
    w_q_up = np.ascontiguousarray(np.concatenate([wqu[:, :, :128].reshape(512, 1024), wqu[:, :, 128:].reshape(512, 512)], axis=1))
    wkvu = np.asarray(inp["w_kv_up"][0], f32).reshape(256, 8, 256)
    w_kv_up = np.ascontiguousarray(np.concatenate([wkvu[:, :, :128].reshape(256, 1024), wkvu[:, :, 128:].reshape(256, 1024)], axis=1))

    ones = np.ones((128, 128), f32)
    bones = np.zeros((128, 128), f32)
    bones[:64, :64] = 1
    bones[64:, 64:] = 1
    perm = np.zeros((128, 128), f32)
    for m in range(128):
        if (m % 64) < 32:
            perm[m + 32, m] = -1.0
        else:
            perm[m - 32, m] = 1.0
    causal = (np.arange(128)[None, :] >= np.arange(128)[:, None]).astype(f32)
    invfreq = (10000.0 ** (-(np.arange(0, 64, 2, dtype=f32)) / f32(64))).astype(f32)
    invf128 = np.tile(invfreq, 4).reshape(128, 1)

    gq = np.asarray(inp["g_q_mla"][0], f32)
    gk = np.asarray(inp["g_k_mla"][0], f32)
    shared = {
        "b_ada": fm(inp["b_ada"][0]), "g1": fm(inp["g_norm1"][0]), "g2": fm(inp["g_norm2"][0]),
        "b_gate": fm(inp["b_gate"][0]),
        "conv_w": np.concatenate([fm(inp["conv_w"][0][j]) for j in range(3)], axis=1),
        "conv_b": fm(inp["conv_b"][0]),
        "g_q_lat": fm(inp["g_q_lat"][0]), "g_kv_lat": fm(inp["g_kv_lat"][0]),
        "gq_nope": gq[:128].reshape(128, 1), "gq_pe": np.tile(gq[128:], 2).reshape(128, 1),
        "gk_nope": gk[:128].reshape(128, 1), "gk_pe": np.tile(gk[128:], 2).reshape(128, 1),
        "gq_diff": np.tile(np.asarray(inp["g_q_diff"][0], f32), 2).reshape(128, 1),
        "gk_diff": np.tile(np.asarray(inp["g_k_diff"][0], f32), 2).reshape(128, 1),
        "g_sub": np.asarray(inp["g_sub_diff"][0], f32).reshape(128, 1),
        "invfreq": invf128,
        "lq1": rep(inp["lam_q1"][0]), "lk1": rep(inp["lam_k1"][0]),
        "lq2": rep(inp["lam_q2"][0]), "lk2": rep(inp["lam_k2"][0]),
    }
    big = {
        "w_ada": np.ascontiguousarray(np.asarray(inp["w_ada"][0], f32)),
        "w_q": w_q, "w_kv": w_kv, "w_q_up": w_q_up, "w_kv_up": w_kv_up,
        "w_o_mla": np.ascontiguousarray(np.asarray(inp["w_o_mla"][0], f32)),
        "w_o_diff": np.ascontiguousarray(np.asarray(inp["w_o_diff"][0], f32)),
        "w_out": np.ascontiguousarray(np.asarray(inp["w_out"][0], f32)),
        "w_up": np.ascontiguousarray(np.asarray(inp["w_up"][0], f32)),
        "w_down": np.ascontiguousarray(np.asarray(inp["w_down"][0], f32)),
    }
    in_maps = []
    own_idx = []
    for core in range(NCORES):
        b, c = core // 2, core % 2
        own_blocks = [2 * i + c for i in range(NOB)]
        oth_blocks = [2 * i + (1 - c) for i in range(NOB)]
        own_tok = np.concatenate([np.arange(j * 128, (j + 1) * 128) for j in own_blocks])
        oth_tok = np.concatenate([np.arange(j * 128, (j + 1) * 128) for j in oth_blocks])
        halo_tok = []
        hvalid = []
        for j in own_blocks:
            for d_ in (2, 1):
                t = j * 128 - d_
                if t >= 0:
                    halo_tok.append(t)
                    hvalid.append(1.0)
                else:
                    halo_tok.append(2 - d_)
                    hvalid.append(0.0)
        halo_tok = np.array(halo_tok)
        tl = np.concatenate([own_tok, halo_tok, oth_tok])
        kv_tok = np.concatenate([own_tok, oth_tok])
        xT = np.ascontiguousarray(x[b][tl].T)
        posr = np.ascontiguousarray(np.repeat(pos[b][tl].reshape(1, -1), 128, axis=0)).astype(np.int32)
        hm = (kv_tok.reshape(NKB, 128).T[:, :, None] <= halo_tok[None, None, :]).astype(f32).reshape(128, NKB * NH)
        cst = np.ascontiguousarray(np.concatenate([ones, bones, perm, causal, hm], axis=1))
        par = np.zeros((128, NPAR), f32)
        for name, arr in shared.items():
            o, n = POFF[name]
            par[:, o:o + n] = arr
        o, n = POFF["c"]
        par[:, o:o + n] = fm(np.asarray(inp["c"], f32)[b])
        o, n = POFF["keep"]
        par[:, o] = 1.0 if c == 1 else 0.0
        o, n = POFF["hvalid"]
        par[:, o:o + n] = np.asarray(hvalid, f32)[None, :]
        m = {"xT": xT, "params": par, "posrep": posr, "consts": cst}
        m.update(big)
        in_maps.append(m)
        own_idx.append((b, own_tok))
    return in_maps, own_idx


_NC_CACHE = {}


def kernel(**inputs):
    S = int(np.asarray(inputs["x"]).shape[1])
    if S not in _NC_CACHE:
        _NC_CACHE[S] = build(S)
    nc = _NC_CACHE[S]
    in_maps, own_idx = host_prep(inputs, S)
    res = run_bass_kernel_spmd(nc, in_maps, core_ids=list(range(NCORES)))
    B = np.asarray(inputs["x"]).shape[0]
    out = np.empty((B, S, D), np.float32)
    for core in range(NCORES):
        b, own_tok = own_idx[core]
        out[b, own_tok, :] = np.asarray(res.results[core]["outT"]).T
    return out
```

```python
import math
import numpy as np
import concourse.bass as bass
import concourse.mybir as mybir
from concourse.bass_utils import run_bass_kernel_spmd

F32 = mybir.dt.float32
BF16 = mybir.dt.bfloat16
I32 = mybir.dt.int32
AF = mybir.ActivationFunctionType
ALU = mybir.AluOpType
AX = mybir.AxisListType

D = 2048
KD = 16
DFF = 5632
NCORES = 8
EPS = 1e-6
LAMBDA_INIT = 0.8 - 0.6 * math.exp(-0.3 * 0)
SAME_ENGINE_WAITS = True
ATTN_PIPE = True
PIPE_MLA = True
PIPE_DIFF = True
PIPE_ORDER = 1


class V:
    __slots__ = ("key", "ap")

    def __init__(self, key, ap):
        self.key = key
        self.ap = ap

    def __getitem__(self, idx):
        return V(self.key, self.ap[idx])

    def k(self, key):
        return V(key, self.ap)

    def re(self, s, **kw):
        return V(self.key, self.ap.rearrange(s, **kw))


class Op:
    __slots__ = ("eng", "fn", "deps", "dma", "semkey", "idx", "ticket", "need_inc", "semval")

    def __init__(self, eng, fn, dma, semkey):
        self.eng = eng
        self.fn = fn
        self.deps = set()
        self.dma = dma
        self.semkey = semkey
        self.ticket = None
        self.need_inc = False
        self.semval = None


class Prog:
    ENGS = ("pe", "act", "dve", "pool", "sp")

    def __init__(self):
        self.ops = []
        self.state = {}
        self.eng_ops = {e: [] for e in self.ENGS}
        self.bar_deps = set()
        self.bar_seen = {e: True for e in self.ENGS}
        self.dma_since_bar = []
        self.out_dmas = []

    def op(self, eng, fn, reads=(), writes=(), dma=False, semkey=None):
        o = Op(eng, fn, dma, semkey)
        o.idx = len(self.ops)
        for v in reads:
            st = self.state.get(v.key)
            if st is not None:
                o.deps.update(st["w"])
        for v in writes:
            st = self.state.setdefault(v.key, {"w": [], "r": [], "pend": []})
            if st["r"]:
                st["pend"] = st["r"] + st["w"]
                st["w"] = []
                st["r"] = []
            o.deps.update(st["pend"])
        for v in reads:
            st = self.state.setdefault(v.key, {"w": [], "r": [], "pend": []})
            st["r"].append(o.idx)
        for v in writes:
            self.state[v.key]["w"].append(o.idx)
        if not self.bar_seen[eng]:
            o.deps.update(self.bar_deps)
            self.bar_seen[eng] = True
        o.deps.discard(o.idx)
        self.ops.append(o)
        self.eng_ops[eng].append(o)
        if dma:
            self.dma_since_bar.append(o.idx)
        return o

    def barrier(self):
        deps = set(self.dma_since_bar)
        for e in self.ENGS:
            if self.eng_ops[e]:
                deps.add(self.eng_ops[e][-1].idx)
        self.bar_deps = deps
        self.bar_seen = {e: False for e in self.ENGS}
        self.dma_since_bar = []
        self.state = {}

    def emit(self, nc, stack):
        ops = self.ops
        waits = {}
        for e in self.ENGS:
            waited = {}
            for o in self.eng_ops[e]:
                best = {}
                for d in o.deps:
                    p = ops[d]
                    if p.dma:
                        key = ("dma", p.semkey)
                    else:
                        if p.eng == e and (e == "pe" or not SAME_ENGINE_WAITS):
                            continue
                        key = ("eng", p.eng)
                    if key not in best or best[key] < d:
                        best[key] = d
                lst = []
                for key, d in best.items():
                    if waited.get(key, -1) >= d:
                        continue
                    waited[key] = d
                    lst.append(d)
                    if not ops[d].dma:
                        ops[d].need_inc = True
                waits[o.idx] = lst
        cnt = {e: 0 for e in self.ENGS}
        dcnt = {}
        for o in ops:
            if o.dma:
                dcnt[o.semkey] = dcnt.get(o.semkey, 0) + 16
                o.semval = dcnt[o.semkey]
            elif o.need_inc:
                cnt[o.eng] += 1
                o.ticket = cnt[o.eng]
        self.waits = waits
        esem = {e: stack.enter_context(nc.semaphore("s_" + e)) for e in self.ENGS}
        dsem = {}
        for i, k in enumerate(sorted(dcnt.keys())):
            dsem[k] = stack.enter_context(nc.semaphore("d%d" % i))
        self.nsem = len(esem) + len(dsem)
        block = stack.enter_context(nc.Block())

        def run(e, h):
            for o in self.eng_ops[e]:
                for d in waits[o.idx]:
                    p = ops[d]
                    if p.dma:
                        h.wait_ge(dsem[p.semkey], p.semval)
                    else:
                        h.wait_ge(esem[p.eng], p.ticket)
                if o.fn is None:
                    continue
                ins = o.fn(h)
                if o.dma:
                    ins.then_inc(dsem[o.semkey], 16)
                elif o.need_inc:
                    ins.then_inc(esem[o.eng], 1)

        @block.tensor
        def _(h):
            run("pe", h)

        @block.scalar
        def _(h):
            run("act", h)

        @block.vector
        def _(h):
            run("dve", h)

        @block.gpsimd
        def _(h):
            run("pool", h)

        @block.sync
        def _(h):
            run("sp", h)


def simulate(P):
    ops = P.ops
    pos = {e: 0 for e in P.ENGS}
    esem = {e: 0 for e in P.ENGS}
    dsem = {}
    done = [False] * len(ops)
    progress = True
    nsteps = 0
    while progress:
        progress = False
        for e in P.ENGS:
            q = P.eng_ops[e]
            while pos[e] < len(q):
                o = q[pos[e]]
                ok = True
                for d in P.waits[o.idx]:
                    p = ops[d]
                    if p.dma:
                        if dsem.get(p.semkey, 0) < p.semval:
                            ok = False
                    else:
                        if esem[p.eng] < p.ticket:
                            ok = False
                if not ok:
                    break
                for d in o.deps:
                    if not done[d] and not (ops[d].eng == e and e == "pe"):
                        print("SEMANTIC VIOLATION: op", o.idx, e, "runs before dep", d, ops[d].eng)
                        return False
                done[o.idx] = True
                if o.dma:
                    dsem[o.semkey] = dsem.get(o.semkey, 0) + 16
                elif o.need_inc:
                    esem[o.eng] += 1
                pos[e] += 1
                progress = True
    stuck = {e: (pos[e], len(P.eng_ops[e])) for e in P.ENGS}
    print("sim end", stuck)
    return all(pos[e] == len(P.eng_ops[e]) for e in P.ENGS)


class Rot:
    def __init__(self, items):
        self.items = items
        self.i = 0

    def next(self):
        x = self.items[self.i % len(self.items)]
        self.i += 1
        return x


def par_layout(NH):
    off = {}
    c = 0
    for name, n in [("b_ada", 96), ("g1", 16), ("g2", 16), ("b_gate", 32), ("conv_w", 3 * 88), ("conv_b", 88),
                    ("g_q_lat", 4), ("g_kv_lat", 2), ("gq_nope", 1), ("gq_pe", 1), ("gk_nope", 1), ("gk_pe", 1),
                    ("gq_diff", 1), ("gk_diff", 1), ("g_sub", 1), ("invfreq", 1), ("lq1", 64), ("lk1", 64),
                    ("lq2", 64), ("lk2", 64), ("c", 16), ("keep", 1), ("hvalid", NH)]:
        off[name] = (c, n)
        c += n
    return off, c


def build(S, debug=False):
    from contextlib import ExitStack
    NO = S // 2
    NOB = NO // 128
    NH = 2 * NOB
    NQ = NO + NH
    NTOK = 2 * NO + NH
    NKB = 2 * NOB
    POFF, NPAR = par_layout(NH)

    nc = bass.Bass("TRN2", target_bir_lowering=False)
    P = Prog()

    def din(name, shape, dt=F32):
        return nc.dram_tensor(name, list(shape), dt, kind="ExternalInput").ap()

    def dscr(name, shape, dt):
        return nc.dram_tensor(name, list(shape), dt, kind="Internal").ap()

    xT = din("xT", [D, NTOK])
    params = din("params", [128, NPAR])
    posrep = din("posrep", [128, NTOK], I32)
    consts = din("consts", [128, 4 * 128 + NKB * NH])
    w_ada = din("w_ada", [D, 6 * D])
    w_q = din("w_q", [D, 5632])
    w_kv = din("w_kv", [D, 2432])
    w_q_up = din("w_q_up", [512, 1536])
    w_kv_up = din("w_kv_up", [256, 2048])
    w_o_mla = din("w_o_mla", [1024, D])
    w_o_diff = din("w_o_diff", [1024, D])
    w_out = din("w_out", [D, D])
    w_up = din("w_up", [D, 2 * DFF])
    w_down = din("w_down", [DFF, D])
    outT = nc.dram_tensor("outT", [D, NO], F32, kind="ExternalOutput").ap()

    QLN = dscr("QLN", [4, 128, NQ], BF16)
    KVN = dscr("KVN", [2, 128, S], BF16)
    KPE = dscr("KPE", [128, S], BF16)
    QN = dscr("QN", [8, 128, NQ], BF16)
    QPE = dscr("QPE", [4, 128, NQ], BF16)
    KN = dscr("KN", [8, 128, S], BF16)
    VM = dscr("VM", [S, 1024], BF16)
    DQ = dscr("DQ", [8, 128, NQ], BF16)
    DK = dscr("DK", [8, 128, S], BF16)
    DV = dscr("DV", [S, 1024], BF16)
    G = dscr("G", [32, 128, NQ], BF16)
    OM = dscr("OM", [8, 128, NQ], BF16)
    OD = dscr("OD", [8, 128, NQ], BF16)
    X1 = dscr("X1", [16, 128, NO], F32)
    H2 = dscr("H2", [16, 128, NQ], BF16)
    Z = dscr("Z", [44, 128, NO], BF16)

    stack = ExitStack()
    with stack:
        PERS_F = 1024 + NPAR + 64
        ARENA_W = 48000
        pers = stack.enter_context(nc.sbuf_tensor("pers", [128, PERS_F], F32))
        cbf = stack.enter_context(nc.sbuf_tensor("cbf", [128, 4 * 128 + NKB * NH], BF16))
        arena = stack.enter_context(nc.sbuf_tensor("arena", [128, ARENA_W], F32))
        arena_b = arena.bitcast(BF16)
        arena_i = arena.bitcast(I32)
        psb = [stack.enter_context(nc.psum_tensor("ps%d" % i, [128, 512], F32)) for i in range(8)]
        PS = [V("ps%d" % i, psb[i][:]) for i in range(8)]

        class Arena:
            def __init__(self):
                self.off = 0

            def reset(self):
                self.off = 0

            def f32(self, key, shape):
                n = int(np.prod(shape))
                v = arena[:, self.off:self.off + n]
                self.off += n
                assert self.off <= ARENA_W, (key, self.off)
                return V(key, _shape(v, shape))

            def bf(self, key, shape):
                n = int(np.prod(shape))
                nw = (n + 1) // 2
                v = arena_b[:, 2 * self.off:2 * self.off + n]
                self.off += nw
                assert self.off <= ARENA_W, (key, self.off)
                return V(key, _shape(v, shape))

            def i32(self, key, shape):
                n = int(np.prod(shape))
                v = arena_i[:, self.off:self.off + n]
                self.off += n
                assert self.off <= ARENA_W, (key, self.off)
                return V(key, _shape(v, shape))

        def _shape(ap, shape):
            if len(shape) == 1:
                return ap
            if len(shape) == 2:
                return ap.rearrange("p (a b) -> p a b", a=shape[0])
            if len(shape) == 3:
                return ap.rearrange("p (a b c) -> p a b c", a=shape[0], b=shape[1])
            raise ValueError

        A = Arena()

        par = V("par", pers[:, 0:NPAR])
        ppos = NPAR

        def pcol(name, j=0, n=1):
            o, _ = POFF[name]
            return pers[:, o + j:o + j + n]

        def persf(key, n):
            nonlocal ppos
            v = V(key, pers[:, ppos:ppos + n])
            ppos += n
            assert ppos <= PERS_F
            return v

        modT = persf("modT", 96)
        a1 = persf("a1", 16)
        a2 = persf("a2", 16)
        lamv = persf("lamv", 8)
        gsub8 = persf("gsub8", 1)
        ones_b = V("cbf", cbf[:, 0:128])
        bones_b = V("cbf", cbf[:, 128:256])
        perm_b = V("cbf", cbf[:, 256:384])
        causal_b = V("cbf", cbf[:, 384:512])
        halo_b = V("cbf", cbf[:, 512:512 + NKB * NH])

        def dma(eng, out, in_, semkey, store=False):
            if store:
                eng = "act"
            return P.op(eng, lambda h, o=out.ap, i=in_.ap: h.dma_start(out=o, in_=i),
                        reads=[in_], writes=[out], dma=True, semkey=semkey)

        def mm(out, lhsT, rhs, start, stop):
            return P.op("pe", lambda h, o=out.ap, l=lhsT.ap, r=rhs.ap: h.matmul(o, lhsT=l, rhs=r, start=start, stop=stop),
                        reads=[lhsT, rhs], writes=[out])

        def act(out, in_, func, bias=None, scale=None, extra_reads=()):
            kw = {}
            if bias is not None:
                kw["bias"] = bias
            if scale is not None:
                kw["scale"] = scale
            return P.op("act", lambda h, o=out.ap, i=in_.ap: h.activation(out=o, in_=i, func=func, **kw),
                        reads=[in_] + list(extra_reads), writes=[out])

        def tt(eng, out, in0, in1, op):
            return P.op(eng, lambda h, o=out.ap, a=in0.ap, b=in1.ap: h.tensor_tensor(out=o, in0=a, in1=b, op=op),
                        reads=[in0, in1], writes=[out])

        def ts(eng, out, in0, s1, s2, op0, op1=None, extra_reads=()):
            def f(h, o=out.ap, a=in0.ap):
                if op1 is None:
                    return h.tensor_scalar(out=o, in0=a, scalar1=s1, scalar2=None, op0=op0)
                return h.tensor_scalar(out=o, in0=a, scalar1=s1, scalar2=s2, op0=op0, op1=op1)
            return P.op(eng, f, reads=[in0] + list(extra_reads), writes=[out])

        def stt(eng, out, in0, scalar, in1, op0, op1, extra_reads=()):
            return P.op(eng, lambda h, o=out.ap, a=in0.ap, b=in1.ap: h.scalar_tensor_tensor(
                out=o, in0=a, scalar=scalar, in1=b, op0=op0, op1=op1),
                reads=[in0, in1] + list(extra_reads), writes=[out])

        def cp(eng, out, in_):
            return P.op(eng, lambda h, o=out.ap, i=in_.ap: h.tensor_copy(out=o, in_=i), reads=[in_], writes=[out])

        def rstd_from(ps_view, out, dim):
            ts("dve", out, ps_view, 1.0 / dim, EPS, ALU.mult, ALU.add)
            act(out, out, AF.Ln)
            act(out, out, AF.Exp, scale=-0.5)

        cosT = A.f32("cosT", [NTOK])
        sinT = A.f32("sinT", [NTOK])
        B1_MARK = A.off
        dma("sp", par, V("params", params[:, :]), "par")
        cst_f = A.f32("cst_f", [4 * 128 + NKB * NH])
        dma("sp", cst_f, V("consts", consts[:, :]), "cst_f")
        cp("dve", V("cbf", cbf[:]), cst_f)
        csil = A.bf("csil", [16])
        act(csil, V("par", pcol("c", 0, 16)), AF.Silu)
        wa_slots = Rot([A.bf("wa%d" % i, [16, 512]) for i in range(2)])
        psA = PS[0]
        for g in range(24):
            wt = wa_slots.next()
            dma("pool", wt, V("w_ada", w_ada.rearrange("(k p) e -> p k e", p=128)[:, :, g * 512:(g + 1) * 512]), wt.key)
            for j in range(4):
                col = g * 4 + j
                for k in range(KD):
                    mm(psA[:, col:col + 1], wt[:, k, j * 128:(j + 1) * 128], csil[:, k:k + 1], k == 0, k == KD - 1)
        tt("dve", modT, psA[:, 0:96], V("par", pcol("b_ada", 0, 96)), ALU.add)
        stt("dve", a1, modT[:, 16:32], 1.0, V("par", pcol("g1", 0, 16)), ALU.add, ALU.mult)
        stt("dve", a2, modT[:, 64:80], 1.0, V("par", pcol("g2", 0, 16)), ALU.add, ALU.mult)
        shift1 = lambda k: modT.ap[:, k:k + 1]
        gate1 = lambda k: modT.ap[:, 32 + k:33 + k]
        shift2 = lambda k: modT.ap[:, 48 + k:49 + k]
        gate2 = lambda k: modT.ap[:, 80 + k:81 + k]
        ltmp = A.f32("ltmp", [64])
        for i, (qn_, kn_) in enumerate((("lq1", "lk1"), ("lq2", "lk2"))):
            tt("dve", ltmp, V("par", pcol(qn_, 0, 64)), V("par", pcol(kn_, 0, 64)), ALU.mult)
            P.op("dve", lambda h, o=lamv.ap[:, i:i + 1], a=ltmp.ap: h.reduce_sum(out=o, in_=a, axis=AX.X),
                 reads=[ltmp], writes=[lamv])
        act(lamv[:, 2:4], lamv[:, 0:2], AF.Exp)
        stt("dve", lamv[:, 4:5], lamv[:, 3:4], -LAMBDA_INIT, lamv[:, 2:3], ALU.add, ALU.subtract)
        ts("dve", gsub8, V("par", pcol("g_sub")), 1.0 - LAMBDA_INIT, None, ALU.mult)
        neglam = lamv.ap[:, 4:5]
        posi = A.i32("posi", [NTOK])
        angT = A.f32("angT", [NTOK])
        dma("sp", posi, V("posrep", posrep[:, :]), "posi")
        cp("dve", angT, posi)
        ts("dve", angT, angT, pcol("invfreq"), None, ALU.mult, extra_reads=[par])
        PI = math.pi
        ki = A.i32("ki", [NTOK])
        kf = A.f32("kf", [NTOK])
        mk = A.f32("mk", [NTOK])
        for tab, shift in ((sinT, 0.0), (cosT, 0.25)):
            ts("dve", tab, angT, 1.0 / (2.0 * PI), shift, ALU.mult, ALU.add)
            cp("dve", ki, tab)
            cp("dve", kf, ki)
            tt("dve", tab, tab, kf, ALU.subtract)
            ts("dve", mk, tab, 0.5, None, ALU.is_gt)
            tt("dve", tab, tab, mk, ALU.subtract)
            ts("dve", mk, tab, -0.5, None, ALU.is_lt)
            tt("dve", tab, tab, mk, ALU.add)
            act(tab, tab, AF.Sin, scale=6.28318)
        P.barrier()

        A.off = B1_MARK

        def tiles_of(start, count):
            return [(start + i, min(512, count - i)) for i in range(0, count, 512)]
        own_tiles = [(t0, n, t0, t0) for (t0, n) in tiles_of(0, NO)]
        halo_tile = (NO, NH, NO, None)
        oth_tiles = [(t0, n, None, t0 - NH) for (t0, n) in tiles_of(NQ, NO)]
        groups = []
        for i in range(0, len(own_tiles), 2):
            g = own_tiles[i:i + 2]
            if i + 2 >= len(own_tiles):
                g = g + [halo_tile]
            groups.append(g)
        for i in range(0, len(oth_tiles), 2):
            groups.append(oth_tiles[i:i + 2])
        GMAX = 1024 + NH

        xt = A.f32("xt", [16, 512])
        hT = A.bf("hT", [16, GMAX])
        wslots = Rot([A.bf("w%d" % i, [16, 512]) for i in range(2)])
        sqb = Rot([A.bf("sqb%d" % i, [512]) for i in range(8)])
        tmpf = Rot([A.f32("tmpf%d" % i, [512]) for i in range(6)])
        ybf = Rot([A.bf("ybf%d" % i, [512]) for i in range(4)])
        yf = Rot([A.f32("yf%d" % i, [512]) for i in range(8)])
        rstds = Rot([A.f32("rstd%d" % i, [512]) for i in range(3)])
        outb = Rot([A.bf("outb%d" % i, [512]) for i in range(4)])
        stg4 = Rot([A.bf("stg4_%d" % i, [4, 512]) for i in range(2)])
        PSM = Rot([PS[0], PS[1], PS[2]])
        PSX = PS[3]
        PSN = Rot([PS[4], PS[5]])
        PSR = Rot([PS[6], PS[7]])

        pn_pending = []
        mid_pending = []

        def flush_mid():
            while mid_pending:
                mid_pending.pop(0)()

        def pn_flush():
            flush_mid()
            while pn_pending:
                pn_pending.pop(0)()

        def post_norm(ps_list, n, gains, ones_v, dim, rope_tl, dests):
            flush_mid()
            psn = PSN.next()
            nj = len(ps_list)
            ys = []
            sqs = []
            for j, ps in enumerate(ps_list):
                if rope_tl is None:
                    y = yf.next()
                else:
                    y = ybf.next()
                act(y[:, :n], ps[:, :n], AF.Identity, scale=gains[j], extra_reads=[par])
                sq = sqb.next()
                act(sq[:, :n], ps[:, :n], AF.Square)
                ys.append(y)
                sqs.append(sq)
            psrs = [PSR.next() for _ in ys] if rope_tl is not None else []

            def mid():
                for j in range(nj):
                    mm(psn[:, :n], ones_v, sqs[j][:, :n], j == 0, j == nj - 1)
                for j in range(len(psrs)):
                    mm(psrs[j][:, :n], perm_b, ys[j][:, :n], True, True)
            mid_pending.append(mid)

            def tail():
                rs = rstds.next()
                rstd_from(psn[:, :n], rs[:, :n], dim)
                for j, y in enumerate(ys):
                    ob = outb.next()
                    if rope_tl is None:
                        tt("dve", ob[:, :n], y[:, :n], rs[:, :n], ALU.mult)
                    else:
                        psr = psrs[j]
                        t1 = tmpf.next()
                        tt("dve", t1[:, :n], y[:, :n], cosT[:, rope_tl:rope_tl + n], ALU.mult)
                        t2 = tmpf.next()
                        tt("dve", t2[:, :n], psr[:, :n], sinT[:, rope_tl:rope_tl + n], ALU.mult)
                        tt("dve", t1[:, :n], t1[:, :n], t2[:, :n], ALU.add)
                        tt("dve", ob[:, :n], t1[:, :n], rs[:, :n], ALU.mult)
                    dma("sp", dests[j], ob[:, :n], ob.key, store=True)
            prev = list(pn_pending)
            del pn_pending[:]
            for f in prev:
                f()
            pn_pending.append(tail)

        def build_h(tile, goff, a_v, shift_fn, src_fn):
            tl0, n = tile[0], tile[1]
            src_fn(xt, tl0, n)
            for k in range(KD):
                sq = sqb.next()
                act(sq[:, :n], xt[:, k, :n], AF.Square)
                mm(PSX[:, :n], ones_b, sq[:, :n], k == 0, k == KD - 1)
            rs = rstds.next()
            rstd_from(PSX[:, :n], rs[:, :n], D)
            for k in range(KD):
                t = tmpf.next()
                stt("dve", t[:, :n], xt[:, k, :n], a_v.ap[:, k:k + 1], rs[:, :n], ALU.mult, ALU.mult, extra_reads=[a_v])
                act(V(hT.key + str(goff), hT.ap[:, k, goff:goff + n]), t[:, :n], AF.Identity, bias=shift_fn(k),
                    extra_reads=[modT])

        def load_x(dst, tl0, n):
            dma("sp", V(dst.key, dst.ap[:, :, :n]),
                V("xT", xT.rearrange("(k p) t -> p k t", p=128)[:, :, tl0:tl0 + n]), dst.key)

        def load_w(src, c0, ncols, kd=KD):
            wt = wslots.next()
            dma("pool", V(wt.key, wt.ap[:, :kd, :ncols]),
                V("w", src.rearrange("(k p) e -> p k e", p=128)[:, :, c0:c0 + ncols]), wt.key)
            return wt

        def proj_fm(wt, wc0, tile_off, n, goff_key):
            ps = PSM.next()
            for k in range(KD):
                mm(ps[:, :n], wt[:, k, wc0:wc0 + 128], V(hT.key + str(goff_key), hT.ap[:, k, tile_off:tile_off + n]),
                   k == 0, k == KD - 1)
            flush_mid()
            return ps

        gq_lat = [pcol("g_q_lat", j) for j in range(4)]
        gkv_lat = [pcol("g_kv_lat", j) for j in range(2)]

        for grp in groups:
            pn_flush()
            goffs = []
            go = 0
            for tile in grp:
                build_h(tile, go, a1, shift1, load_x)
                goffs.append(go)
                go += tile[1]
            has_q = grp[0][2] is not None
            if has_q:
                wt = load_w(w_q, 0, 512)
                for tile, go in zip(grp, goffs):
                    tl0, n, q0, _ = tile
                    pss = [proj_fm(wt, j * 128, go, n, go) for j in range(3)]
                    ps4 = PSR.next()
                    for k in range(KD):
                        mm(ps4[:, :n], wt[:, k, 384:512], V(hT.key + str(go), hT.ap[:, k, go:go + n]), k == 0, k == KD - 1)
                    flush_mid()
                    pss.append(ps4)
                    post_norm(pss, n, gq_lat, ones_b, 512, None,
                              [V("QLN", QLN[j, :, q0:q0 + n]) for j in range(4)])
                for cg in range(2):
                    wt = load_w(w_q, 512 + cg * 512, 512)
                    for tile, go in zip(grp, goffs):
                        tl0, n, q0, _ = tile
                        for j in range(4):
                            hd = cg * 4 + j
                            ps = proj_fm(wt, j * 128, go, n, go)
                            post_norm([ps], n, [pcol("gq_diff")], bones_b, 64, tl0, [V("DQ", DQ[hd, :, q0:q0 + n])])
                for cg in range(8):
                    wt = load_w(w_q, 1536 + cg * 512, 512)
                    for tile, go in zip(grp, goffs):
                        tl0, n, q0, _ = tile
                        st4 = stg4.next()
                        for j in range(4):
                            ch = cg * 4 + j
                            ps = proj_fm(wt, j * 128, go, n, go)
                            act(st4[:, j, :n], ps[:, :n], AF.Sigmoid, bias=pcol("b_gate", ch), extra_reads=[par])
                        dma("sp", V("G", G[cg * 4:cg * 4 + 4, :, q0:q0 + n].rearrange("c p q -> p c q")),
                            V(st4.key, st4.ap[:, :, :n]), st4.key, store=True)
            kv_tiles = [(tile, go) for tile, go in zip(grp, goffs) if tile[3] is not None]
            wt = load_w(w_kv, 0, 384)
            for tile, go in kv_tiles:
                tl0, n, _, kv0 = tile
                pss = [proj_fm(wt, j * 128, go, n, go) for j in range(2)]
                post_norm(pss, n, gkv_lat, ones_b, 256, None, [V("KVN", KVN[j, :, kv0:kv0 + n]) for j in range(2)])
                ps = proj_fm(wt, 256, go, n, go)
                post_norm([ps], n, [pcol("gk_pe")], bones_b, 64, tl0, [V("KPE", KPE[:, kv0:kv0 + n])])
            for cg in range(2):
                wt = load_w(w_kv, 384 + cg * 512, 512)
                for tile, go in kv_tiles:
                    tl0, n, _, kv0 = tile
                    for j in range(4):
                        hd = cg * 4 + j
                        ps = proj_fm(wt, j * 128, go, n, go)
                        post_norm([ps], n, [pcol("gk_diff")], bones_b, 64, tl0, [V("DK", DK[hd, :, kv0:kv0 + n])])
            for cg in range(2):
                wt = load_w(w_kv, 1408 + cg * 512, 512)
                for tile, go in kv_tiles:
                    tl0, n, _, kv0 = tile
                    for b in range(n // 128):
                        ps = PSM.next()
                        for k in range(KD):
                            mm(ps, V(hT.key + str(go), hT.ap[:, k, go + b * 128:go + (b + 1) * 128]), wt[:, k, :],
                               k == 0, k == KD - 1)
                        ob = outb.next()
                        act(ob, ps, AF.Identity)
                        dma("sp", V("DV", DV[kv0 + b * 128:kv0 + (b + 1) * 128, cg * 512:(cg + 1) * 512]), ob, ob.key, store=True)
        pn_flush()
        P.barrier()

        A.off = B1_MARK
        qln = A.bf("qln", [4, NQ])
        kvn = A.bf("kvn", [2, S])
        wqu = A.bf("wqu", [4, 1536])
        wkvu = A.bf("wkvu", [2, 2048])
        sqb = Rot([A.bf("sqb%d" % i, [512]) for i in range(8)])
        tmpf = Rot([A.f32("tmpf%d" % i, [512]) for i in range(6)])
        ybf = Rot([A.bf("ybf%d" % i, [512]) for i in range(4)])
        yf = Rot([A.f32("yf%d" % i, [512]) for i in range(8)])
        rstds = Rot([A.f32("rstd%d" % i, [512]) for i in range(3)])
        outb = Rot([A.bf("outb%d" % i, [512]) for i in range(4)])
        dma("sp", qln, V("QLN", QLN.rearrange("c p q -> p c q")), "qln")
        dma("sp", kvn, V("KVN", KVN.rearrange("c p q -> p c q")), "kvn")
        dma("pool", wqu, V("w_q_up", w_q_up.rearrange("(k p) e -> p k e", p=128)), "wqu")
        for hh in range(2):
            dma("pool", V("wkvu", wkvu.ap[:, :, hh * 1024:(hh + 1) * 1024]),
                V("w_kv_up", w_kv_up.rearrange("(k p) e -> p k e", p=128)[:, :, hh * 1024:(hh + 1) * 1024]), "wkvu")
        q_tiles = [(t0, n) for (t0, n) in tiles_of(0, NO)] + [(NO, NH)]
        for (q0, n) in q_tiles:
            for hd in range(8):
                ps = PSM.next()
                for k in range(4):
                    mm(ps[:, :n], wqu[:, k, hd * 128:(hd + 1) * 128], qln[:, k, q0:q0 + n], k == 0, k == 3)
                flush_mid()
                post_norm([ps], n, [pcol("gq_nope")], ones_b, 128, None, [V("QN", QN[hd, :, q0:q0 + n])])
            for j in range(4):
                ps = PSM.next()
                for k in range(4):
                    mm(ps[:, :n], wqu[:, k, 1024 + j * 128:1024 + (j + 1) * 128], qln[:, k, q0:q0 + n], k == 0, k == 3)
                flush_mid()
                post_norm([ps], n, [pcol("gq_pe")], bones_b, 64, q0, [V("QPE", QPE[j, :, q0:q0 + n])])
        for (kv0, n) in tiles_of(0, S):
            for hd in range(8):
                ps = PSM.next()
                for k in range(2):
                    mm(ps[:, :n], wkvu[:, k, hd * 128:(hd + 1) * 128], kvn[:, k, kv0:kv0 + n], k == 0, k == 1)
                flush_mid()
                post_norm([ps], n, [pcol("gk_nope")], ones_b, 128, None, [V("KN", KN[hd, :, kv0:kv0 + n])])
            for b in range(n // 128):
                for cg in range(2):
                    ps = PSM.next()
                    for k in range(2):
                        mm(ps, kvn[:, k, kv0 + b * 128:kv0 + (b + 1) * 128], wkvu[:, k, 1024 + cg * 512:1024 + (cg + 1) * 512],
                           k == 0, k == 1)
                    ob = outb.next()
                    act(ob, ps, AF.Identity)
                    dma("sp", V("VM", VM[kv0 + b * 128:kv0 + (b + 1) * 128, cg * 512:(cg + 1) * 512]), ob, ob.key, store=True)
        pn_flush()
        P.barrier()

        A.reset()
        kpeA = A.bf("kpeA", [S])
        kpeB = A.bf("kpeB", [S])
        P.op("dve", lambda h, o=kpeA.ap[64:128, :]: h.memset(o, 0.0), writes=[kpeA])
        P.op("dve", lambda h, o=kpeB.ap[0:64, :]: h.memset(o, 0.0), writes=[kpeB])
        dma("sp", V("kpeA", kpeA.ap[0:64, :]), V("KPE", KPE[0:64, :]), "kpeA")
        dma("sp", V("kpeB", kpeB.ap[64:128, :]), V("KPE", KPE[64:128, :]), "kpeB")
        kpes = [kpeA, kpeB]
        Kh = Rot([A.bf("Kh%d" % i, [S]) for i in range(2)])
        KhD = Rot([A.bf("KhD%d" % i, [S]) for i in range(2)])
        KhB = Rot([A.bf("KhB%d" % i, [S]) for i in range(2)])
        for i_ in range(2):
            P.op("dve", lambda h, o=KhD.items[i_].ap[64:128, :]: h.memset(o, 0.0), writes=[KhD.items[i_]])
            P.op("dve", lambda h, o=KhB.items[i_].ap[0:64, :]: h.memset(o, 0.0), writes=[KhB.items[i_]])
        Vh = Rot([A.bf("Vh%d" % i, [NKB, 128]) for i in range(2)])
        Qh = Rot([A.bf("Qh%d" % i, [NQ]) for i in range(2)])
        Qp = Rot([A.bf("Qp%d" % i, [NQ]) for i in range(2)])
        Oh = Rot([A.bf("Oh%d" % i, [NQ]) for i in range(2)])
        Pb = Rot([A.bf("Pb%d" % i, [512]) for i in range(8)])
        recs = Rot([A.f32("rec%d" % i, [128]) for i in range(6)])
        odf = Rot([A.f32("odf%d" % i, [128]) for i in range(6)])
        sqc = Rot([A.bf("sqc%d" % i, [128]) for i in range(3)])

        qblocks = []
        for i in range(NOB):
            kl = [(l, "full") for l in range(i)] + [(NOB + l, "full") for l in range(i)] + [(NOB + i, "keep"), (i, "causal")]
            qblocks.append((i * 128, 128, kl))
        qblocks.append((NO, NH, [(l, "halo") for l in range(NKB)]))

        def exp_group(Sps, qn, grp_list, scale, pb):
            ng = len(grp_list)
            if qn == 128:
                act(pb[:, :ng * 128], Sps[:, :ng * 128], AF.Exp, scale=scale)
            else:
                act(V(pb.key, pb.ap.rearrange("p (j q) -> p j q", q=128)[:, :ng, :qn]),
                    V(Sps.key, Sps.ap.rearrange("p (j q) -> p j q", q=128)[:, :ng, :qn]), AF.Exp, scale=scale)
            for jj, (l, mode) in enumerate(grp_list):
                reg = pb[:, jj * 128:jj * 128 + qn]
                if mode == "keep":
                    ts("dve", reg, reg, pcol("keep"), None, ALU.mult, extra_reads=[par])
                elif mode == "causal":
                    tt("dve", reg, reg, causal_b, ALU.mult)
                elif mode == "halo":
                    tt("dve", reg, reg, halo_b[:, l * NH:(l + 1) * NH], ALU.mult)

        def load_head(Ksrc, Vsrc, Qsrc, hd):
            kh = Kh.next()
            dma("sp", kh, V("K", Ksrc[hd, :, :]), kh.key)
            vh = Vh.next()
            dma("sp", vh, V("Vs", Vsrc.rearrange("(j p) d -> p j d", p=128)[:, :, hd * 128:(hd + 1) * 128]), vh.key)
            qh = Qh.next()
            dma("sp", qh, V("Q", Qsrc[hd, :, :]), qh.key)
            return kh, vh, qh

        SC_MLA = 192.0 ** -0.5
        SC_DIFF = 64.0 ** -0.5
        def attn_driver(heads, emit_S, emit_exp, emit_PV, emit_fin_a, emit_fin_b, head_done, ATTN_PIPE=True):
            items = []
            for hd in heads:
                for bi, (q0, qn, kl) in enumerate(qblocks):
                    nk = len(kl)
                    ng = (nk + 3) // 4
                    for gi in range(ng):
                        items.append({"hd": hd, "bi": bi, "q0": q0, "qn": qn, "kl": kl, "g0": gi * 4,
                                      "gl": kl[gi * 4:gi * 4 + 4], "nk": nk,
                                      "first_of_head": bi == 0 and gi == 0,
                                      "last_of_qb": gi == ng - 1,
                                      "last_of_head": bi == len(qblocks) - 1 and gi == ng - 1})
            pend_b = []
            if ATTN_PIPE:
                emit_S(items[0])
            for k, it in enumerate(items):
                if ATTN_PIPE and PIPE_ORDER == 1:
                    if k + 1 < len(items):
                        emit_S(items[k + 1])
                elif not ATTN_PIPE:
                    emit_S(it)
                emit_exp(it)
                if ATTN_PIPE and PIPE_ORDER == 2:
                    if k + 1 < len(items):
                        emit_S(items[k + 1])
                emit_PV(it)
                while pend_b:
                    pend_b.pop(0)()
                if it["last_of_qb"]:
                    emit_fin_a(it)
                    if emit_fin_b is not None:
                        pend_b.append(lambda it=it: emit_fin_b(it))
                if it["last_of_head"]:
                    while pend_b:
                        pend_b.pop(0)()
                    head_done(it)

        SB = Rot([PS[0], PS[1], PS[4], PS[5]])
        hctx = {}

        def mla_S(it):
            hd = it["hd"]
            if it["first_of_head"]:
                kh, vh, qh = load_head(KN, VM, QN, hd)
                if hd % 2 == 0:
                    qp = Qp.next()
                    dma("sp", qp, V("QPE", QPE[hd // 2, :, :]), qp.key)
                    hctx["qp"] = qp
                hctx[hd] = (kh, vh, qh, hctx["qp"], Oh.next())
            kh, vh, qh, qp, oh = hctx[hd]
            hp = (hd % 2) * 64
            q0, qn = it["q0"], it["qn"]
            Sps = SB.next()
            it["S"] = Sps
            for jj, (l, mode) in enumerate(it["gl"]):
                reg = Sps[:, jj * 128:jj * 128 + qn]
                mm(reg, kh[:, l * 128:(l + 1) * 128], qh[:, q0:q0 + qn], True, False)
                mm(reg, kpes[hd % 2][:, l * 128:(l + 1) * 128], qp[:, q0:q0 + qn], False, True)

        def mla_exp(it):
            pb = Pb.next()
            it["pb"] = pb
            exp_group(it["S"], it["qn"], it["gl"], SC_MLA, pb)

        def mla_acc(it):
            qn = it["qn"]
            bo, bd = (2, 3) if it["bi"] % 2 == 0 else (6, 7)
            return (V("ps%d" % bo, PS[bo].ap[:, 0:qn]), V("ps%d" % bd, PS[bd].ap[:, 0:qn]))

        def mla_PV(it):
            kh, vh, qh, qp, oh = hctx[it["hd"]]
            oacc, dacc = mla_acc(it)
            qn, pb = it["qn"], it["pb"]
            for jj, (l, mode) in enumerate(it["gl"]):
                first = (it["g0"] + jj == 0)
                last = (it["g0"] + jj == it["nk"] - 1)
                mm(oacc, vh[:, l, :], pb[:, jj * 128:jj * 128 + qn], first, last)
                mm(dacc, ones_b, pb[:, jj * 128:jj * 128 + qn], first, last)

        def mla_fin(it):
            kh, vh, qh, qp, oh = hctx[it["hd"]]
            oacc, dacc = mla_acc(it)
            q0, qn = it["q0"], it["qn"]
            rc = recs.next()
            P.op("dve", lambda h, o=rc.ap[:, :qn], i=dacc.ap: h.reciprocal(out=o, in_=i), reads=[dacc], writes=[rc])
            tt("dve", oh[:, q0:q0 + qn], oacc, rc[:, :qn], ALU.mult)

        def mla_done(it):
            oh = hctx[it["hd"]][4]
            dma("sp", V("OM", OM[it["hd"], :, :]), oh, oh.key, store=True)

        attn_driver(range(8), mla_S, mla_exp, mla_PV, mla_fin, None, mla_done, PIPE_MLA)

        S1B = Rot([PS[0], PS[1]])
        S2B = Rot([PS[4], PS[5]])
        dctx = {}

        def df_S(it):
            hd = it["hd"]
            if it["first_of_head"]:
                ka = KhD.next()
                kb = KhB.next()
                dma("sp", V(ka.key, ka.ap[0:64, :]), V("K", DK[hd, 0:64, :]), ka.key)
                dma("sp", V(kb.key, kb.ap[64:128, :]), V("K", DK[hd, 64:128, :]), kb.key)
                vh = Vh.next()
                dma("sp", vh, V("Vs", DV.rearrange("(j p) d -> p j d", p=128)[:, :, hd * 128:(hd + 1) * 128]), vh.key)
                qh = Qh.next()
                dma("sp", qh, V("Q", DQ[hd, :, :]), qh.key)
                dctx[hd] = ((ka, kb), vh, qh, Oh.next())
            (ka, kb), vh, qh, oh = dctx[hd]
            q0, qn = it["q0"], it["qn"]
            S1 = S1B.next()
            S2 = S2B.next()
            it["S1"], it["S2"] = S1, S2
            for jj, (l, mode) in enumerate(it["gl"]):
                mm(S1[:, jj * 128:jj * 128 + qn], ka[:, l * 128:(l + 1) * 128], qh[:, q0:q0 + qn], True, True)
                mm(S2[:, jj * 128:jj * 128 + qn], kb[:, l * 128:(l + 1) * 128], qh[:, q0:q0 + qn], True, True)

        def df_exp(it):
            p1 = Pb.next()
            p2 = Pb.next()
            it["p1"], it["p2"] = p1, p2
            exp_group(it["S1"], it["qn"], it["gl"], SC_DIFF, p1)
            exp_group(it["S2"], it["qn"], it["gl"], SC_DIFF, p2)

        def df_acc(it):
            qn = it["qn"]
            return [V("ps%d" % b_, PS[b_].ap[:, 0:qn]) for b_ in (2, 3, 6, 7)]

        def df_PV(it):
            _, vh, qh, oh = dctx[it["hd"]]
            o1, d1, o2, d2 = df_acc(it)
            qn, p1, p2 = it["qn"], it["p1"], it["p2"]
            for jj, (l, mode) in enumerate(it["gl"]):
                first = (it["g0"] + jj == 0)
                last = (it["g0"] + jj == it["nk"] - 1)
                mm(o1, vh[:, l, :], p1[:, jj * 128:jj * 128 + qn], first, last)
                mm(d1, ones_b, p1[:, jj * 128:jj * 128 + qn], first, last)
                mm(o2, vh[:, l, :], p2[:, jj * 128:jj * 128 + qn], first, last)
                mm(d2, ones_b, p2[:, jj * 128:jj * 128 + qn], first, last)

        def df_fin_a(it):
            o1, d1, o2, d2 = df_acc(it)
            qn = it["qn"]
            r1 = recs.next()
            r2 = recs.next()
            P.op("dve", lambda h, o=r1.ap[:, :qn], i=d1.ap: h.reciprocal(out=o, in_=i), reads=[d1], writes=[r1])
            P.op("dve", lambda h, o=r2.ap[:, :qn], i=d2.ap: h.reciprocal(out=o, in_=i), reads=[d2], writes=[r2])
            ts("dve", r2[:, :qn], r2[:, :qn], neglam, None, ALU.mult, extra_reads=[lamv])
            t1 = odf.next()
            tt("dve", t1[:, :qn], o1, r1[:, :qn], ALU.mult)
            t2 = odf.next()
            tt("dve", t2[:, :qn], o2, r2[:, :qn], ALU.mult)
            tt("dve", t1[:, :qn], t1[:, :qn], t2[:, :qn], ALU.add)
            sq = sqc.next()
            act(sq[:, :qn], t1[:, :qn], AF.Square)
            it["t1"], it["sq"] = t1, sq

        def df_fin_b(it):
            _, vh, qh, oh = dctx[it["hd"]]
            q0, qn = it["q0"], it["qn"]
            t1, sq = it["t1"], it["sq"]
            psn = S1B.items[S1B.i % len(S1B.items)]
            mm(psn[:, :qn], ones_b, sq[:, :qn], True, True)
            rs = recs.next()
            rstd_from(psn[:, :qn], rs[:, :qn], 128)
            stt("dve", oh[:, q0:q0 + qn], t1[:, :qn], gsub8.ap[:, 0:1], rs[:, :qn], ALU.mult, ALU.mult,
                extra_reads=[gsub8])

        def df_done(it):
            oh = dctx[it["hd"]][3]
            dma("sp", V("OD", OD[it["hd"], :, :]), oh, oh.key, store=True)

        attn_driver(range(8), df_S, df_exp, df_PV, df_fin_a, df_fin_b, df_done, PIPE_DIFF)
        pn_flush()
        P.barrier()

        A.reset()
        omt = A.bf("omt", [8, 512])
        odt = A.bf("odt", [8, 512])
        gts = Rot([A.bf("gt%d" % i, [8, 512]) for i in range(2)])
        mixed = A.bf("mixed", [16, 512])
        x1t = A.f32("x1t", [16, 512])
        xcs = Rot([A.f32("xcD%d" % i, [512]) for i in range(3)])
        h2t = A.bf("h2t", [16, 512])
        wos = Rot([A.bf("wo%d" % i, [8, 512]) for i in range(4)])
        wslots = Rot([A.bf("wD%d" % i, [16, 512]) for i in range(2)])
        tmpf = Rot([A.f32("tmpfD%d" % i, [512]) for i in range(6)])
        sqb = Rot([A.bf("sqbD%d" % i, [512]) for i in range(3)])
        rstds = Rot([A.f32("rstdD%d" % i, [512]) for i in range(2)])
        PSMD = Rot([PS[0], PS[1], PS[2], PS[3]])
        PSO = Rot([PS[4], PS[5]])
        PSX = PS[6]
        for (q0, n) in q_tiles:
            dma("sp", V(omt.key, omt.ap[:, :, :n]), V("OM", OM[:, :, q0:q0 + n].rearrange("h p q -> p h q")), omt.key)
            dma("sp", V(odt.key, odt.ap[:, :, :n]), V("OD", OD[:, :, q0:q0 + n].rearrange("h p q -> p h q")), odt.key)
            for eg in range(4):
                wm = wos.next()
                dma("pool", wm, V("w", w_o_mla.rearrange("(k p) e -> p k e", p=128)[:, :, eg * 512:(eg + 1) * 512]), wm.key)
                wd_ = wos.next()
                dma("pool", wd_, V("w", w_o_diff.rearrange("(k p) e -> p k e", p=128)[:, :, eg * 512:(eg + 1) * 512]), wd_.key)
                gt = gts.next()
                for ab in range(2):
                    dma("sp", V(gt.key, gt.ap[:, ab * 4:ab * 4 + 4, :n]),
                        V("G", G[ab * 16 + eg * 4:ab * 16 + eg * 4 + 4, :, q0:q0 + n].rearrange("h p q -> p h q")), gt.key)
                for j in range(4):
                    e = eg * 4 + j
                    psm = PSMD.next()
                    for k in range(8):
                        mm(psm[:, :n], wm[:, k, j * 128:(j + 1) * 128], omt[:, k, :n], k == 0, k == 7)
                    psd = PSMD.next()
                    for k in range(8):
                        mm(psd[:, :n], wd_[:, k, j * 128:(j + 1) * 128], odt[:, k, :n], k == 0, k == 7)
                    t1 = tmpf.next()
                    tt("dve", t1[:, :n], psm[:, :n], gt[:, j, :n], ALU.mult)
                    t2 = tmpf.next()
                    tt("dve", t2[:, :n], psd[:, :n], gt[:, 4 + j, :n], ALU.mult)
                    tt("dve", mixed[:, e, :n], t1[:, :n], t2[:, :n], ALU.add)
            for eg in range(4):
                wt = wslots.next()
                dma("pool", wt, V("w", w_out.rearrange("(k p) e -> p k e", p=128)[:, :, eg * 512:(eg + 1) * 512]), wt.key)
                for j in range(4):
                    e = eg * 4 + j
                    ps = PSO.next()
                    for k in range(KD):
                        mm(ps[:, :n], wt[:, k, j * 128:(j + 1) * 128], mixed[:, k, :n], k == 0, k == KD - 1)
                    xc = xcs.next()
                    dma("sp", V(xc.key, xc.ap[:, :n]), V("xT", xT[e * 128:(e + 1) * 128, q0:q0 + n]), xc.key)
                    stt("dve", x1t[:, e, :n], ps[:, :n], gate1(e), xc[:, :n], ALU.mult, ALU.add, extra_reads=[modT])
            if q0 < NO:
                dma("sp", V("X1", X1[:, :, q0:q0 + n].rearrange("c p q -> p c q")), V(x1t.key, x1t.ap[:, :, :n]), x1t.key, store=True)
            for k in range(KD):
                sq = sqb.next()
                act(sq[:, :n], x1t[:, k, :n], AF.Square)
                mm(PSX[:, :n], ones_b, sq[:, :n], k == 0, k == KD - 1)
            rs = rstds.next()
            rstd_from(PSX[:, :n], rs[:, :n], D)
            for k in range(KD):
                t = tmpf.next()
                stt("dve", t[:, :n], x1t[:, k, :n], a2.ap[:, k:k + 1], rs[:, :n], ALU.mult, ALU.mult, extra_reads=[a2])
                if q0 >= NO:
                    act(t[:, :n], t[:, :n], AF.Identity, bias=shift2(k), extra_reads=[modT])
                    tt("dve", h2t[:, k, :n], t[:, :n], V("par", pcol("hvalid", 0, NH)), ALU.mult)
                else:
                    act(h2t[:, k, :n], t[:, :n], AF.Identity, bias=shift2(k), extra_reads=[modT])
            dma("sp", V("H2", H2[:, :, q0:q0 + n].rearrange("c p q -> p c q")), V(h2t.key, h2t.ap[:, :, :n]), h2t.key, store=True)
        pn_flush()
        P.barrier()

        A.reset()
        h2 = A.bf("h2", [16, NQ])
        dma("sp", h2, V("H2", H2.rearrange("c p q -> p c q")), "h2")
        wv_s = Rot([A.bf("wv%d" % i, [16, 256]) for i in range(2)])
        wg_s = Rot([A.bf("wg%d" % i, [16, 256]) for i in range(2)])
        uext = Rot([A.f32("uext%d" % i, [NOB, 130]) for i in range(4)])
        ycv = Rot([A.f32("ycv%d" % i, [NOB, 128]) for i in range(4)])
        zb = Rot([A.bf("zb%d" % i, [NOB, 128]) for i in range(2)])
        PSE = Rot(PS)
        own_q_tiles = tiles_of(0, NO)

        def conv_chunk(wt, wc0, ch):
            ue = uext.next()
            for (q0, n) in own_q_tiles:
                ps = PSE.next()
                for k in range(KD):
                    mm(ps[:, :n], wt[:, k, wc0:wc0 + 128], h2[:, k, q0:q0 + n], k == 0, k == KD - 1)
                nb = n // 128
                b0 = q0 // 128
                act(V(ue.key, ue.ap[:, b0:b0 + nb, 2:130]), V(ps.key, ps.ap[:, :n].rearrange("p (b t) -> p b t", t=128)),
                    AF.Identity)
            ps = PSE.next()
            for k in range(KD):
                mm(ps[:, :NH], wt[:, k, wc0:wc0 + 128], h2[:, k, NO:NO + NH], k == 0, k == KD - 1)
            cp("dve", V(ue.key, ue.ap[:, :, 0:2]), V(ps.key, ps.ap[:, :NH].rearrange("p (b t) -> p b t", t=2)))
            cw = lambda j: pcol("conv_w", j * 88 + ch)
            y = ycv.next()
            act(y, V(ue.key, ue.ap[:, :, 2:130]), AF.Identity, bias=pcol("conv_b", ch), scale=cw(2), extra_reads=[par])
            stt("dve", y, V(ue.key, ue.ap[:, :, 1:129]), cw(1), y, ALU.mult, ALU.add, extra_reads=[par])
            stt("dve", y, V(ue.key, ue.ap[:, :, 0:128]), cw(0), y, ALU.mult, ALU.add, extra_reads=[par])
            return y

        for cp0 in range(0, 44, 2):
            wv = wv_s.next()
            dma("pool", wv, V("w", w_up.rearrange("(k p) e -> p k e", p=128)[:, :, cp0 * 128:cp0 * 128 + 256]), wv.key)
            wg = wg_s.next()
            dma("pool", wg, V("w", w_up.rearrange("(k p) e -> p k e", p=128)[:, :, DFF + cp0 * 128:DFF + cp0 * 128 + 256]), wg.key)
            for cc in range(2):
                c = cp0 + cc
                yv = conv_chunk(wv, cc * 128, c)
                yg = conv_chunk(wg, cc * 128, 44 + c)
                act(yg, yg, AF.Silu)
                z = zb.next()
                tt("dve", z, yg, yv, ALU.mult)
                dma("sp", V("Z", Z[c, :, :]), V(z.key, z.ap.rearrange("p b t -> p (b t)")), z.key, store=True)
        pn_flush()
        P.barrier()

        A.reset()
        wd_s = Rot([A.bf("wdn%d" % i, [44, 512]) for i in range(2)])
        zt_s = Rot([A.bf("zt%d" % i, [44, 512]) for i in range(2)])
        x1c = Rot([A.f32("x1c%d" % i, [512]) for i in range(2)])
        oc = Rot([A.f32("oc%d" % i, [512]) for i in range(2)])
        PSE = Rot(PS)
        for eg in range(4):
            wd = wd_s.next()
            dma("pool", wd, V("w", w_down.rearrange("(k p) e -> p k e", p=128)[:, :, eg * 512:(eg + 1) * 512]), wd.key)
            for (q0, n) in own_q_tiles:
                zt = zt_s.next()
                dma("sp", V(zt.key, zt.ap[:, :, :n]), V("Z", Z[:, :, q0:q0 + n].rearrange("c p q -> p c q")), zt.key)
                for j in range(4):
                    e = eg * 4 + j
                    ps = PSE.next()
                    for c in range(44):
                        mm(ps[:, :n], wd[:, c, j * 128:(j + 1) * 128], zt[:, c, :n], c == 0, c == 43)
                    xc = x1c.next()
                    dma("sp", V(xc.key, xc.ap[:, :n]), V("X1", X1[e, :, q0:q0 + n]), xc.key)
                    o = oc.next()
                    stt("dve", o[:, :n], ps[:, :n], gate2(e), xc[:, :n], ALU.mult, ALU.add, extra_reads=[modT])
                    od_ = dma("sp", V("outT", outT[e * 128:(e + 1) * 128, q0:q0 + n]), V(o.key, o.ap[:, :n]), o.key, store=True)
                    P.out_dmas.append(od_)
        fin = P.op("sp", None)
        fin.deps.update(o.idx for o in P.out_dmas)
        P.emit(nc, stack)
    build.last_prog = P
    return nc


def host_prep(inp, S):
    NO = S // 2
    NOB = NO // 128
    NH = 2 * NOB
    NKB = 2 * NOB
    POFF, NPAR = par_layout(NH)
    f32 = np.float32
    x = np.asarray(inp["x"], f32)
    pos = np.asarray(inp["positions"], np.int32)
    B = x.shape[0]

    def fm(v):
        v = np.asarray(v, f32).reshape(-1, 128)
        return np.ascontiguousarray(v.T)

    def rep(v):
        v = np.asarray(v, f32).reshape(1, -1)
        return np.repeat(v, 128, axis=0)

    w_in = np.asarray(inp["w_in"][0], f32)
    q_lat, kv_lat, k_pe, dq, dk, dv, gl = np.split(w_in, np.cumsum([512, 256, 64, 1024, 1024, 1024])[:], axis=1)
    w_q = np.ascontiguousarray(np.concatenate([q_lat, dq, gl], axis=1))
    w_kv = np.ascontiguousarray(np.concatenate([kv_lat, k_pe, k_pe, dk, dv], axis=1))
    wqu = np.asarray(inp["w_q_up"][0], f32).reshape(512, 8, 192)
    w_q_up = np.ascontiguousarray(np.concatenate([wqu[:, :, :128].reshape(512, 1024), wqu[:, :, 128:].reshape(512, 512)], axis=1))
    wkvu = np.asarray(inp["w_kv_up"][0], f32).reshape(256, 8, 256)
    w_kv_up = np.ascontiguousarray(np.concatenate([wkvu[:, :, :128].reshape(256, 1024), wkvu[:, :, 128:].reshape(256, 1024)], axis=1))

    ones = np.ones((128, 128), f32)
    bones = np.zeros((128, 128), f32)
    bones[:64, :64] = 1
    bones[64:, 64:] = 1
    perm = np.zeros((128, 128), f32)
    for m in range(128):
        if (m % 64) < 32:
            perm[m + 32, m] = -1.0
        else:
            perm[m - 32, m] = 1.0
    causal = (np.arange(128)[None, :] >= np.arange(128)[:, None]).astype(f32)
    invfreq = (10000.0 ** (-(np.arange(0, 64, 2, dtype=f32)) / f32(64))).astype(f32)
    invf128 = np.tile(invfreq, 4).reshape(128, 1)

    gq = np.asarray(inp["g_q_mla"][0], f32)
    gk = np.asarray(inp["g_k_mla"][0], f32)
    shared = {
        "b_ada": fm(inp["b_ada"][0]), "g1": fm(inp["g_norm1"][0]), "g2": fm(inp["g_norm2"][0]),
        "b_gate": fm(inp["b_gate"][0]),
        "conv_w": np.concatenate([fm(inp["conv_w"][0][j]) for j in range(3)], axis=1),
        "conv_b": fm(inp["conv_b"][0]),
        "g_q_lat": fm(inp["g_q_lat"][0]), "g_kv_lat": fm(inp["g_kv_lat"][0]),
        "gq_nope": gq[:128].reshape(128, 1), "gq_pe": np.tile(gq[128:], 2).reshape(128, 1),
        "gk_nope": gk[:128].reshape(128, 1), "gk_pe": np.tile(gk[128:], 2).reshape(128, 1),
        "gq_diff": np.tile(np.asarray(inp["g_q_diff"][0], f32), 2).reshape(128, 1),
        "gk_diff": np.tile(np.asarray(inp["g_k_diff"][0], f32), 2).reshape(128, 1),
        "g_sub": np.asarray(inp["g_sub_diff"][0], f32).reshape(128, 1),
        "invfreq": invf128,
        "lq1": rep(inp["lam_q1"][0]), "lk1": rep(inp["lam_k1"][0]),
        "lq2": rep(inp["lam_q2"][0]), "lk2": rep(inp["lam_k2"][0]),
    }
    big = {
        "w_ada": np.ascontiguousarray(np.asarray(inp["w_ada"][0], f32)),
        "w_q": w_q, "w_kv": w_kv, "w_q_up": w_q_up, "w_kv_up": w_kv_up,
        "w_o_mla": np.ascontiguousarray(np.asarray(inp["w_o_mla"][0], f32)),
        "w_o_diff": np.ascontiguousarray(np.asarray(inp["w_o_diff"][0], f32)),
        "w_out": np.ascontiguousarray(np.asarray(inp["w_out"][0], f32)),
        "w_up": np.ascontiguousarray(np.asarray(inp["w_up"][0], f32)),
        "w_down": np.ascontiguousarray(np.asarray(inp["w_down"][0], f32)),
    }
    in_maps = []
    own_idx = []
    for core in range(NCORES):
        b, c = core // 2, core % 2
        own_blocks = [2 * i + c for i in range(NOB)]
        oth_blocks = [2 * i + (1 - c) for i in range(NOB)]
        own_tok = np.concatenate([np.arange(j * 128, (j + 1) * 128) for j in own_blocks])
        oth_tok = np.concatenate([np.arange(j * 128, (j + 1) * 128) for j in oth_blocks])
        halo_tok = []
        hvalid = []
        for j in own_blocks:
            for d_ in (2, 1):
                t = j * 128 - d_
                if t >= 0:
                    halo_tok.append(t)
                    hvalid.append(1.0)
                else:
                    halo_tok.append(2 - d_)
                    hvalid.append(0.0)
        halo_tok = np.array(halo_tok)
        tl = np.concatenate([own_tok, halo_tok, oth_tok])
        kv_tok = np.concatenate([own_tok, oth_tok])
        xT = np.ascontiguousarray(x[b][tl].T)
        posr = np.ascontiguousarray(np.repeat(pos[b][tl].reshape(1, -1), 128, axis=0)).astype(np.int32)
        hm = (kv_tok.reshape(NKB, 128).T[:, :, None] <= halo_tok[None, None, :]).astype(f32).reshape(128, NKB * NH)
        cst = np.ascontiguousarray(np.concatenate([ones, bones, perm, causal, hm], axis=1))
        par = np.zeros((128, NPAR), f32)
        for name, arr in shared.items():
            o, n = POFF[name]
            par[:, o:o + n] = arr
        o, n = POFF["c"]
        par[:, o:o + n] = fm(np.asarray(inp["c"], f32)[b])
        o, n = POFF["keep"]
        par[:, o] = 1.0 if c == 1 else 0.0
        o, n = POFF["hvalid"]
        par[:, o:o + n] = np.asarray(hvalid, f32)[None, :]
        m = {"xT": xT, "params": par, "posrep": posr, "consts": cst}
        m.update(big)
        in_maps.append(m)
        own_idx.append((b, own_tok))
    return in_maps, own_idx


_NC_CACHE = {}


def kernel(**inputs):
    S = int(np.asarray(inputs["x"]).shape[1])
    if S not in _NC_CACHE:
        _NC_CACHE[S] = build(S)
    nc = _NC_CACHE[S]
    in_maps, own_idx = host_prep(inputs, S)
    res = run_bass_kernel_spmd(nc, in_maps, core_ids=list(range(NCORES)))
    B = np.asarray(inputs["x"]).shape[0]
    out = np.empty((B, S, D), np.float32)
    for core in range(NCORES):
        b, own_tok = own_idx[core]
        out[b, own_tok, :] = np.asarray(res.results[core]["outT"]).T
    return out
```

```python
import math
import numpy as np
import concourse.bass as bass
import concourse.mybir as mybir
from concourse.bass_utils import run_bass_kernel_spmd

F32 = mybir.dt.float32
BF16 = mybir.dt.bfloat16
I32 = mybir.dt.int32
AF = mybir.ActivationFunctionType
ALU = mybir.AluOpType
AX = mybir.AxisListType

D = 2048
KD = 16
DFF = 5632
NCORES = 8
EPS = 1e-6
LAMBDA_INIT = 0.8 - 0.6 * math.exp(-0.3 * 0)
SAME_ENGINE_WAITS = True
ATTN_PIPE = True
PIPE_MLA = True
PIPE_DIFF = True
PIPE_ORDER = 1


class V:
    __slots__ = ("key", "ap")

    def __init__(self, key, ap):
        self.key = key
        self.ap = ap

    def __getitem__(self, idx):
        return V(self.key, self.ap[idx])

    def k(self, key):
        return V(key, self.ap)

    def re(self, s, **kw):
        return V(self.key, self.ap.rearrange(s, **kw))


class Op:
    __slots__ = ("eng", "fn", "deps", "dma", "semkey", "idx", "ticket", "need_inc", "semval")

    def __init__(self, eng, fn, dma, semkey):
        self.eng = eng
        self.fn = fn
        self.deps = set()
        self.dma = dma
        self.semkey = semkey
        self.ticket = None
        self.need_inc = False
        self.semval = None


class Prog:
    ENGS = ("pe", "act", "dve", "pool", "sp")

    def __init__(self):
        self.ops = []
        self.state = {}
        self.eng_ops = {e: [] for e in self.ENGS}
        self.bar_deps = set()
        self.bar_seen = {e: True for e in self.ENGS}
        self.dma_since_bar = []
        self.out_dmas = []

    def op(self, eng, fn, reads=(), writes=(), dma=False, semkey=None):
        o = Op(eng, fn, dma, semkey)
        o.idx = len(self.ops)
        for v in reads:
            st = self.state.get(v.key)
            if st is not None:
                o.deps.update(st["w"])
        for v in writes:
            st = self.state.setdefault(v.key, {"w": [], "r": [], "pend": []})
            if st["r"]:
                st["pend"] = st["r"] + st["w"]
                st["w"] = []
                st["r"] = []
            o.deps.update(st["pend"])
        for v in reads:
            st = self.state.setdefault(v.key, {"w": [], "r": [], "pend": []})
            st["r"].append(o.idx)
        for v in writes:
            self.state[v.key]["w"].append(o.idx)
        if not self.bar_seen[eng]:
            o.deps.update(self.bar_deps)
            self.bar_seen[eng] = True
        o.deps.discard(o.idx)
        self.ops.append(o)
        self.eng_ops[eng].append(o)
        if dma:
            self.dma_since_bar.append(o.idx)
        return o

    def barrier(self):
        deps = set(self.dma_since_bar)
        for e in self.ENGS:
            if self.eng_ops[e]:
                deps.add(self.eng_ops[e][-1].idx)
        self.bar_deps = deps
        self.bar_seen = {e: False for e in self.ENGS}
        self.dma_since_bar = []
        self.state = {}

    def emit(self, nc, stack):
        ops = self.ops
        waits = {}
        for e in self.ENGS:
            waited = {}
            for o in self.eng_ops[e]:
                best = {}
                for d in o.deps:
                    p = ops[d]
                    if p.dma:
                        key = ("dma", p.semkey)
                    else:
                        if p.eng == e and (e == "pe" or not SAME_ENGINE_WAITS):
                            continue
                        key = ("eng", p.eng)
                    if key not in best or best[key] < d:
                        best[key] = d
                lst = []
                for key, d in best.items():
                    if waited.get(key, -1) >= d:
                        continue
                    waited[key] = d
                    lst.append(d)
                    if not ops[d].dma:
                        ops[d].need_inc = True
                waits[o.idx] = lst
        cnt = {e: 0 for e in self.ENGS}
        dcnt = {}
        for o in ops:
            if o.dma:
                dcnt[o.semkey] = dcnt.get(o.semkey, 0) + 16
                o.semval = dcnt[o.semkey]
            elif o.need_inc:
                cnt[o.eng] += 1
                o.ticket = cnt[o.eng]
        self.waits = waits
        esem = {e: stack.enter_context(nc.semaphore("s_" + e)) for e in self.ENGS}
        dsem = {}
        for i, k in enumerate(sorted(dcnt.keys())):
            dsem[k] = stack.enter_context(nc.semaphore("d%d" % i))
        self.nsem = len(esem) + len(dsem)
        block = stack.enter_context(nc.Block())

        def run(e, h):
            for o in self.eng_ops[e]:
                for d in waits[o.idx]:
                    p = ops[d]
                    if p.dma:
                        h.wait_ge(dsem[p.semkey], p.semval)
                    else:
                        h.wait_ge(esem[p.eng], p.ticket)
                if o.fn is None:
                    continue
                ins = o.fn(h)
                if o.dma:
                    ins.then_inc(dsem[o.semkey], 16)
                elif o.need_inc:
                    ins.then_inc(esem[o.eng], 1)

        @block.tensor
        def _(h):
            run("pe", h)

        @block.scalar
        def _(h):
            run("act", h)

        @block.vector
        def _(h):
            run("dve", h)

        @block.gpsimd
        def _(h):
            run("pool", h)

        @block.sync
        def _(h):
            run("sp", h)


def simulate(P):
    ops = P.ops
    pos = {e: 0 for e in P.ENGS}
    esem = {e: 0 for e in P.ENGS}
    dsem = {}
    done = [False] * len(ops)
    progress = True
    nsteps = 0
    while progress:
        progress = False
        for e in P.ENGS:
            q = P.eng_ops[e]
            while pos[e] < len(q):
                o = q[pos[e]]
                ok = True
                for d in P.waits[o.idx]:
                    p = ops[d]
                    if p.dma:
                        if dsem.get(p.semkey, 0) < p.semval:
                            ok = False
                    else:
                        if esem[p.eng] < p.ticket:
                            ok = False
                if not ok:
                    break
                for d in o.deps:
                    if not done[d] and not (ops[d].eng == e and e == "pe"):
                        print("SEMANTIC VIOLATION: op", o.idx, e, "runs before dep", d, ops[d].eng)
                        return False
                done[o.idx] = True
                if o.dma:
                    dsem[o.semkey] = dsem.get(o.semkey, 0) + 16
                elif o.need_inc:
                    esem[o.eng] += 1
                pos[e] += 1
                progress = True
    stuck = {e: (pos[e], len(P.eng_ops[e])) for e in P.ENGS}
    print("sim end", stuck)
    return all(pos[e] == len(P.eng_ops[e]) for e in P.ENGS)


class Rot:
    def __init__(self, items):
        self.items = items
        self.i = 0

    def next(self):
        x = self.items[self.i % len(self.items)]
        self.i += 1
        return x


def par_layout(NH):
    off = {}
    c = 0
    for name, n in [("b_ada", 96), ("g1", 16), ("g2", 16), ("b_gate", 32), ("conv_w", 3 * 88), ("conv_b", 88),
                    ("g_q_lat", 4), ("g_kv_lat", 2), ("gq_nope", 1), ("gq_pe", 1), ("gk_nope", 1), ("gk_pe", 1),
                    ("gq_diff", 1), ("gk_diff", 1), ("g_sub", 1), ("invfreq", 1), ("lq1", 64), ("lk1", 64),
                    ("lq2", 64), ("lk2", 64), ("c", 16), ("keep", 1), ("hvalid", NH)]:
        off[name] = (c, n)
        c += n
    return off, c


def build(S, debug=False):
    from contextlib import ExitStack
    NO = S // 2
    NOB = NO // 128
    NH = 2 * NOB
    NQ = NO + NH
    NTOK = 2 * NO + NH
    NKB = 2 * NOB
    POFF, NPAR = par_layout(NH)

    nc = bass.Bass("TRN2", target_bir_lowering=False)
    P = Prog()

    def din(name, shape, dt=F32):
        return nc.dram_tensor(name, list(shape), dt, kind="ExternalInput").ap()

    def dscr(name, shape, dt):
        return nc.dram_tensor(name, list(shape), dt, kind="Internal").ap()

    xT = din("xT", [D, NTOK])
    params = din("params", [128, NPAR])
    posrep = din("posrep", [128, NTOK], I32)
    consts = din("consts", [128, 4 * 128 + NKB * NH])
    w_ada = din("w_ada", [D, 6 * D])
    w_q = din("w_q", [D, 5632])
    w_kv = din("w_kv", [D, 2432])
    w_q_up = din("w_q_up", [512, 1536])
    w_kv_up = din("w_kv_up", [256, 2048])
    w_o_mla = din("w_o_mla", [1024, D])
    w_o_diff = din("w_o_diff", [1024, D])
    w_out = din("w_out", [D, D])
    w_up = din("w_up", [D, 2 * DFF])
    w_down = din("w_down", [DFF, D])
    outT = nc.dram_tensor("outT", [D, NO], F32, kind="ExternalOutput").ap()

    QLN = dscr("QLN", [4, 128, NQ], BF16)
    KVN = dscr("KVN", [2, 128, S], BF16)
    KPE = dscr("KPE", [128, S], BF16)
    QN = dscr("QN", [8, 128, NQ], BF16)
    QPE = dscr("QPE", [4, 128, NQ], BF16)
    KN = dscr("KN", [8, 128, S], BF16)
    VM = dscr("VM", [S, 1024], BF16)
    DQ = dscr("DQ", [8, 128, NQ], BF16)
    DK = dscr("DK", [8, 128, S], BF16)
    DV = dscr("DV", [S, 1024], BF16)
    G = dscr("G", [32, 128, NQ], BF16)
    OM = dscr("OM", [8, 128, NQ], BF16)
    OD = dscr("OD", [8, 128, NQ], BF16)
    X1 = dscr("X1", [16, 128, NO], F32)
    H2 = dscr("H2", [16, 128, NQ], BF16)
    Z = dscr("Z", [44, 128, NO], BF16)

    stack = ExitStack()
    with stack:
        PERS_F = 1024 + NPAR + 64
        ARENA_W = 48000
        pers = stack.enter_context(nc.sbuf_tensor("pers", [128, PERS_F], F32))
        cbf = stack.enter_context(nc.sbuf_tensor("cbf", [128, 4 * 128 + NKB * NH], BF16))
        arena = stack.enter_context(nc.sbuf_tensor("arena", [128, ARENA_W], F32))
        arena_b = arena.bitcast(BF16)
        arena_i = arena.bitcast(I32)
        psb = [stack.enter_context(nc.psum_tensor("ps%d" % i, [128, 512], F32)) for i in range(8)]
        PS = [V("ps%d" % i, psb[i][:]) for i in range(8)]

        class Arena:
            def __init__(self):
                self.off = 0

            def reset(self):
                self.off = 0

            def f32(self, key, shape):
                n = int(np.prod(shape))
                v = arena[:, self.off:self.off + n]
                self.off += n
                assert self.off <= ARENA_W, (key, self.off)
                return V(key, _shape(v, shape))

            def bf(self, key, shape):
                n = int(np.prod(shape))
                nw = (n + 1) // 2
                v = arena_b[:, 2 * self.off:2 * self.off + n]
                self.off += nw
                assert self.off <= ARENA_W, (key, self.off)
                return V(key, _shape(v, shape))

            def i32(self, key, shape):
                n = int(np.prod(shape))
                v = arena_i[:, self.off:self.off + n]
                self.off += n
                assert self.off <= ARENA_W, (key, self.off)
                return V(key, _shape(v, shape))

        def _shape(ap, shape):
            if len(shape) == 1:
                return ap
            if len(shape) == 2:
                return ap.rearrange("p (a b) -> p a b", a=shape[0])
            if len(shape) == 3:
                return ap.rearrange("p (a b c) -> p a b c", a=shape[0], b=shape[1])
            raise ValueError

        A = Arena()

        par = V("par", pers[:, 0:NPAR])
        ppos = NPAR

        def pcol(name, j=0, n=1):
            o, _ = POFF[name]
            return pers[:, o + j:o + j + n]

        def persf(key, n):
            nonlocal ppos
            v = V(key, pers[:, ppos:ppos + n])
            ppos += n
            assert ppos <= PERS_F
            return v

        modT = persf("modT", 96)
        a1 = persf("a1", 16)
        a2 = persf("a2", 16)
        lamv = persf("lamv", 8)
        gsub8 = persf("gsub8", 1)
        ones_b = V("cbf", cbf[:, 0:128])
        bones_b = V("cbf", cbf[:, 128:256])
        perm_b = V("cbf", cbf[:, 256:384])
        causal_b = V("cbf", cbf[:, 384:512])
        halo_b = V("cbf", cbf[:, 512:512 + NKB * NH])

        def dma(eng, out, in_, semkey, store=False):
            if store:
                eng = "act"
            return P.op(eng, lambda h, o=out.ap, i=in_.ap: h.dma_start(out=o, in_=i),
                        reads=[in_], writes=[out], dma=True, semkey=semkey)

        def mm(out, lhsT, rhs, start, stop):
            return P.op("pe", lambda h, o=out.ap, l=lhsT.ap, r=rhs.ap: h.matmul(o, lhsT=l, rhs=r, start=start, stop=stop),
                        reads=[lhsT, rhs], writes=[out])

        def act(out, in_, func, bias=None, scale=None, extra_reads=()):
            kw = {}
            if bias is not None:
                kw["bias"] = bias
            if scale is not None:
                kw["scale"] = scale
            return P.op("act", lambda h, o=out.ap, i=in_.ap: h.activation(out=o, in_=i, func=func, **kw),
                        reads=[in_] + list(extra_reads), writes=[out])

        def tt(eng, out, in0, in1, op):
            return P.op(eng, lambda h, o=out.ap, a=in0.ap, b=in1.ap: h.tensor_tensor(out=o, in0=a, in1=b, op=op),
                        reads=[in0, in1], writes=[out])

        def ts(eng, out, in0, s1, s2, op0, op1=None, extra_reads=()):
            def f(h, o=out.ap, a=in0.ap):
                if op1 is None:
                    return h.tensor_scalar(out=o, in0=a, scalar1=s1, scalar2=None, op0=op0)
                return h.tensor_scalar(out=o, in0=a, scalar1=s1, scalar2=s2, op0=op0, op1=op1)
            return P.op(eng, f, reads=[in0] + list(extra_reads), writes=[out])

        def stt(eng, out, in0, scalar, in1, op0, op1, extra_reads=()):
            return P.op(eng, lambda h, o=out.ap, a=in0.ap, b=in1.ap: h.scalar_tensor_tensor(
                out=o, in0=a, scalar=scalar, in1=b, op0=op0, op1=op1),
                reads=[in0, in1] + list(extra_reads), writes=[out])

        def cp(eng, out, in_):
            return P.op(eng, lambda h, o=out.ap, i=in_.ap: h.tensor_copy(out=o, in_=i), reads=[in_], writes=[out])

        def rstd_from(ps_view, out, dim):
            ts("dve", out, ps_view, 1.0 / dim, EPS, ALU.mult, ALU.add)
            act(out, out, AF.Ln)
            act(out, out, AF.Exp, scale=-0.5)

        cosT = A.f32("cosT", [NTOK])
        sinT = A.f32("sinT", [NTOK])
        B1_MARK = A.off
        dma("sp", par, V("params", params[:, :]), "par")
        cst_f = A.f32("cst_f", [4 * 128 + NKB * NH])
        dma("sp", cst_f, V("consts", consts[:, :]), "cst_f")
        cp("dve", V("cbf", cbf[:]), cst_f)
        csil = A.bf("csil", [16])
        act(csil, V("par", pcol("c", 0, 16)), AF.Silu)
        wa_slots = Rot([A.bf("wa%d" % i, [16, 512]) for i in range(2)])
        psA = PS[0]
        for g in range(24):
            wt = wa_slots.next()
            dma("pool", wt, V("w_ada", w_ada.rearrange("(k p) e -> p k e", p=128)[:, :, g * 512:(g + 1) * 512]), wt.key)
            for j in range(4):
                col = g * 4 + j
                for k in range(KD):
                    mm(psA[:, col:col + 1], wt[:, k, j * 128:(j + 1) * 128], csil[:, k:k + 1], k == 0, k == KD - 1)
        tt("dve", modT, psA[:, 0:96], V("par", pcol("b_ada", 0, 96)), ALU.add)
        stt("dve", a1, modT[:, 16:32], 1.0, V("par", pcol("g1", 0, 16)), ALU.add, ALU.mult)
        stt("dve", a2, modT[:, 64:80], 1.0, V("par", pcol("g2", 0, 16)), ALU.add, ALU.mult)
        shift1 = lambda k: modT.ap[:, k:k + 1]
        gate1 = lambda k: modT.ap[:, 32 + k:33 + k]
        shift2 = lambda k: modT.ap[:, 48 + k:49 + k]
        gate2 = lambda k: modT.ap[:, 80 + k:81 + k]
        ltmp = A.f32("ltmp", [64])
        for i, (qn_, kn_) in enumerate((("lq1", "lk1"), ("lq2", "lk2"))):
            tt("dve", ltmp, V("par", pcol(qn_, 0, 64)), V("par", pcol(kn_, 0, 64)), ALU.mult)
            P.op("dve", lambda h, o=lamv.ap[:, i:i + 1], a=ltmp.ap: h.reduce_sum(out=o, in_=a, axis=AX.X),
                 reads=[ltmp], writes=[lamv])
        act(lamv[:, 2:4], lamv[:, 0:2], AF.Exp)
        stt("dve", lamv[:, 4:5], lamv[:, 3:4], -LAMBDA_INIT, lamv[:, 2:3], ALU.add, ALU.subtract)
        ts("dve", gsub8, V("par", pcol("g_sub")), 1.0 - LAMBDA_INIT, None, ALU.mult)
        neglam = lamv.ap[:, 4:5]
        posi = A.i32("posi", [NTOK])
        angT = A.f32("angT", [NTOK])
        dma("sp", posi, V("posrep", posrep[:, :]), "posi")
        cp("dve", angT, posi)
        ts("dve", angT, angT, pcol("invfreq"), None, ALU.mult, extra_reads=[par])
        PI = math.pi
        ki = A.i32("ki", [NTOK])
        kf = A.f32("kf", [NTOK])
        mk = A.f32("mk", [NTOK])
        for tab, shift in ((sinT, 0.0), (cosT, 0.25)):
            ts("dve", tab, angT, 1.0 / (2.0 * PI), shift, ALU.mult, ALU.add)
            cp("dve", ki, tab)
            cp("dve", kf, ki)
            tt("dve", tab, tab, kf, ALU.subtract)
            ts("dve", mk, tab, 0.5, None, ALU.is_gt)
            tt("dve", tab, tab, mk, ALU.subtract)
            ts("dve", mk, tab, -0.5, None, ALU.is_lt)
            tt("dve", tab, tab, mk, ALU.add)
            act(tab, tab, AF.Sin, scale=6.28318)
        P.barrier()

        A.off = B1_MARK

        def tiles_of(start, count):
            return [(start + i, min(512, count - i)) for i in range(0, count, 512)]
        own_tiles = [(t0, n, t0, t0) for (t0, n) in tiles_of(0, NO)]
        halo_tile = (NO, NH, NO, None)
        oth_tiles = [(t0, n, None, t0 - NH) for (t0, n) in tiles_of(NQ, NO)]
        groups = []
        for i in range(0, len(own_tiles), 2):
            g = own_tiles[i:i + 2]
            if i + 2 >= len(own_tiles):
                g = g + [halo_tile]
            groups.append(g)
        for i in range(0, len(oth_tiles), 2):
            groups.append(oth_tiles[i:i + 2])
        GMAX = 1024 + NH

        xt = A.f32("xt", [16, 512])
        hT = A.bf("hT", [16, GMAX])
        wslots = Rot([A.bf("w%d" % i, [16, 512]) for i in range(2)])
        sqb = Rot([A.bf("sqb%d" % i, [512]) for i in range(8)])
        tmpf = Rot([A.f32("tmpf%d" % i, [512]) for i in range(6)])
        ybf = Rot([A.bf("ybf%d" % i, [512]) for i in range(4)])
        yf = Rot([A.f32("yf%d" % i, [512]) for i in range(8)])
        rstds = Rot([A.f32("rstd%d" % i, [512]) for i in range(3)])
        outb = Rot([A.bf("outb%d" % i, [512]) for i in range(4)])
        stg4 = Rot([A.bf("stg4_%d" % i, [4, 512]) for i in range(2)])
        PSM = Rot([PS[0], PS[1], PS[2]])
        PSX = PS[3]
        PSN = Rot([PS[4], PS[5]])
        PSR = Rot([PS[6], PS[7]])

        pn_pending = []
        mid_pending = []

        def flush_mid():
            while mid_pending:
                mid_pending.pop(0)()

        def pn_flush():
            flush_mid()
            while pn_pending:
                pn_pending.pop(0)()

        def post_norm(ps_list, n, gains, ones_v, dim, rope_tl, dests):
            flush_mid()
            psn = PSN.next()
            nj = len(ps_list)
            ys = []
            sqs = []
            for j, ps in enumerate(ps_list):
                if rope_tl is None:
                    y = yf.next()
                else:
                    y = ybf.next()
                act(y[:, :n], ps[:, :n], AF.Identity, scale=gains[j], extra_reads=[par])
                sq = sqb.next()
                act(sq[:, :n], ps[:, :n], AF.Square)
                ys.append(y)
                sqs.append(sq)
            psrs = [PSR.next() for _ in ys] if rope_tl is not None else []

            def mid():
                for j in range(nj):
                    mm(psn[:, :n], ones_v, sqs[j][:, :n], j == 0, j == nj - 1)
                for j in range(len(psrs)):
                    mm(psrs[j][:, :n], perm_b, ys[j][:, :n], True, True)
            mid_pending.append(mid)

            def tail():
                rs = rstds.next()
                rstd_from(psn[:, :n], rs[:, :n], dim)
                for j, y in enumerate(ys):
                    ob = outb.next()
                    if rope_tl is None:
                        tt("dve", ob[:, :n], y[:, :n], rs[:, :n], ALU.mult)
                    else:
                        psr = psrs[j]
                        t1 = tmpf.next()
                        tt("dve", t1[:, :n], y[:, :n], cosT[:, rope_tl:rope_tl + n], ALU.mult)
                        t2 = tmpf.next()
                        tt("dve", t2[:, :n], psr[:, :n], sinT[:, rope_tl:rope_tl + n], ALU.mult)
                        tt("dve", t1[:, :n], t1[:, :n], t2[:, :n], ALU.add)
                        tt("dve", ob[:, :n], t1[:, :n], rs[:, :n], ALU.mult)
                    dma("sp", dests[j], ob[:, :n], ob.key, store=True)
            prev = list(pn_pending)
            del pn_pending[:]
            for f in prev:
                f()
            pn_pending.append(tail)

        def build_h(tile, goff, a_v, shift_fn, src_fn):
            tl0, n = tile[0], tile[1]
            src_fn(xt, tl0, n)
            for k in range(KD):
                sq = sqb.next()
                act(sq[:, :n], xt[:, k, :n], AF.Square)
                mm(PSX[:, :n], ones_b, sq[:, :n], k == 0, k == KD - 1)
            rs = rstds.next()
            rstd_from(PSX[:, :n], rs[:, :n], D)
            for k in range(KD):
                t = tmpf.next()
                stt("dve", t[:, :n], xt[:, k, :n], a_v.ap[:, k:k + 1], rs[:, :n], ALU.mult, ALU.mult, extra_reads=[a_v])
                act(V(hT.key + str(goff), hT.ap[:, k, goff:goff + n]), t[:, :n], AF.Identity, bias=shift_fn(k),
                    extra_reads=[modT])

        def load_x(dst, tl0, n):
            dma("sp", V(dst.key, dst.ap[:, :, :n]),
                V("xT", xT.rearrange("(k p) t -> p k t", p=128)[:, :, tl0:tl0 + n]), dst.key)

        def load_w(src, c0, ncols, kd=KD):
            wt = wslots.next()
            dma("pool", V(wt.key, wt.ap[:, :kd, :ncols]),
                V("w", src.rearrange("(k p) e -> p k e", p=128)[:, :, c0:c0 + ncols]), wt.key)
            return wt

        def proj_fm(wt, wc0, tile_off, n, goff_key):
            ps = PSM.next()
            for k in range(KD):
                mm(ps[:, :n], wt[:, k, wc0:wc0 + 128], V(hT.key + str(goff_key), hT.ap[:, k, tile_off:tile_off + n]),
                   k == 0, k == KD - 1)
            flush_mid()
            return ps

        gq_lat = [pcol("g_q_lat", j) for j in range(4)]
        gkv_lat = [pcol("g_kv_lat", j) for j in range(2)]

        for grp in groups:
            pn_flush()
            goffs = []
            go = 0
            for tile in grp:
                build_h(tile, go, a1, shift1, load_x)
                goffs.append(go)
                go += tile[1]
            has_q = grp[0][2] is not None
            if has_q:
                wt = load_w(w_q, 0, 512)
                for tile, go in zip(grp, goffs):
                    tl0, n, q0, _ = tile
                    pss = [proj_fm(wt, j * 128, go, n, go) for j in range(3)]
                    ps4 = PSR.next()
                    for k in range(KD):
                        mm(ps4[:, :n], wt[:, k, 384:512], V(hT.key + str(go), hT.ap[:, k, go:go + n]), k == 0, k == KD - 1)
                    flush_mid()
                    pss.append(ps4)
                    post_norm(pss, n, gq_lat, ones_b, 512, None,
                              [V("QLN", QLN[j, :, q0:q0 + n]) for j in range(4)])
                for cg in range(2):
                    wt = load_w(w_q, 512 + cg * 512, 512)
                    for tile, go in zip(grp, goffs):
                        tl0, n, q0, _ = tile
                        for j in range(4):
                            hd = cg * 4 + j
                            ps = proj_fm(wt, j * 128, go, n, go)
                            post_norm([ps], n, [pcol("gq_diff")], bones_b, 64, tl0, [V("DQ", DQ[hd, :, q0:q0 + n])])
                for cg in range(8):
                    wt = load_w(w_q, 1536 + cg * 512, 512)
                    for tile, go in zip(grp, goffs):
                        tl0, n, q0, _ = tile
                        st4 = stg4.next()
                        for j in range(4):
                            ch = cg * 4 + j
                            ps = proj_fm(wt, j * 128, go, n, go)
                            act(st4[:, j, :n], ps[:, :n], AF.Sigmoid, bias=pcol("b_gate", ch), extra_reads=[par])
                        dma("sp", V("G", G[cg * 4:cg * 4 + 4, :, q0:q0 + n].rearrange("c p q -> p c q")),
                            V(st4.key, st4.ap[:, :, :n]), st4.key, store=True)
            kv_tiles = [(tile, go) for tile, go in zip(grp, goffs) if tile[3] is not None]
            wt = load_w(w_kv, 0, 384)
            for tile, go in kv_tiles:
                tl0, n, _, kv0 = tile
                pss = [proj_fm(wt, j * 128, go, n, go) for j in range(2)]
                post_norm(pss, n, gkv_lat, ones_b, 256, None, [V("KVN", KVN[j, :, kv0:kv0 + n]) for j in range(2)])
                ps = proj_fm(wt, 256, go, n, go)
                post_norm([ps], n, [pcol("gk_pe")], bones_b, 64, tl0, [V("KPE", KPE[:, kv0:kv0 + n])])
            for cg in range(2):
                wt = load_w(w_kv, 384 + cg * 512, 512)
                for tile, go in kv_tiles:
                    tl0, n, _, kv0 = tile
                    for j in range(4):
                        hd = cg * 4 + j
                        ps = proj_fm(wt, j * 128, go, n, go)
                        post_norm([ps], n, [pcol("gk_diff")], bones_b, 64, tl0, [V("DK", DK[hd, :, kv0:kv0 + n])])
            for cg in range(2):
                wt = load_w(w_kv, 1408 + cg * 512, 512)
                for tile, go in kv_tiles:
                    tl0, n, _, kv0 = tile
                    for b in range(n // 128):
                        ps = PSM.next()
                        for k in range(KD):
                            mm(ps, V(hT.key + str(go), hT.ap[:, k, go + b * 128:go + (b + 1) * 128]), wt[:, k, :],
                               k == 0, k == KD - 1)
                        ob = outb.next()
                        act(ob, ps, AF.Identity)
                        dma("sp", V("DV", DV[kv0 + b * 128:kv0 + (b + 1) * 128, cg * 512:(cg + 1) * 512]), ob, ob.key, store=True)
        pn_flush()
        P.barrier()

        A.off = B1_MARK
        qln = A.bf("qln", [4, NQ])
        kvn = A.bf("kvn", [2, S])
        wqu = A.bf("wqu", [4, 1536])
        wkvu = A.bf("wkvu", [2, 2048])
        sqb = Rot([A.bf("sqb%d" % i, [512]) for i in range(8)])
        tmpf = Rot([A.f32("tmpf%d" % i, [512]) for i in range(6)])
        ybf = Rot([A.bf("ybf%d" % i, [512]) for i in range(4)])
        yf = Rot([A.f32("yf%d" % i, [512]) for i in range(8)])
        rstds = Rot([A.f32("rstd%d" % i, [512]) for i in range(3)])
        outb = Rot([A.bf("outb%d" % i, [512]) for i in range(4)])
        dma("sp", qln, V("QLN", QLN.rearrange("c p q -> p c q")), "qln")
        dma("sp", kvn, V("KVN", KVN.rearrange("c p q -> p c q")), "kvn")
        dma("pool", wqu, V("w_q_up", w_q_up.rearrange("(k p) e -> p k e", p=128)), "wqu")
        for hh in range(2):
            dma("pool", V("wkvu", wkvu.ap[:, :, hh * 1024:(hh + 1) * 1024]),
                V("w_kv_up", w_kv_up.rearrange("(k p) e -> p k e", p=128)[:, :, hh * 1024:(hh + 1) * 1024]), "wkvu")
        q_tiles = [(t0, n) for (t0, n) in tiles_of(0, NO)] + [(NO, NH)]
        for (q0, n) in q_tiles:
            for hd in range(8):
                ps = PSM.next()
                for k in range(4):
                    mm(ps[:, :n], wqu[:, k, hd * 128:(hd + 1) * 128], qln[:, k, q0:q0 + n], k == 0, k == 3)
                flush_mid()
                post_norm([ps], n, [pcol("gq_nope")], ones_b, 128, None, [V("QN", QN[hd, :, q0:q0 + n])])
            for j in range(4):
                ps = PSM.next()
                for k in range(4):
                    mm(ps[:, :n], wqu[:, k, 1024 + j * 128:1024 + (j + 1) * 128], qln[:, k, q0:q0 + n], k == 0, k == 3)
                flush_mid()
                post_norm([ps], n, [pcol("gq_pe")], bones_b, 64, q0, [V("QPE", QPE[j, :, q0:q0 + n])])
        for (kv0, n) in tiles_of(0, S):
            for hd in range(8):
                ps = PSM.next()
                for k in range(2):
                    mm(ps[:, :n], wkvu[:, k, hd * 128:(hd + 1) * 128], kvn[:, k, kv0:kv0 + n], k == 0, k == 1)
                flush_mid()
                post_norm([ps], n, [pcol("gk_nope")], ones_b, 128, None, [V("KN", KN[hd, :, kv0:kv0 + n])])
            for b in range(n // 128):
                for cg in range(2):
                    ps = PSM.next()
                    for k in range(2):
                        mm(ps, kvn[:, k, kv0 + b * 128:kv0 + (b + 1) * 128], wkvu[:, k, 1024 + cg * 512:1024 + (cg + 1) * 512],
                           k == 0, k == 1)
                    ob = outb.next()
                    act(ob, ps, AF.Identity)
                    dma("sp", V("VM", VM[kv0 + b * 128:kv0 + (b + 1) * 128, cg * 512:(cg + 1) * 512]), ob, ob.key, store=True)
        pn_flush()
        P.barrier()

        A.reset()
        kpeA = A.bf("kpeA", [S])
        kpeB = A.bf("kpeB", [S])
        P.op("dve", lambda h, o=kpeA.ap[64:128, :]: h.memset(o, 0.0), writes=[kpeA])
        P.op("dve", lambda h, o=kpeB.ap[0:64, :]: h.memset(o, 0.0), writes=[kpeB])
        dma("sp", V("kpeA", kpeA.ap[0:64, :]), V("KPE", KPE[0:64, :]), "kpeA")
        dma("sp", V("kpeB", kpeB.ap[64:128, :]), V("KPE", KPE[64:128, :]), "kpeB")
        kpes = [kpeA, kpeB]
        Kh = Rot([A.bf("Kh%d" % i, [S]) for i in range(2)])
        KhD = Rot([A.bf("KhD%d" % i, [S]) for i in range(2)])
        KhB = Rot([A.bf("KhB%d" % i, [S]) for i in range(2)])
        for i_ in range(2):
            P.op("dve", lambda h, o=KhD.items[i_].ap[64:128, :]: h.memset(o, 0.0), writes=[KhD.items[i_]])
            P.op("dve", lambda h, o=KhB.items[i_].ap[0:64, :]: h.memset(o, 0.0), writes=[KhB.items[i_]])
        Vh = Rot([A.bf("Vh%d" % i, [NKB, 128]) for i in range(2)])
        Qh = Rot([A.bf("Qh%d" % i, [NQ]) for i in range(2)])
        Qp = Rot([A.bf("Qp%d" % i, [NQ]) for i in range(2)])
        Oh = Rot([A.bf("Oh%d" % i, [NQ]) for i in range(2)])
        Pb = Rot([A.bf("Pb%d" % i, [512]) for i in range(8)])
        recs = Rot([A.f32("rec%d" % i, [128]) for i in range(6)])
        odf = Rot([A.f32("odf%d" % i, [128]) for i in range(6)])
        sqc = Rot([A.bf("sqc%d" % i, [128]) for i in range(3)])

        qblocks = []
        for i in range(NOB):
            kl = [(l, "full") for l in range(i)] + [(NOB + l, "full") for l in range(i)] + [(NOB + i, "keep"), (i, "causal")]
            qblocks.append((i * 128, 128, kl))
        qblocks.append((NO, NH, [(l, "halo") for l in range(NKB)]))

        def exp_group(Sps, qn, grp_list, scale, pb):
            ng = len(grp_list)
            if qn == 128:
                act(pb[:, :ng * 128], Sps[:, :ng * 128], AF.Exp, scale=scale)
            else:
                act(V(pb.key, pb.ap.rearrange("p (j q) -> p j q", q=128)[:, :ng, :qn]),
                    V(Sps.key, Sps.ap.rearrange("p (j q) -> p j q", q=128)[:, :ng, :qn]), AF.Exp, scale=scale)
            for jj, (l, mode) in enumerate(grp_list):
                reg = pb[:, jj * 128:jj * 128 + qn]
                if mode == "keep":
                    ts("dve", reg, reg, pcol("keep"), None, ALU.mult, extra_reads=[par])
                elif mode == "causal":
                    tt("dve", reg, reg, causal_b, ALU.mult)
                elif mode == "halo":
                    tt("dve", reg, reg, halo_b[:, l * NH:(l + 1) * NH], ALU.mult)

        def load_head(Ksrc, Vsrc, Qsrc, hd):
            kh = Kh.next()
            dma("sp", kh, V("K", Ksrc[hd, :, :]), kh.key)
            vh = Vh.next()
            dma("sp", vh, V("Vs", Vsrc.rearrange("(j p) d -> p j d", p=128)[:, :, hd * 128:(hd + 1) * 128]), vh.key)
            qh = Qh.next()
            dma("sp", qh, V("Q", Qsrc[hd, :, :]), qh.key)
            return kh, vh, qh

        SC_MLA = 192.0 ** -0.5
        SC_DIFF = 64.0 ** -0.5
        def attn_driver(heads, emit_S, emit_exp, emit_PV, emit_fin_a, emit_fin_b, head_done, ATTN_PIPE=True, gsz=4, look=1):
            items = []
            for hd in heads:
                for bi, (q0, qn, kl) in enumerate(qblocks):
                    nk = len(kl)
                    ng = (nk + gsz - 1) // gsz
                    for gi in range(ng):
                        items.append({"hd": hd, "bi": bi, "q0": q0, "qn": qn, "kl": kl, "g0": gi * gsz,
                                      "gl": kl[gi * gsz:gi * gsz + gsz], "nk": nk,
                                      "first_of_head": bi == 0 and gi == 0,
                                      "last_of_qb": gi == ng - 1,
                                      "last_of_head": bi == len(qblocks) - 1 and gi == ng - 1})
            pend_b = []
            if ATTN_PIPE:
                for kk in range(min(look, len(items))):
                    emit_S(items[kk])
            for k, it in enumerate(items):
                if ATTN_PIPE and PIPE_ORDER == 1:
                    if k + look < len(items):
                        emit_S(items[k + look])
                elif not ATTN_PIPE:
                    emit_S(it)
                emit_exp(it)
                if ATTN_PIPE and PIPE_ORDER == 2:
                    if k + 1 < len(items):
                        emit_S(items[k + 1])
                emit_PV(it)
                while pend_b:
                    pend_b.pop(0)()
                if it["last_of_qb"]:
                    emit_fin_a(it)
                    if emit_fin_b is not None:
                        pend_b.append(lambda it=it: emit_fin_b(it))
                if it["last_of_head"]:
                    while pend_b:
                        pend_b.pop(0)()
                    head_done(it)

        SB = Rot([PS[0], PS[1], PS[4], PS[5]])
        hctx = {}

        def mla_S(it):
            hd = it["hd"]
            if it["first_of_head"]:
                kh, vh, qh = load_head(KN, VM, QN, hd)
                if hd % 2 == 0:
                    qp = Qp.next()
                    dma("sp", qp, V("QPE", QPE[hd // 2, :, :]), qp.key)
                    hctx["qp"] = qp
                hctx[hd] = (kh, vh, qh, hctx["qp"], Oh.next())
            kh, vh, qh, qp, oh = hctx[hd]
            hp = (hd % 2) * 64
            q0, qn = it["q0"], it["qn"]
            Sps = SB.next()
            it["S"] = Sps
            for jj, (l, mode) in enumerate(it["gl"]):
                reg = Sps[:, jj * 128:jj * 128 + qn]
                mm(reg, kh[:, l * 128:(l + 1) * 128], qh[:, q0:q0 + qn], True, False)
                mm(reg, kpes[hd % 2][:, l * 128:(l + 1) * 128], qp[:, q0:q0 + qn], False, True)

        def mla_exp(it):
            pb = Pb.next()
            it["pb"] = pb
            exp_group(it["S"], it["qn"], it["gl"], SC_MLA, pb)

        def mla_acc(it):
            qn = it["qn"]
            bo, bd = (2, 3) if it["bi"] % 2 == 0 else (6, 7)
            return (V("ps%d" % bo, PS[bo].ap[:, 0:qn]), V("ps%d" % bd, PS[bd].ap[:, 0:qn]))

        def mla_PV(it):
            kh, vh, qh, qp, oh = hctx[it["hd"]]
            oacc, dacc = mla_acc(it)
            qn, pb = it["qn"], it["pb"]
            for jj, (l, mode) in enumerate(it["gl"]):
                first = (it["g0"] + jj == 0)
                last = (it["g0"] + jj == it["nk"] - 1)
                mm(oacc, vh[:, l, :], pb[:, jj * 128:jj * 128 + qn], first, last)
                mm(dacc, ones_b, pb[:, jj * 128:jj * 128 + qn], first, last)

        def mla_fin(it):
            kh, vh, qh, qp, oh = hctx[it["hd"]]
            oacc, dacc = mla_acc(it)
            q0, qn = it["q0"], it["qn"]
            rc = recs.next()
            P.op("dve", lambda h, o=rc.ap[:, :qn], i=dacc.ap: h.reciprocal(out=o, in_=i), reads=[dacc], writes=[rc])
            tt("dve", oh[:, q0:q0 + qn], oacc, rc[:, :qn], ALU.mult)

        def mla_done(it):
            oh = hctx[it["hd"]][4]
            dma("sp", V("OM", OM[it["hd"], :, :]), oh, oh.key, store=True)

        attn_driver(range(8), mla_S, mla_exp, mla_PV, mla_fin, None, mla_done, PIPE_MLA)

        SBd = Rot([PS[0], PS[1], PS[4], PS[5]])
        cof = Rot([A.f32("cof%d" % i, [128]) for i in range(4)])
        dctx = {}

        def df_S(it):
            hd = it["hd"]
            if it["first_of_head"]:
                ka = KhD.next()
                kb = KhB.next()
                dma("sp", V(ka.key, ka.ap[0:64, :]), V("K", DK[hd, 0:64, :]), ka.key)
                dma("sp", V(kb.key, kb.ap[64:128, :]), V("K", DK[hd, 64:128, :]), kb.key)
                vh = Vh.next()
                dma("sp", vh, V("Vs", DV.rearrange("(j p) d -> p j d", p=128)[:, :, hd * 128:(hd + 1) * 128]), vh.key)
                qh = Qh.next()
                dma("sp", qh, V("Q", DQ[hd, :, :]), qh.key)
                dctx[hd] = ((ka, kb), vh, qh, Oh.next())
            (ka, kb), vh, qh, oh = dctx[hd]
            q0, qn = it["q0"], it["qn"]
            Sb = SBd.next()
            it["S"] = Sb
            for jj, (l, mode) in enumerate(it["gl"]):
                mm(Sb[:, jj * 128:jj * 128 + qn], ka[:, l * 128:(l + 1) * 128], qh[:, q0:q0 + qn], True, True)
                mm(Sb[:, 256 + jj * 128:256 + jj * 128 + qn], kb[:, l * 128:(l + 1) * 128], qh[:, q0:q0 + qn], True, True)

        def df_exp(it):
            pb = Pb.next()
            it["pb"] = pb
            qn = it["qn"]
            Sb = it["S"]
            assert len(it["gl"]) == 2
            if qn == 128:
                act(pb, Sb, AF.Exp, scale=SC_DIFF)
            else:
                act(V(pb.key, pb.ap.rearrange("p (j q) -> p j q", q=128)[:, :, :qn]),
                    V(Sb.key, Sb.ap.rearrange("p (j q) -> p j q", q=128)[:, :, :qn]), AF.Exp, scale=SC_DIFF)
            for jj, (l, mode) in enumerate(it["gl"]):
                for m_ in range(2):
                    reg = pb[:, m_ * 256 + jj * 128:m_ * 256 + jj * 128 + qn]
                    if mode == "keep":
                        ts("dve", reg, reg, pcol("keep"), None, ALU.mult, extra_reads=[par])
                    elif mode == "causal":
                        tt("dve", reg, reg, causal_b, ALU.mult)
                    elif mode == "halo":
                        tt("dve", reg, reg, halo_b[:, l * NH:(l + 1) * NH], ALU.mult)

        def df_acc(it):
            qn = it["qn"]
            return [V("ps%d" % b_, PS[b_].ap[:, 0:qn]) for b_ in (2, 3, 6, 7)]

        def df_PV(it):
            _, vh, qh, oh = dctx[it["hd"]]
            o1, d1, o2, d2 = df_acc(it)
            qn, pb = it["qn"], it["pb"]
            for jj, (l, mode) in enumerate(it["gl"]):
                first = (it["g0"] + jj == 0)
                last = (it["g0"] + jj == it["nk"] - 1)
                mm(o1, vh[:, l, :], pb[:, jj * 128:jj * 128 + qn], first, last)
                mm(d1, ones_b, pb[:, jj * 128:jj * 128 + qn], first, last)
                mm(o2, vh[:, l, :], pb[:, 256 + jj * 128:256 + jj * 128 + qn], first, last)
                mm(d2, ones_b, pb[:, 256 + jj * 128:256 + jj * 128 + qn], first, last)

        def df_fin_a(it):
            o1, d1, o2, d2 = df_acc(it)
            qn = it["qn"]
            c1 = cof.next()
            c2 = cof.next()
            act(c1[:, :qn], o1, AF.Identity)
            act(c2[:, :qn], o2, AF.Identity)
            r1 = recs.next()
            r2 = recs.next()
            P.op("dve", lambda h, o=r1.ap[:, :qn], i=d1.ap: h.reciprocal(out=o, in_=i), reads=[d1], writes=[r1])
            P.op("dve", lambda h, o=r2.ap[:, :qn], i=d2.ap: h.reciprocal(out=o, in_=i), reads=[d2], writes=[r2])
            t1 = odf.next()
            tt("dve", t1[:, :qn], c1[:, :qn], r1[:, :qn], ALU.mult)
            t2 = odf.next()
            stt("dve", t2[:, :qn], c2[:, :qn], neglam, r2[:, :qn], ALU.mult, ALU.mult, extra_reads=[lamv])
            tt("dve", t1[:, :qn], t1[:, :qn], t2[:, :qn], ALU.add)
            sq = sqc.next()
            act(sq[:, :qn], t1[:, :qn], AF.Square)
            it["t1"], it["sq"] = t1, sq

        def df_fin_b(it):
            _, vh, qh, oh = dctx[it["hd"]]
            q0, qn = it["q0"], it["qn"]
            t1, sq = it["t1"], it["sq"]
            psn = SBd.items[SBd.i % len(SBd.items)]
            mm(psn[:, :qn], ones_b, sq[:, :qn], True, True)
            rs = recs.next()
            rstd_from(psn[:, :qn], rs[:, :qn], 128)
            stt("dve", oh[:, q0:q0 + qn], t1[:, :qn], gsub8.ap[:, 0:1], rs[:, :qn], ALU.mult, ALU.mult,
                extra_reads=[gsub8])

        def df_done(it):
            oh = dctx[it["hd"]][3]
            dma("sp", V("OD", OD[it["hd"], :, :]), oh, oh.key, store=True)

        attn_driver(range(8), df_S, df_exp, df_PV, df_fin_a, df_fin_b, df_done, PIPE_DIFF, gsz=2, look=2)
        pn_flush()
        P.barrier()

        A.reset()
        omt = A.bf("omt", [8, 512])
        odt = A.bf("odt", [8, 512])
        gts = Rot([A.bf("gt%d" % i, [8, 512]) for i in range(2)])
        mixed = A.bf("mixed", [16, 512])
        x1t = A.f32("x1t", [16, 512])
        xcs = Rot([A.f32("xcD%d" % i, [512]) for i in range(3)])
        h2t = A.bf("h2t", [16, 512])
        wos = Rot([A.bf("wo%d" % i, [8, 512]) for i in range(4)])
        wslots = Rot([A.bf("wD%d" % i, [16, 512]) for i in range(2)])
        tmpf = Rot([A.f32("tmpfD%d" % i, [512]) for i in range(6)])
        sqb = Rot([A.bf("sqbD%d" % i, [512]) for i in range(3)])
        rstds = Rot([A.f32("rstdD%d" % i, [512]) for i in range(2)])
        PSMD = Rot([PS[0], PS[1], PS[2], PS[3]])
        PSO = Rot([PS[4], PS[5]])
        PSX = PS[6]
        for (q0, n) in q_tiles:
            dma("sp", V(omt.key, omt.ap[:, :, :n]), V("OM", OM[:, :, q0:q0 + n].rearrange("h p q -> p h q")), omt.key)
            dma("sp", V(odt.key, odt.ap[:, :, :n]), V("OD", OD[:, :, q0:q0 + n].rearrange("h p q -> p h q")), odt.key)
            for eg in range(4):
                wm = wos.next()
                dma("pool", wm, V("w", w_o_mla.rearrange("(k p) e -> p k e", p=128)[:, :, eg * 512:(eg + 1) * 512]), wm.key)
                wd_ = wos.next()
                dma("pool", wd_, V("w", w_o_diff.rearrange("(k p) e -> p k e", p=128)[:, :, eg * 512:(eg + 1) * 512]), wd_.key)
                gt = gts.next()
                for ab in range(2):
                    dma("sp", V(gt.key, gt.ap[:, ab * 4:ab * 4 + 4, :n]),
                        V("G", G[ab * 16 + eg * 4:ab * 16 + eg * 4 + 4, :, q0:q0 + n].rearrange("h p q -> p h q")), gt.key)
                for j in range(4):
                    e = eg * 4 + j
                    psm = PSMD.next()
                    for k in range(8):
                        mm(psm[:, :n], wm[:, k, j * 128:(j + 1) * 128], omt[:, k, :n], k == 0, k == 7)
                    psd = PSMD.next()
                    for k in range(8):
                        mm(psd[:, :n], wd_[:, k, j * 128:(j + 1) * 128], odt[:, k, :n], k == 0, k == 7)
                    t1 = tmpf.next()
                    tt("dve", t1[:, :n], psm[:, :n], gt[:, j, :n], ALU.mult)
                    t2 = tmpf.next()
                    tt("dve", t2[:, :n], psd[:, :n], gt[:, 4 + j, :n], ALU.mult)
                    tt("dve", mixed[:, e, :n], t1[:, :n], t2[:, :n], ALU.add)
            for eg in range(4):
                wt = wslots.next()
                dma("pool", wt, V("w", w_out.rearrange("(k p) e -> p k e", p=128)[:, :, eg * 512:(eg + 1) * 512]), wt.key)
                for j in range(4):
                    e = eg * 4 + j
                    ps = PSO.next()
                    for k in range(KD):
                        mm(ps[:, :n], wt[:, k, j * 128:(j + 1) * 128], mixed[:, k, :n], k == 0, k == KD - 1)
                    xc = xcs.next()
                    dma("sp", V(xc.key, xc.ap[:, :n]), V("xT", xT[e * 128:(e + 1) * 128, q0:q0 + n]), xc.key)
                    stt("dve", x1t[:, e, :n], ps[:, :n], gate1(e), xc[:, :n], ALU.mult, ALU.add, extra_reads=[modT])
            if q0 < NO:
                dma("sp", V("X1", X1[:, :, q0:q0 + n].rearrange("c p q -> p c q")), V(x1t.key, x1t.ap[:, :, :n]), x1t.key, store=True)
            for k in range(KD):
                sq = sqb.next()
                act(sq[:, :n], x1t[:, k, :n], AF.Square)
                mm(PSX[:, :n], ones_b, sq[:, :n], k == 0, k == KD - 1)
            rs = rstds.next()
            rstd_from(PSX[:, :n], rs[:, :n], D)
            for k in range(KD):
                t = tmpf.next()
                stt("dve", t[:, :n], x1t[:, k, :n], a2.ap[:, k:k + 1], rs[:, :n], ALU.mult, ALU.mult, extra_reads=[a2])
                if q0 >= NO:
                    act(t[:, :n], t[:, :n], AF.Identity, bias=shift2(k), extra_reads=[modT])
                    tt("dve", h2t[:, k, :n], t[:, :n], V("par", pcol("hvalid", 0, NH)), ALU.mult)
                else:
                    act(h2t[:, k, :n], t[:, :n], AF.Identity, bias=shift2(k), extra_reads=[modT])
            dma("sp", V("H2", H2[:, :, q0:q0 + n].rearrange("c p q -> p c q")), V(h2t.key, h2t.ap[:, :, :n]), h2t.key, store=True)
        pn_flush()
        P.barrier()

        A.reset()
        h2 = A.bf("h2", [16, NQ])
        dma("sp", h2, V("H2", H2.rearrange("c p q -> p c q")), "h2")
        wv_s = Rot([A.bf("wv%d" % i, [16, 256]) for i in range(2)])
        wg_s = Rot([A.bf("wg%d" % i, [16, 256]) for i in range(2)])
        uext = Rot([A.f32("uext%d" % i, [NOB, 130]) for i in range(4)])
        ycv = Rot([A.f32("ycv%d" % i, [NOB, 128]) for i in range(4)])
        zb = Rot([A.bf("zb%d" % i, [NOB, 128]) for i in range(2)])
        PSE = Rot(PS)
        own_q_tiles = tiles_of(0, NO)

        def conv_chunk(wt, wc0, ch):
            ue = uext.next()
            for (q0, n) in own_q_tiles:
                ps = PSE.next()
                for k in range(KD):
                    mm(ps[:, :n], wt[:, k, wc0:wc0 + 128], h2[:, k, q0:q0 + n], k == 0, k == KD - 1)
                nb = n // 128
                b0 = q0 // 128
                act(V(ue.key, ue.ap[:, b0:b0 + nb, 2:130]), V(ps.key, ps.ap[:, :n].rearrange("p (b t) -> p b t", t=128)),
                    AF.Identity)
            ps = PSE.next()
            for k in range(KD):
                mm(ps[:, :NH], wt[:, k, wc0:wc0 + 128], h2[:, k, NO:NO + NH], k == 0, k == KD - 1)
            cp("dve", V(ue.key, ue.ap[:, :, 0:2]), V(ps.key, ps.ap[:, :NH].rearrange("p (b t) -> p b t", t=2)))
            cw = lambda j: pcol("conv_w", j * 88 + ch)
            y = ycv.next()
            act(y, V(ue.key, ue.ap[:, :, 2:130]), AF.Identity, bias=pcol("conv_b", ch), scale=cw(2), extra_reads=[par])
            stt("dve", y, V(ue.key, ue.ap[:, :, 1:129]), cw(1), y, ALU.mult, ALU.add, extra_reads=[par])
            stt("dve", y, V(ue.key, ue.ap[:, :, 0:128]), cw(0), y, ALU.mult, ALU.add, extra_reads=[par])
            return y

        for cp0 in range(0, 44, 2):
            wv = wv_s.next()
            dma("pool", wv, V("w", w_up.rearrange("(k p) e -> p k e", p=128)[:, :, cp0 * 128:cp0 * 128 + 256]), wv.key)
            wg = wg_s.next()
            dma("pool", wg, V("w", w_up.rearrange("(k p) e -> p k e", p=128)[:, :, DFF + cp0 * 128:DFF + cp0 * 128 + 256]), wg.key)
            for cc in range(2):
                c = cp0 + cc
                yv = conv_chunk(wv, cc * 128, c)
                yg = conv_chunk(wg, cc * 128, 44 + c)
                act(yg, yg, AF.Silu)
                z = zb.next()
                tt("dve", z, yg, yv, ALU.mult)
                dma("sp", V("Z", Z[c, :, :]), V(z.key, z.ap.rearrange("p b t -> p (b t)")), z.key, store=True)
        pn_flush()
        P.barrier()

        A.reset()
        wd_s = Rot([A.bf("wdn%d" % i, [44, 512]) for i in range(2)])
        zt_s = Rot([A.bf("zt%d" % i, [44, 512]) for i in range(2)])
        x1c = Rot([A.f32("x1c%d" % i, [512]) for i in range(2)])
        oc = Rot([A.f32("oc%d" % i, [512]) for i in range(2)])
        PSE = Rot(PS)
        for eg in range(4):
            wd = wd_s.next()
            dma("pool", wd, V("w", w_down.rearrange("(k p) e -> p k e", p=128)[:, :, eg * 512:(eg + 1) * 512]), wd.key)
            for (q0, n) in own_q_tiles:
                zt = zt_s.next()
                dma("sp", V(zt.key, zt.ap[:, :, :n]), V("Z", Z[:, :, q0:q0 + n].rearrange("c p q -> p c q")), zt.key)
                for j in range(4):
                    e = eg * 4 + j
                    ps = PSE.next()
                    for c in range(44):
                        mm(ps[:, :n], wd[:, c, j * 128:(j + 1) * 128], zt[:, c, :n], c == 0, c == 43)
                    xc = x1c.next()
                    dma("sp", V(xc.key, xc.ap[:, :n]), V("X1", X1[e, :, q0:q0 + n]), xc.key)
                    o = oc.next()
                    stt("dve", o[:, :n], ps[:, :n], gate2(e), xc[:, :n], ALU.mult, ALU.add, extra_reads=[modT])
                    od_ = dma("sp", V("outT", outT[e * 128:(e + 1) * 128, q0:q0 + n]), V(o.key, o.ap[:, :n]), o.key, store=True)
                    P.out_dmas.append(od_)
        fin = P.op("sp", None)
        fin.deps.update(o.idx for o in P.out_dmas)
        P.emit(nc, stack)
    build.last_prog = P
    return nc


def host_prep(inp, S):
    NO = S // 2
    NOB = NO // 128
    NH = 2 * NOB
    NKB = 2 * NOB
    POFF, NPAR = par_layout(NH)
    f32 = np.float32
    x = np.asarray(inp["x"], f32)
    pos = np.asarray(inp["positions"], np.int32)
    B = x.shape[0]

    def fm(v):
        v = np.asarray(v, f32).reshape(-1, 128)
        return np.ascontiguousarray(v.T)

    def rep(v):
        v = np.asarray(v, f32).reshape(1, -1)
        return np.repeat(v, 128, axis=0)

    w_in = np.asarray(inp["w_in"][0], f32)
    q_lat, kv_lat, k_pe, dq, dk, dv, gl = np.split(w_in, np.cumsum([512, 256, 64, 1024, 1024, 1024])[:], axis=1)
    w_q = np.ascontiguousarray(np.concatenate([q_lat, dq, gl], axis=1))
    w_kv = np.ascontiguousarray(np.concatenate([kv_lat, k_pe, k_pe, dk, dv], axis=1))
    wqu = np.asarray(inp["w_q_up"][0], f32).reshape(512, 8, 192)
    w_q_up = np.ascontiguousarray(np.concatenate([wqu[:, :, :128].reshape(512, 1024), wqu[:, :, 128:].reshape(512, 512)], axis=1))
    wkvu = np.asarray(inp["w_kv_up"][0], f32).reshape(256, 8, 256)
    w_kv_up = np.ascontiguousarray(np.concatenate([wkvu[:, :, :128].reshape(256, 1024), wkvu[:, :, 128:].reshape(256, 1024)], axis=1))

    ones = np.ones((128, 128), f32)
    bones = np.zeros((128, 128), f32)
    bones[:64, :64] = 1
    bones[64:, 64:] = 1
    perm = np.zeros((128, 128), f32)
    for m in range(128):
        if (m % 64) < 32:
            perm[m + 32, m] = -1.0
        else:
            perm[m - 32, m] = 1.0
    causal = (np.arange(128)[None, :] >= np.arange(128)[:, None]).astype(f32)
    invfreq = (10000.0 ** (-(np.arange(0, 64, 2, dtype=f32)) / f32(64))).astype(f32)
    invf128 = np.tile(invfreq, 4).reshape(128, 1)

    gq = np.asarray(inp["g_q_mla"][0], f32)
    gk = np.asarray(inp["g_k_mla"][0], f32)
    shared = {
        "b_ada": fm(inp["b_ada"][0]), "g1": fm(inp["g_norm1"][0]), "g2": fm(inp["g_norm2"][0]),
        "b_gate": fm(inp["b_gate"][0]),
        "conv_w": np.concatenate([fm(inp["conv_w"][0][j]) for j in range(3)], axis=1),
        "conv_b": fm(inp["conv_b"][0]),
        "g_q_lat": fm(inp["g_q_lat"][0]), "g_kv_lat": fm(inp["g_kv_lat"][0]),
        "gq_nope": gq[:128].reshape(128, 1), "gq_pe": np.tile(gq[128:], 2).reshape(128, 1),
        "gk_nope": gk[:128].reshape(128, 1), "gk_pe": np.tile(gk[128:], 2).reshape(128, 1),
        "gq_diff": np.tile(np.asarray(inp["g_q_diff"][0], f32), 2).reshape(128, 1),
        "gk_diff": np.tile(np.asarray(inp["g_k_diff"][0], f32), 2).reshape(128, 1),
        "g_sub": np.asarray(inp["g_sub_diff"][0], f32).reshape(128, 1),
        "invfreq": invf128,
        "lq1": rep(inp["lam_q1"][0]), "lk1": rep(inp["lam_k1"][0]),
        "lq2": rep(inp["lam_q2"][0]), "lk2": rep(inp["lam_k2"][0]),
    }
    big = {
        "w_ada": np.ascontiguousarray(np.asarray(inp["w_ada"][0], f32)),
        "w_q": w_q, "w_kv": w_kv, "w_q_up": w_q_up, "w_kv_up": w_kv_up,
        "w_o_mla": np.ascontiguousarray(np.asarray(inp["w_o_mla"][0], f32)),
        "w_o_diff": np.ascontiguousarray(np.asarray(inp["w_o_diff"][0], f32)),
        "w_out": np.ascontiguousarray(np.asarray(inp["w_out"][0], f32)),
        "w_up": np.ascontiguousarray(np.asarray(inp["w_up"][0], f32)),
        "w_down": np.ascontiguousarray(np.asarray(inp["w_down"][0], f32)),
    }
    in_maps = []
    own_idx = []
    for core in range(NCORES):
        b, c = core // 2, core % 2
        own_blocks = [2 * i + c for i in range(NOB)]
        oth_blocks = [2 * i + (1 - c) for i in range(NOB)]
        own_tok = np.concatenate([np.arange(j * 128, (j + 1) * 128) for j in own_blocks])
        oth_tok = np.concatenate([np.arange(j * 128, (j + 1) * 128) for j in oth_blocks])
        halo_tok = []
        hvalid = []
        for j in own_blocks:
            for d_ in (2, 1):
                t = j * 128 - d_
                if t >= 0:
                    halo_tok.append(t)
                    hvalid.append(1.0)
                else:
                    halo_tok.append(2 - d_)
                    hvalid.append(0.0)
        halo_tok = np.array(halo_tok)
        tl = np.concatenate([own_tok, halo_tok, oth_tok])
        kv_tok = np.concatenate([own_tok, oth_tok])
        xT = np.ascontiguousarray(x[b][tl].T)
        posr = np.ascontiguousarray(np.repeat(pos[b][tl].reshape(1, -1), 128, axis=0)).astype(np.int32)
        hm = (kv_tok.reshape(NKB, 128).T[:, :, None] <= halo_tok[None, None, :]).astype(f32).reshape(128, NKB * NH)
        cst = np.ascontiguousarray(np.concatenate([ones, bones, perm, causal, hm], axis=1))
        par = np.zeros((128, NPAR), f32)
        for name, arr in shared.items():
            o, n = POFF[name]
            par[:, o:o + n] = arr
        o, n = POFF["c"]
        par[:, o:o + n] = fm(np.asarray(inp["c"], f32)[b])
        o, n = POFF["keep"]
        par[:, o] = 1.0 if c == 1 else 0.0
        o, n = POFF["hvalid"]
        par[:, o:o + n] = np.asarray(hvalid, f32)[None, :]
        m = {"xT": xT, "params": par, "posrep": posr, "consts": cst}
        m.update(big)
        in_maps.append(m)
        own_idx.append((b, own_tok))
    return in_maps, own_idx


_NC_CACHE = {}


def kernel(**inputs):
    S = int(np.asarray(inputs["x"]).shape[1])
    if S not in _NC_CACHE:
        _NC_CACHE[S] = build(S)
    nc = _NC_CACHE[S]
    in_maps, own_idx = host_prep(inputs, S)
    res = run_bass_kernel_spmd(nc, in_maps, core_ids=list(range(NCORES)))
    B = np.asarray(inputs["x"]).shape[0]
    out = np.empty((B, S, D), np.float32)
    for core in range(NCORES):
        b, own_tok = own_idx[core]
        out[b, own_tok, :] = np.asarray(res.results[core]["outT"]).T
    return out
```
